# Optimizing a Trainium2 kernel written in Bass

```python
import jax, jax.numpy as jnp
from jax import lax
import numpy as np

D_MODEL = 1024
BATCH = 2
SEQ = 8192
DEPTH = 1
DEC_BATCH = 128
DEC_SEQ = 4
PAST_LEN = 8192
PAGE_SIZE = 128

M_HEADS = 4
M_DK = 128
M_DV = 128
M_WIDTH = M_HEADS * M_DV
MLSTM_CHUNK = 64
A_HEADS = 8
A_KV_HEADS = 2
A_HD = 64
A_GROUP = A_HEADS // A_KV_HEADS
A_WIDTH = A_HEADS * A_HD
WINDOW = 128
D_MIX = M_WIDTH + A_WIDTH
SPLITS = (M_HEADS * M_DK, M_HEADS * M_DK, M_WIDTH, M_WIDTH, M_HEADS, M_HEADS,
          A_WIDTH, A_KV_HEADS * A_HD, A_KV_HEADS * A_HD)
D_IN = 2 * M_HEADS * M_DK + 2 * M_WIDTH + 2 * M_HEADS + A_WIDTH + 2 * A_KV_HEADS * A_HD
D_FF = 2816
RMS_EPS = 1e-6

kernel_name = "hymba_mlstm_swa_sink_alibi_macaron_step"


def rmsnorm(x, g):
    xf = x.astype(jnp.float32)
    y = xf * lax.rsqrt(jnp.mean(xf * xf, axis=-1, keepdims=True) + RMS_EPS)
    return (y * g.astype(jnp.float32)).astype(x.dtype)


def swiglu(x, w_gate, w_up, w_down):
    return (jax.nn.silu(x @ w_gate) * (x @ w_up)) @ w_down


def split_proj(h, w_in, b_gate):
    B, T, _ = h.shape
    z = h @ w_in
    idx = np.cumsum(SPLITS)[:-1].tolist()
    qm, km, vm, om, ip, fp, qa, ka, va = jnp.split(z, idx, axis=-1)
    bg = b_gate.astype(z.dtype)
    m_parts = (qm.reshape(B, T, M_HEADS, M_DK),
               km.reshape(B, T, M_HEADS, M_DK) * (M_DK ** -0.5),
               vm.reshape(B, T, M_HEADS, M_DV),
               jax.nn.sigmoid(om.astype(jnp.float32)),
               ip + bg[:M_HEADS],
               fp + bg[M_HEADS:])
    a_parts = (qa.reshape(B, T, A_HEADS, A_HD),
               ka.reshape(B, T, A_KV_HEADS, A_HD),
               va.reshape(B, T, A_KV_HEADS, A_HD))
    return m_parts, a_parts


def mlstm_chunk(carry, xs):
    C, n, m = carry
    q, k, v, ig, fg = xs
    L = q.shape[2]
    b = jnp.cumsum(jax.nn.log_sigmoid(fg), axis=-1)
    causal = jnp.tril(jnp.ones((L, L), dtype=bool))
    D = jnp.where(causal, b[..., :, None] - b[..., None, :] + ig[..., None, :], -jnp.inf)
    inter = b + m[..., None]
    m_t = jnp.maximum(inter, jnp.max(D, axis=-1))
    W = jnp.exp(D - m_t[..., None])
    a = jnp.exp(inter - m_t)
    S = jnp.einsum('bhtd,bhsd->bhts', q, k) * W
    num = a[..., None] * jnp.einsum('bhtk,bhkv->bhtv', q, C) + jnp.einsum('bhts,bhsv->bhtv', S, v)
    den = a * jnp.einsum('bhtk,bhk->bht', q, n) + jnp.sum(S, axis=-1)
    h = num / jnp.maximum(jnp.abs(den), jnp.exp(-m_t))[..., None]
    bL = b[..., -1]
    g = bL[..., None] - b + ig
    m_new = jnp.maximum(bL + m, jnp.max(g, axis=-1))
    decay = jnp.exp(bL + m - m_new)
    wk = jnp.exp(g - m_new[..., None])
    C_new = decay[..., None, None] * C + jnp.einsum('bhs,bhsk,bhsv->bhkv', wk, k, v)
    n_new = decay[..., None] * n + jnp.einsum('bhs,bhsk->bhk', wk, k)
    return (C_new, n_new, m_new), h


def mlstm_heads(q, k, v, o, ig, fg, state, norm_gain, out_dtype):
    B, T = q.shape[:2]
    L = MLSTM_CHUNK if T % MLSTM_CHUNK == 0 else T
    nc = T // L

    def chunks(a):
        a = a.astype(jnp.float32).reshape((B, nc, L) + a.shape[2:])
        return jnp.moveaxis(a, (1, 3), (0, 2))

    C0, n0, m0 = state
    init = (C0.astype(jnp.float32), n0.astype(jnp.float32), m0.astype(jnp.float32))
    (C1, n1, m1), h = lax.scan(mlstm_chunk, init,
                               (chunks(q), chunks(k), chunks(v), chunks(ig), chunks(fg)))
    h = jnp.moveaxis(h, (0, 2), (1, 3)).reshape(B, T, M_HEADS, M_DV)
    h = h * lax.rsqrt(jnp.mean(h * h, axis=-1, keepdims=True) + RMS_EPS)
    h = h * norm_gain.astype(jnp.float32).reshape(M_HEADS, M_DV)
    h = o.reshape(B, T, M_HEADS, M_DV) * h
    return h.reshape(B, T, M_WIDTH).astype(out_dtype), (C1, n1, m1)


def alibi(dist):
    slopes = jnp.exp2(-8.0 * jnp.arange(1, A_HEADS + 1, dtype=jnp.float32) / A_HEADS)
    slopes = slopes.reshape(A_KV_HEADS, A_GROUP)
    return -slopes[:, :, None, None] * dist.astype(jnp.float32)


def sink_softmax(s, sinks):
    sk = sinks.astype(jnp.float32).reshape(A_KV_HEADS, A_GROUP, 1)
    mx = jnp.maximum(jnp.max(s, axis=-1), sk)
    p = jnp.exp(s - mx[..., None])
    den = jnp.sum(p, axis=-1) + jnp.exp(sk - mx)
    return p / den[..., None]


def swa_prompt(q, k, v, sinks):
    B, T = q.shape[:2]
    nb = T // WINDOW
    qb = q.reshape(B, nb, WINDOW, A_KV_HEADS, A_GROUP, A_HD)
    kb = k.reshape(B, nb, WINDOW, A_KV_HEADS, A_HD)
    vb = v.reshape(B, nb, WINDOW, A_KV_HEADS, A_HD)
    shift = lambda a: jnp.concatenate([jnp.zeros_like(a[:, :1]), a[:, :-1]], axis=1)
    kk = jnp.concatenate([shift(kb), kb], axis=2)
    vv = jnp.concatenate([shift(vb), vb], axis=2)
    qi = jnp.arange(WINDOW)[:, None]
    kj = jnp.arange(2 * WINDOW)[None, :]
    dist = WINDOW + qi - kj
    exists = (jnp.arange(nb)[:, None, None] * WINDOW + kj - WINDOW) >= 0
    valid = (dist >= 0) & (dist < WINDOW) & exists
    s = jnp.einsum('bnqhgd,bnkhd->bnhgqk', qb, kk).astype(jnp.float32) * (A_HD ** -0.5) + alibi(dist)
    s = jnp.where(valid[None, :, None, None], s, -jnp.inf)
    p = sink_softmax(s, sinks)
    o = jnp.einsum('bnhgqk,bnkhd->bnqhgd', p.astype(v.dtype), vv).reshape(B, T, A_WIDTH)
    keep = min(WINDOW, T)
    return o, (k[:, T - keep:], v[:, T - keep:])


def swa_sample(q, k, v, sinks, buf_k, buf_v):
    Bd, S = q.shape[:2]
    Wc = buf_k.shape[1]
    kk = jnp.concatenate([buf_k.astype(k.dtype), k], axis=1)
    vv = jnp.concatenate([buf_v.astype(v.dtype), v], axis=1)
    dist = Wc + jnp.arange(S)[:, None] - jnp.arange(Wc + S)[None, :]
    valid = (dist >= 0) & (dist < WINDOW)
    qg = q.reshape(Bd, S, A_KV_HEADS, A_GROUP, A_HD)
    s = jnp.einsum('bqhgd,bkhd->bhgqk', qg, kk).astype(jnp.float32) * (A_HD ** -0.5) + alibi(dist)
    s = jnp.where(valid, s, -jnp.inf)
    p = sink_softmax(s, sinks)
    o = jnp.einsum('bhgqk,bkhd->bqhgd', p.astype(v.dtype), vv).reshape(Bd, S, A_WIDTH)
    return o, (kk[:, S:], vv[:, S:])


def trunk_layer(x, l, mstate, kv_buf, norm_gains, ffn_w_gate, ffn_w_up, ffn_w_down,
                w_in, b_gate, mlstm_norm_gain, attn_sinks, w_out):
    g = norm_gains[l]

    def half_ffn(x, j, gpre, gpost):
        y = swiglu(rmsnorm(x, g[gpre]), ffn_w_gate[l, j], ffn_w_up[l, j], ffn_w_down[l, j])
        return x + 0.5 * rmsnorm(y, g[gpost])

    x = half_ffn(x, 0, 0, 1)
    h = rmsnorm(x, g[2])
    (qm, km, vm, om, ip, fp), (qa, ka, va) = split_proj(h, w_in[l], b_gate[l])
    u_m, new_m = mlstm_heads(qm, km, vm, om, ip, fp, mstate, mlstm_norm_gain[l], h.dtype)
    if kv_buf is None:
        u_a, new_kv = swa_prompt(qa, ka, va, attn_sinks[l])
    else:
        u_a, new_kv = swa_sample(qa, ka, va, attn_sinks[l], kv_buf[0], kv_buf[1])
    u = jnp.concatenate([u_m, u_a], axis=-1)
    x = x + rmsnorm(u @ w_out[l], g[3])
    x = half_ffn(x, 1, 4, 5)
    return x, (new_kv[0], new_kv[1], new_m[0], new_m[1], new_m[2])


def setup_inputs(seed: int = 0) -> dict:
    key = jax.random.key(seed)
    ks = jax.random.split(key, 20)
    f32 = jnp.float32
    nrm = lambda k, shape, scale: scale * jax.random.normal(k, shape, f32)
    w_cache = min(WINDOW, PAST_LEN)
    b_gate = jnp.concatenate(
        [nrm(ks[0], (DEPTH, M_HEADS), 0.1),
         jnp.linspace(3.0, 6.0, M_HEADS, dtype=f32)[None, :] + nrm(ks[1], (DEPTH, M_HEADS), 0.1)], axis=-1)
    return {
        "x_prompt": nrm(ks[2], (BATCH, SEQ, D_MODEL), 1.0),
        "x_sample": nrm(ks[3], (DEC_BATCH, DEC_SEQ, D_MODEL), 1.0),
        "cache_swa_k": nrm(ks[4], (DEPTH, DEC_BATCH, w_cache, A_KV_HEADS, A_HD), 1.0),
        "cache_swa_v": nrm(ks[5], (DEPTH, DEC_BATCH, w_cache, A_KV_HEADS, A_HD), 1.0),
        "state_mlstm_C": nrm(ks[6], (DEPTH, DEC_BATCH, M_HEADS, M_DK, M_DV), 0.3),
        "state_mlstm_n": nrm(ks[7], (DEPTH, DEC_BATCH, M_HEADS, M_DK), 0.3),
        "state_mlstm_m": nrm(ks[8], (DEPTH, DEC_BATCH, M_HEADS), 1.0),
        "norm_gains": 1.0 + nrm(ks[9], (DEPTH, 6, D_MODEL), 0.05),
        "ffn_w_gate": nrm(ks[10], (DEPTH, 2, D_MODEL, D_FF), D_MODEL ** -0.5),
        "ffn_w_up": nrm(ks[11], (DEPTH, 2, D_MODEL, D_FF), D_MODEL ** -0.5),
        "ffn_w_down": nrm(ks[12], (DEPTH, 2, D_FF, D_MODEL), D_FF ** -0.5),
        "w_in": nrm(ks[13], (DEPTH, D_MODEL, D_IN), D_MODEL ** -0.5),
        "b_gate": b_gate,
        "mlstm_norm_gain": 1.0 + nrm(ks[14], (DEPTH, M_WIDTH), 0.05),
        "attn_sinks": nrm(ks[15], (DEPTH, A_HEADS), 1.0),
        "w_out": nrm(ks[16], (DEPTH, D_MIX, D_MODEL), D_MIX ** -0.5),
    }


def reference(x_prompt, x_sample, cache_swa_k, cache_swa_v, state_mlstm_C, state_mlstm_n,
              state_mlstm_m, norm_gains, ffn_w_gate, ffn_w_up, ffn_w_down, w_in, b_gate,
              mlstm_norm_gain, attn_sinks, w_out):
    B = x_prompt.shape[0]
    zero_state = (jnp.zeros((B, M_HEADS, M_DK, M_DV), jnp.float32),
                  jnp.zeros((B, M_HEADS, M_DK), jnp.float32),
                  jnp.zeros((B, M_HEADS), jnp.float32))
    yp, ys = x_prompt, x_sample
    p_new, s_new = [], []
    for l in range(DEPTH):
        yp, st_p = trunk_layer(yp, l, zero_state, None, norm_gains, ffn_w_gate, ffn_w_up,
                               ffn_w_down, w_in, b_gate, mlstm_norm_gain, attn_sinks, w_out)
        p_new.append(st_p)
        ys, st_s = trunk_layer(ys, l, (state_mlstm_C[l], state_mlstm_n[l], state_mlstm_m[l]),
                               (cache_swa_k[l], cache_swa_v[l]), norm_gains, ffn_w_gate, ffn_w_up,
                               ffn_w_down, w_in, b_gate, mlstm_norm_gain, attn_sinks, w_out)
        s_new.append(st_s)
    stk = lambda lst, i: jnp.stack([st[i] for st in lst], axis=0)
    return (yp, ys,
            stk(p_new, 0), stk(p_new, 1), stk(p_new, 2), stk(p_new, 3), stk(p_new, 4),
            stk(s_new, 0), stk(s_new, 1), stk(s_new, 2), stk(s_new, 3), stk(s_new, 4))
```

```python
import contextlib
import os
import numpy as np
import concourse.bass as bass
import concourse.mybir as mybir
from concourse.bass_utils import run_bass_kernel_spmd
F32 = mybir.dt.float32
BF16 = mybir.dt.bfloat16
ALU = mybir.AluOpType
AF = mybir.ActivationFunctionType
AX = mybir.AxisListType
ENGS = ('pe', 'act', 'dve', 'pool', 'sp')
D = 1024
DFF = 2816
NJ = 22
DIN = 2824
SEQ = 2048
NTILE = 4
PW = 840
TP = 512
NS = 64
NB = 16
EPS = 1e-06
NEG = -30000.0

class _Op:
    __slots__ = ('eng', 'fn', 'deps', 'dma_key', 'dma_cnt', 'signal', 'sig_val', 'idx', 'inc')

class Sched:
    def __init__(self, nc):
        self.nc = nc
        self.ops = []
        self.last_writer = {}
        self.readers = {}
        self.dma_counts = {}
    ALIAS = {'e1': 'FGs', 'lfn': 'FGs', 'Fst': 'tg', 'Ug': 'IGs', 't5': 'tmpo'}

    def _norm(self, k):
        if isinstance(k, tuple) and k[0] in ('IGs', 'FGs'):
            k = k[0]
        return self.ALIAS.get(k, k) if not isinstance(k, tuple) else k

    def op(self, eng, fn, reads=(), writes=(), dma_key=None, inc=16):
        reads = [self._norm(k) for k in reads]
        writes = [self._norm(k) for k in writes]
        writes = writes + [k for k in reads if isinstance(k, tuple) and k[0] == 'bank']
        reads = [k for k in reads if not (isinstance(k, tuple) and k[0] == 'bank')]
        o = _Op()
        o.eng, o.fn, o.idx, o.dma_key, o.inc = (eng, fn, len(self.ops), dma_key, inc)
        o.signal, o.sig_val = (False, None)
        deps = set()
        for r in reads:
            w = self.last_writer.get(r)
            if w is not None:
                deps.add(w)
        for r in writes:
            w = self.last_writer.get(r)
            if w is not None:
                deps.add(w)
            deps.update(self.readers.get(r, ()))
        o.deps = deps
        if dma_key is not None:
            self.dma_counts[dma_key] = self.dma_counts.get(dma_key, 0) + inc
            o.dma_cnt = self.dma_counts[dma_key]
        else:
            o.dma_cnt = None
        self.ops.append(o)
        for r in reads:
            self.readers.setdefault(r, []).append(o.idx)
        for r in writes:
            self.last_writer[r] = o.idx
            self.readers[r] = []
        return o.idx

    def emit(self, final_wait_keys=()):
        nc, ops = (self.nc, self.ops)
        for o in ops:
            nd = set()
            for d in o.deps:
                p = ops[d]
                if p.dma_key is None and o.dma_key is None and (p.eng == o.eng == 'pe'):
                    continue
                nd.add(d)
            o.deps = nd
            for d in nd:
                if ops[d].dma_key is None:
                    ops[d].signal = True
        cnt = {e: 0 for e in ENGS}
        for o in ops:
            if o.dma_key is None and o.signal:
                cnt[o.eng] += 1
                o.sig_val = cnt[o.eng]
        with contextlib.ExitStack() as st:
            esem = {e: st.enter_context(nc.semaphore('s_' + e)) for e in ENGS}
            dsem = {}
            for i, k in enumerate(self.dma_counts):
                dsem[k] = st.enter_context(nc.semaphore('d_%d' % i))
            block = st.enter_context(nc.Block())

            def run(ename):

                def body(eng):
                    waited = {}
                    for o in ops:
                        if o.eng != ename:
                            continue
                        need = {}
                        for d in o.deps:
                            p = ops[d]
                            if p.dma_key is not None:
                                s, v = (dsem[p.dma_key], p.dma_cnt)
                            else:
                                s, v = (esem[p.eng], p.sig_val)
                            if need.get(id(s), (None, 0))[1] < v:
                                need[id(s)] = (s, v)
                        for key, (s, v) in need.items():
                            if waited.get(key, 0) < v:
                                eng.wait_ge(s, v)
                                waited[key] = v
                        ins = o.fn(eng)
                        if o.dma_key is not None:
                            ins.then_inc(dsem[o.dma_key], o.inc)
                        elif o.signal:
                            ins.then_inc(esem[ename], 1)
                    if ename == 'sp':
                        for k in final_wait_keys:
                            eng.wait_ge(dsem[k], self.dma_counts[k])
                return body
            block.tensor(run('pe'))
            block.scalar(run('act'))
            block.vector(run('dve'))
            block.gpsimd(run('pool'))
            block.sync(run('sp'))

def drain(gen):
    try:
        while True:
            next(gen)
    except StopIteration as e:
        return e.value

def run_rr(gens):
    gens = list(gens)
    while gens:
        for g_ in list(gens):
            try:
                next(g_)
            except StopIteration:
                gens.remove(g_)

def V(ap, dims):
    return bass.AP(ap.tensor, ap.offset, [list(ap.ap[0])] + [list(d) for d in dims])

def build_program(nt=NTILE, upto=9, sample=True):
    assert nt == NTILE
    nc = bass.Bass('TRN2', target_bir_lowering=False)

    def din(name, shape, dt=F32):
        return nc.dram_tensor(name, list(shape), dt, kind='ExternalInput').ap()

    def dout(name, shape, dt=F32):
        return nc.dram_tensor(name, list(shape), dt, kind='ExternalOutput').ap()
    xp = din('xp', [SEQ, D])
    xs = din('xs', [NS, D])
    ck = din('ck', [NB, 128, 128])
    cv = din('cv', [NB, 128, 128])
    sC = din('sC', [NB, 4, 128, 128])
    sn = din('sn', [NB * 4, 128])
    sm = din('sm', [NB, 4])
    gains = din('gains', [6, D])
    wg = din('wg', [2, D, DFF])
    wu = din('wu', [2, D, DFF])
    wd = din('wd', [2, DFF, D])
    win = din('win', [D, DIN])
    bgate = din('bgate', [8])
    mng = din('mng', [512])
    sinks = din('sinks', [8])
    wout = din('wout', [D, D])
    c_ident = din('c_ident', [128, 128])
    c_maskp = din('c_maskp', [128, 128])
    c_masks = din('c_masks', [64, 64])
    c_bias = din('c_bias', [128, 256])
    c_slope = din('c_slope', [128, 8])
    c_bsc = din('c_bsc', [4, 128])
    c_tb = din('c_tb', [4, 124])
    c_bm = din('c_bm', [128, NB * 64])
    c_bmT = din('c_bmT', [64, NB])
    c_E = din('c_E', [4, 4 * 128])
    c_role = din('c_role', [128, 17])
    x1s = nc.dram_tensor('x1s', [NTILE, 128, 5 * D], F32).ap()
    zs = nc.dram_tensor('zs', [NTILE, 128, 18304], BF16).ap()
    gsI = nc.dram_tensor('gsI', [NTILE, 128, TP + NS], F32).ap()
    gsF = nc.dram_tensor('gsF', [NTILE, 128, TP + NS], F32).ap()
    exin = nc.dram_tensor('exin', [128, PW], F32)
    exout = nc.dram_tensor('exout', [4 * 128, PW], F32)
    yp = dout('yp', [SEQ, D])
    ys = dout('ys', [NS, D])
    pk = dout('pk', [128, 128])
    pv = dout('pv', [128, 128])
    pC = dout('pC', [4, 128, 128])
    pn = dout('pn', [4, 128])
    pm = dout('pm', [4, 1])
    sk = dout('sk', [NB, 128, 128])
    sv = dout('sv', [NB, 128, 128])
    sCo = dout('sCo', [NB, 4, 128, 128])
    sno = dout('sno', [NB * 4, 128])
    smo = dout('smo', [NB, 4])
    S = Sched(nc)
    out_keys = []
    with contextlib.ExitStack() as st:

        def sb(name, shape, dt=F32):
            return st.enter_context(nc.sbuf_tensor(name, list(shape), dt))
        TT_ = TP + NS
        X = sb('X', [128, 5, D])
        hnT = sb('hnT', [128, 8, TT_], BF16)
        big = sb('big', [128, 18304], BF16)
        wblk = [sb('wblk%d' % i, [128, 8, 256], BF16) for i in range(4)]
        wbig = sb('wbig', [128, NJ, D], BF16)
        gT = sb('gT', [128, 6, 8])
        gp = sb('gp', [128, 1, D])
        ident = sb('ident', [128, 128])
        identb = sb('identb', [128, 128], BF16)
        maskp = sb('maskp', [128, 128])
        masks = sb('masks', [64, 64])
        biasT = sb('biasT', [128, 1, 256])
        slopeb = sb('slopeb', [128, 8])
        bsc = sb('bsc', [4, 1, 128])
        tbl = sb('tbl', [4, 1, 124])
        bmb = sb('bmb', [128, NB, 64], BF16)
        bmT = sb('bmT', [64, NB])
        Esel = sb('Esel', [4, 4, 128])
        mngb = sb('mngb', [128, 512])
        sinkb = sb('sinkb', [128, 8])
        bi_l = sb('bi_l', [128, 1])
        nbf_l = sb('nbf_l', [128, 1])
        tmpn = sb('tmpn', [128, D])
        stt = sb('stt', [128, 8])
        xn = sb('xn', [128, D], BF16)
        sg = [sb('sg%d' % i, [128, 512]) for i in range(1)]
        IGs = sb('IGs', [128, TT_])
        FGs = sb('FGs', [128, TT_])
        e1 = FGs
        lfn = FGs
        Bneg = sb('Bneg', [128, TT_])
        Ug = IGs
        MU = sb('MU', [128, TP + 1])
        MUs = sb('MUs', [128, NB, 5])
        tg = sb('tg', [128, TT_])
        Fst = tg
        Bprev = sb('Bprev', [128, 1])
        MUprev = sb('MUprev', [128, 1])
        dd = sb('dd', [4, 4])
        dds = sb('dds', [4, NB])
        DECs = sb('DECs', [128, 4, 4])
        DECss = sb('DECss', [128, 4, NB])
        FT = sb('FT', [128, 5, 16])
        smin = sb('smin', [128, NB])
        smout = sb('smout', [128, NB])
        Cst = sb('Cst', [128, 4, 129])
        Cbf = sb('Cbf', [128, 4, 129], BF16)
        Sp = sb('Sp', [128, 4, 128], BF16)
        kw = sb('kw', [128, 4, 128], BF16)
        kwm = sb('kwm', [64, 4, 128], BF16)
        tmpo = sb('tmpo', [128, 4, 129])
        ND = sb('ND', [128, 4, 129])
        q5 = sb('q5', [128, 8, 4])
        og = sb('og', [128, 512], BF16)
        t5 = tmpo
        U = sb('U', [128, 5, D], BF16)
        Ssb = sb('Ssb', [128, 4, 256])
        Pb = sb('Pb', [128, 4, 256], BF16)
        PTs = sb('PTs', [128, 8, 128], BF16)
        sst = sb('sst', [128, 8, 8])
        kvf = sb('kvf', [128, 2, 256])
        qTm = sb('qTm', [128, 2, 4, 64], BF16)
        Cb = [sb('Cb%d' % i, [128, 4, 129]) for i in range(2)]
        Cbb = [sb('Cbb%d' % i, [128, 4, 129], BF16) for i in range(2)]
        snin = sb('snin', [64, 128])
        nTin = sb('nTin', [128, 64])
        nTout = sb('nTout', [128, 64])
        snout = sb('snout', [64, 128])
        ckb = [sb('ckb%d' % i, [128, 4, 128], BF16) for i in range(2)]
        kTc = [sb('kTc%d' % i, [128, 4, 128], BF16) for i in range(2)]
        cvb = [sb('cvb%d' % i, [128, 128], BF16) for i in range(2)]
        PTss = sb('PTss', [128, 8, 2, 4], BF16)
        uab = [sb('uab%d' % i, [4, 512], BF16) for i in range(2)]
        pn_sb = sb('pn_sb', [4, 128])
        pm_sb = sb('pm_sb', [128, 1])
        kaTh = sb('kaTh', [128, 4, 128], BF16)
        vah = sb('vah', [128, 128], BF16)
        role = sb('role', [128, 17])
        cmb = sb('cmb', [128, 12])
        ABt = sb('ABt', [128, 4, 2])
        hT = big[:, 0:NJ * TT_].rearrange('p (j t) -> p j t', t=TT_)
        o_ = [0]

        def carve(n):
            a = big[:, o_[0]:o_[0] + n]
            o_[0] += n
            return a
        qT = carve(4 * TT_).rearrange('p (h t) -> p h t', t=TT_)
        kT = carve(4 * TT_).rearrange('p (h t) -> p h t', t=TT_)
        osig = carve(5 * 512).rearrange('p (g c) -> p g c', c=512)
        qaT = carve(4 * TT_).rearrange('p (h t) -> p h t', t=TT_)
        KW = 128 + TT_
        kaT = carve(4 * KW).rearrange('p (h t) -> p h t', t=KW)
        va = carve(6 * 128).rearrange('p (g c) -> p g c', c=128)
        assert o_[0] >= NJ * TT_
        ktok = carve(5 * 512).rearrange('p (g c) -> p g c', c=512)
        vaug = carve(5 * 4 * 130).rearrange('p (g h c) -> p g h c', h=4, c=130)
        assert o_[0] <= 18304
        ps = st.enter_context(nc.psum_tensor('ps', [128, 8, 512], F32))

        def bank(i, n=1):
            return ps[:, i:i + n, :].rearrange('p a b -> p (a b)')

        def bkeys(i, n=1):
            return [('bank', i + k) for k in range(n)]

        def MM(out, lhsT, rhs, start, stop, R, W, **kw_):
            S.op('pe', lambda e: e.matmul(out, lhsT=lhsT, rhs=rhs, start=start, stop=stop, **kw_), R, W)

        def TR(out, in_, idn, R, W):
            S.op('pe', lambda e: e.transpose(out=out, in_=in_, identity=idn), R, W)

        def ACT(out, in_, func, R, W, **kw_):
            S.op('act', lambda e: e.activation(out=out, in_=in_, func=func, **kw_), R, W)

        def TTo(out, in0, in1, op, R, W, eng='dve'):
            S.op(eng, lambda e: e.tensor_tensor(out=out, in0=in0, in1=in1, op=op), R, W)

        def STT(out, in0, scalar, in1, op0, op1, R, W, eng='dve'):
            S.op(eng, lambda e: e.scalar_tensor_tensor(out=out, in0=in0, scalar=scalar, in1=in1, op0=op0, op1=op1), R, W)

        def TS(out, in0, s1, s2, op0, op1, R, W, eng='dve'):
            if s2 is None:
                S.op(eng, lambda e: e.tensor_scalar(out=out, in0=in0, scalar1=s1, scalar2=None, op0=op0), R, W)
            else:
                S.op(eng, lambda e: e.tensor_scalar(out=out, in0=in0, scalar1=s1, scalar2=s2, op0=op0, op1=op1), R, W)

        def CP(out, in_, R, W, eng='dve'):
            if eng == 'act':
                S.op('act', lambda e: e.copy(out=out, in_=in_), R, W)
            else:
                S.op(eng, lambda e: e.tensor_copy(out=out, in_=in_), R, W)

        def RCP(out, in_, R, W):
            S.op('dve', lambda e: e.reciprocal(out=out, in_=in_), R, W)

        def RED(out, in_, op, R, W):
            S.op('dve', lambda e: e.tensor_reduce(out=out, in_=in_, axis=AX.X, op=op), R, W)

        def SCAN(out, d0, init, op0, R, W):
            S.op('dve', lambda e: e.tensor_tensor_scan(out=out, data0=d0, data1=d0, initial=init, op0=op0, op1=ALU.bypass), R, W)

        def MSET(ap, val, W, eng='dve'):
            S.op(eng, lambda e: e.memset(ap, val), (), W)
        dctr = [0]
        nodma = [False]

        def DMA(out, in_, R, W, key=None, eng='sp', slow=False):
            if nodma[0] and eng == 'pool' and (key is not None) and key.startswith('w_'):
                return key
            if key is None:
                dctr[0] += 1
                key = 'dk%d' % (dctr[0] % 24)
            if slow:
                S.op(eng, lambda e: e.dma_start(out=out, in_=in_, allow_slow_non_contiguous=True), R, W, dma_key=key)
            else:
                S.op(eng, lambda e: e.dma_start(out=out, in_=in_), R, W, dma_key=key)
            return key

        DMA(X[:, 0:4, :], xp[0:TP, :].rearrange('(g p) d -> p g d', p=128), (), [('X', g) for g in range(4)], key='x_in')

        def LD(t, src, name):
            DMA(t, src, (), [name], key='c_' + name)
        LD(ident[:], c_ident[:, :], 'ident')
        LD(maskp[:], c_maskp[:, :], 'maskp')
        LD(masks[:], c_masks[:, :], 'masks')
        LD(biasT[:].rearrange('p h k -> p (h k)'), c_bias[:, :], 'biasT')
        LD(slopeb[:], c_slope[:, :], 'slopeb')
        LD(bsc[:].rearrange('p h k -> p (h k)'), c_bsc[:, :], 'bsc')
        LD(tbl[:].rearrange('p h k -> p (h k)'), c_tb[:, :], 'tbl')
        DMA(bmb[:].rearrange('p b t -> p (b t)'), c_bm[:, :], (), ['bmb'], key='c_bmb', eng='pool')
        LD(bmT[:], c_bmT[:, :], 'bmT')
        LD(Esel[:].rearrange('p h k -> p (h k)'), c_E[:, :], 'Esel')
        LD(role[:], c_role[:, :], 'role')
        CP(identb[:], ident[:], ['ident'], ['identb'])
        DMA(tmpn[0:6, :], gains[:, :], (), ['tmpn'], key='c_gT')
        pg_ = bank(1)
        for c in range(8):
            TR(pg_[:, 6 * c:6 * c + 6], tmpn[0:6, c * 128:(c + 1) * 128], ident[0:6, 0:6], ['tmpn', 'ident'], bkeys(1))
        CP(gT[:], V(pg_[:, 0:1], [[1, 6], [6, 8]]), bkeys(1), ['gT'])

        def load_gp(gi):
            DMA(gp[:, 0, :], bass.AP(gains.tensor, gains[gi, :].offset, [[0, 128], [1, D]]), (), ['gp'], key='c_gp')
        DMA(mngb[:], bass.AP(mng.tensor, mng.offset, [[0, 128], [1, 512]]), (), ['mngb'], key='c_mngb')
        DMA(sinkb[:], bass.AP(sinks.tensor, sinks.offset, [[0, 128], [1, 8]]), (), ['sinkb'], key='c_sinkb')
        MSET(bi_l[:], 0.0, ['bi_l'])
        MSET(nbf_l[:], 0.0, ['nbf_l'])
        MSET(smin[:], 0.0, ['smin'])
        for f in range(4):
            DMA(bi_l[32 * f:32 * f + 4, :], bgate[0:4].rearrange('(p o) -> p o', o=1), ['bi_l'], [('bi_l', f)], key='c_bil', slow=True)
            DMA(nbf_l[32 * f:32 * f + 4, :], bgate[4:8].rearrange('(p o) -> p o', o=1), ['nbf_l'], [('nbf_l', f)], key='c_bil', slow=True)
            DMA(smin[32 * f:32 * f + 4, :], sm.rearrange('b h -> h b'), ['smin'], [('smin', f)], key='c_smin', slow=True)
        lane_keys = [(nm_, f) for nm_ in ('bi_l', 'nbf_l', 'smin') for f in range(4)]
        neg_done = [False]
        MSET(big[:], 0.0, ['kaT_h', 'va0', 'vones'], eng='pool')
        MSET(IGs[:], 0.0, ['IGs'], eng='pool')
        MSET(FGs[:], 0.0, ['FGs'], eng='pool')
        MSET(X[:, 4, :], 0.0, [('X', 4)], eng='pool')
        MSET(Cst[:], 0.0, ['Cst'])
        MSET(Cbf[:], 0.0, ['Cbf'], eng='pool')
        MSET(Bprev[:], 0.0, ['Bprev'])
        MSET(MUprev[:], NEG, ['MUprev'])
        MSET(kaT[:, :, 0:128], 0.0, ['kaT_h'], eng='pool')
        MSET(va[:, 0, :], 0.0, ['va0'], eng='pool')
        MSET(vaug[:, :, :, 128:129], 1.0, ['vones'], eng='pool')
        for i in range(2):
            MSET(ckb[i][:], 0.0, [('ckb', i)], eng='pool')
        DMA(snin[:], sn[:, :], (), ['snin'], key='c_snin')
        TR(bank(1)[:, 0:64], snin[:, :], ident[0:64, 0:64], ['snin', 'ident'], bkeys(1))
        CP(nTin[:], bank(1)[:, 0:64], bkeys(1), ['nTin'])
        wq = []
        wslot = [0]

        extra_keys = {}

        def wload(src_fn):
            s = wslot[0] % 4
            wslot[0] += 1
            rk = 'wblk%d' % s
            extra_keys.pop(rk, None)
            src_fn(wblk[s], rk, 'w_slot%d' % s)
            return (wblk[s], rk)

        def colblk(Wap, c0, ncols):

            def f(slot, rk, key):
                DMA(slot[:, :, 0:ncols], Wap[:, c0:c0 + ncols].rearrange('(kc p) c -> p kc c', p=128), (), [rk], key=key, eng='pool')
            return f

        def colparts(Wap, parts):

            def f(slot, rk, key):
                extra_keys[rk] = [(rk, pi) for pi in range(1, len(parts))]
                for pi, (d0, c0, ncols) in enumerate(parts):
                    DMA(slot[:, :, d0:d0 + ncols], Wap[:, c0:c0 + ncols].rearrange('(kc p) c -> p kc c', p=128),
                        () if pi == 0 else [rk], [rk] if pi == 0 else [(rk, pi)], key=key if pi == 0 else key + '_p%d' % pi, eng='pool')
            return f

        def load_wbig(f):
            for j2 in range(11):
                DMA(wbig[:, 2 * j2:2 * j2 + 2, :], wd[f, 256 * j2:256 * j2 + 256, :].rearrange('(j p) c -> p j c', p=128), (), [('wbig', j2)], key='w_big%d' % j2, eng='pool')
        rot = {}

        def R2(name, n=2):
            rot[name] = (rot.get(name, -1) + 1) % n
            return rot[name]

        def groups_of(has_s):
            gs = [(g, 128, g * 128) for g in range(4)]
            if has_s:
                gs.append((4, 64, TP))
            return gs

        next_pre = [None]

        def prenorm(gi, has_s):
            for g, npp, c0 in groups_of(has_s):
                prenorm_group(gi, g, npp, c0)

        def prenorm_group(gi, g, npp, c0):
            if True:
                Xg = ('X', g)
                MSET(stt[:npp, 0:1], 0.0, ['stt'])
                ACT(xn[:npp, :], X[:npp, g, :], AF.Square, [Xg], ['xn', 'stt'], accum_out=stt[:npp, 0:1])
                ACT(stt[:npp, 1:2], stt[:npp, 0:1], AF.Sqrt, ['stt'], ['stt'], scale=1.0 / D, bias=EPS)
                RCP(stt[:npp, 2:3], stt[:npp, 1:2], ['stt'], ['stt'])
                TS(xn[:npp, :], X[:npp, g, :], stt[:npp, 2:3], None, ALU.mult, None, [Xg, 'stt'], ['xn'])
                b = 6 + R2('pT')
                pT = bank(b).bitcast(BF16).rearrange('p (c t) -> p c t', t=128)
                for c in range(8):
                    TR(pT[:, c, 0:npp], xn[:npp, c * 128:(c + 1) * 128], identb[:npp, :npp], ['xn', 'identb'], bkeys(b))
                TTo(hnT[:, :, c0:c0 + npp], pT[:, :, 0:npp], V(gT[:, gi, :], [[1, 8], [0, npp]]), ALU.mult, bkeys(b) + ['gT'], [('hnT', g)])

        def postnorm(py, pkeys, fac, g, npp):
            Xg = ('X', g)
            MSET(stt[:npp, 4:5], 0.0, ['stt'])
            ACT(tmpn[:npp, :], py[:npp, :], AF.Square, pkeys, ['tmpn', 'stt'], accum_out=stt[:npp, 4:5])
            ACT(stt[:npp, 5:6], stt[:npp, 4:5], AF.Sqrt, ['stt'], ['stt'], scale=1.0 / D, bias=EPS)
            RCP(stt[:npp, 6:7], stt[:npp, 5:6], ['stt'], ['stt'])
            STT(tmpn[:npp, :], py[:npp, :], stt[:npp, 6:7], gp[:npp, 0, :], ALU.mult, ALU.mult, pkeys + ['stt', 'gp'], ['tmpn'])
            STT(X[:npp, g, :], tmpn[:npp, :], fac, X[:npp, g, :], ALU.mult, ALU.add, [Xg, 'tmpn'], [Xg])

        def nsplits(has_s):
            return [(0, TP)] + ([(TP, NS)] if has_s else [])

        first_wbig = [False]

        def ffn(f, gi_post, has_s):
            hkeys = [('hnT', g) for g, _, _ in groups_of(has_s)]
            for blk in range(11):
                wgs, wgk = wload(colblk(wg[f], 256 * blk, 256))
                wus, wuk = wload(colblk(wu[f], 256 * blk, 256))
                if blk == 1 and first_wbig[0]:
                    first_wbig[0] = False
                    load_wbig(0)
                for jj in range(2):
                    j = 2 * blk + jj
                    for n0, nn in nsplits(has_s):
                        bg = R2('pg')
                        bu = 2 + R2('pu')
                        pg = bank(bg)
                        pu = bank(bu)
                        for kc in range(8):
                            MM(pg[:, 0:nn], wgs[:, kc, jj * 128:(jj + 1) * 128], hnT[:, kc, n0:n0 + nn], kc == 0, kc == 7, [wgk] + hkeys, bkeys(bg))
                        for kc in range(8):
                            MM(pu[:, 0:nn], wus[:, kc, jj * 128:(jj + 1) * 128], hnT[:, kc, n0:n0 + nn], kc == 0, kc == 7, [wuk] + hkeys, bkeys(bu))
                        si = 0
                        ACT(sg[si][:, 0:nn], pg[:, 0:nn], AF.Silu, bkeys(bg), ['sg%d' % si])
                        TTo(hT[:, j, n0:n0 + nn], sg[si][:, 0:nn], pu[:, 0:nn], ALU.mult, ['sg%d' % si] + bkeys(bu), [('hT', j, n0)])
            load_gp(gi_post)
            hall = [('hT', j, n0) for j in range(NJ) for n0, _ in nsplits(has_s)]
            for g, npp, c0 in groups_of(has_s):
                pyb = (4, 0)[R2('py')]
                py = bank(pyb, 2)
                for hf in range(2):
                    for j in range(NJ):
                        MM(py[:npp, hf * 512:(hf + 1) * 512], hT[:, j, c0:c0 + npp], wbig[:, j, hf * 512:(hf + 1) * 512], j == 0, j == NJ - 1, hall + [('wbig', j // 2)], bkeys(pyb, 2))
                postnorm(py, bkeys(pyb, 2), 0.5, g, npp)
                if next_pre[0] is not None:
                    prenorm_group(next_pre[0], g, npp, c0)
        zk = lambda nm, g: ('z', nm, g)

        def w_in(has_s):
            grp = groups_of(has_s)
            hkeys = [('hnT', g) for g, _, _ in grp]

            def fm_chunk(ws, wk_, lhs_fn, evac):
                for n0, nn in nsplits(has_s):
                    b = R2('pg')
                    p = bank(b)
                    for kc in range(8):
                        MM(p[:, 0:nn], lhs_fn(ws, kc), hnT[:, kc, n0:n0 + nn], kc == 0, kc == 7, [wk_] + extra_keys.get(wk_, []) + hkeys, bkeys(b))
                    evac(p, b, n0, nn)

            def tm_block(ws, wk_, ncols, evac):
                for g, npp, c0 in grp:
                    b = 2 + R2('pu')
                    p = bank(b)
                    for kc in range(8):
                        MM(p[:npp, 0:ncols], hnT[:, kc, c0:c0 + npp], ws[:, kc, 0:ncols], kc == 0, kc == 7, [wk_, ('hnT', g)], bkeys(b))
                    evac(p, b, g, npp)
            sc_k = 128.0 ** (-0.5)
            for blk in range(2):
                ws, wk_ = wload(colblk(win, 256 * blk, 256))
                for jj in range(2):
                    h = 2 * blk + jj
                    fm_chunk(ws, wk_, lambda w_, kc, jj=jj: w_[:, kc, jj * 128:(jj + 1) * 128], lambda p, b, n0, nn, h=h: CP(qT[:, h, n0:n0 + nn], p[:, 0:nn], bkeys(b), [('qT', n0)], eng='act'))
            if os.environ.get('WSTOP') == '1':
                return
            for blk in range(2):
                ws, wk_ = wload(colblk(win, 512 + 256 * blk, 256))
                for jj in range(2):
                    h = 2 * blk + jj
                    fm_chunk(ws, wk_, lambda w_, kc, jj=jj: w_[:, kc, jj * 128:(jj + 1) * 128], lambda p, b, n0, nn, h=h: S.op('act', lambda e: e.mul(out=kT[:, h, n0:n0 + nn], in_=p[:, 0:nn], mul=sc_k), bkeys(b), [('kT', n0)]))
                tm_block(ws, wk_, 256, lambda p, b, g, npp, blk=blk: TS(ktok[:npp, g, 256 * blk:256 * blk + 256], p[:npp, 0:256], sc_k, None, ALU.mult, None, bkeys(b), [zk('ktok', g)]))
            if os.environ.get('WSTOP') == '2':
                return
            for blk in range(2):
                ws, wk_ = wload(colblk(win, 1024 + 256 * blk, 256))

                def ev_v(p, b, g, npp, blk=blk):
                    CP(vaug[:npp, g, 2 * blk:2 * blk + 2, 0:128], p[:npp, 0:256].rearrange('p (h c) -> p h c', c=128), bkeys(b), [zk('vaug', g)])
                    if blk == 1:
                        MSET(vaug[:npp, g, :, 128:129], 1.0, [zk('vaug', g)])
                tm_block(ws, wk_, 256, ev_v)
            if os.environ.get('WSTOP') == '3':
                return
            for blk in range(2):
                ws, wk_ = wload(colblk(win, 1536 + 256 * blk, 256))
                tm_block(ws, wk_, 256, lambda p, b, g, npp, blk=blk: ACT(osig[:npp, g, 256 * blk:256 * blk + 256], p[:npp, 0:256], AF.Sigmoid, bkeys(b), [zk('osig', g)]))
            if os.environ.get('WSTOP') == '4':
                return
            ws, wk_ = wload(colparts(win, [(32 * f_, 2048, 32) for f_ in range(4)] + [(128 + 32 * f_, 2052, 32) for f_ in range(4)]))
            fm_chunk(ws, wk_, lambda w_, kc: w_[:, kc, 0:128], lambda p, b, n0, nn: CP(IGs[:, n0:n0 + nn], p[:, 0:nn], bkeys(b), [('IGs', n0)], eng='act'))
            fm_chunk(ws, wk_, lambda w_, kc: w_[:, kc, 128:256], lambda p, b, n0, nn: CP(FGs[:, n0:n0 + nn], p[:, 0:nn], bkeys(b), [('FGs', n0)], eng='act'))
            if os.environ.get('WSTOP') == '5':
                return
            for blk in range(2):
                ws, wk_ = wload(colblk(win, 2056 + 256 * blk, 256))
                for jj in range(2):
                    c = 2 * blk + jj
                    fm_chunk(ws, wk_, lambda w_, kc, jj=jj: w_[:, kc, jj * 128:(jj + 1) * 128], lambda p, b, n0, nn, c=c: S.op('act', lambda e: e.mul(out=qaT[:, c, n0:n0 + nn], in_=p[:, 0:nn], mul=0.125), bkeys(b), [('qaT', n0)]))
            if os.environ.get('WSTOP') == '6':
                return
            for hk in range(2):

                def kaf(slot, rk, key, hk=hk):
                    MSET(slot[:, :, 64:192], 0.0, [rk], eng='pool')
                    for d0 in (0, 192):
                        DMA(slot[:, :, d0:d0 + 64], win[:, 2568 + 64 * hk:2568 + 64 * hk + 64].rearrange('(kc p) c -> p kc c', p=128), (), [rk], key=key, eng='pool')
                ws, wk_ = wload(kaf)
                for par in range(2):
                    fm_chunk(ws, wk_, lambda w_, kc, par=par: w_[:, kc, par * 128:(par + 1) * 128], lambda p, b, n0, nn, v=2 * hk + par: CP(kaT[:, v, 128 + n0:128 + n0 + nn], p[:, 0:nn], bkeys(b), [('kaT', n0)], eng='act'))
            if os.environ.get('WSTOP') == '7':
                return
            ws, wk_ = wload(colblk(win, 2568, 256))

            def ev_kv(p, b, g, npp):
                if has_s and g >= 3:
                    CP(kvf[:npp, g - 3, :], p[:npp, 0:256], bkeys(b), [('kvf', g)])
                CP(va[:npp, 1 + g, :], p[:npp, 128:256], bkeys(b), [('va', g)], eng='act')
            tm_block(ws, wk_, 256, ev_kv)

        def gates(has_s, phase=2):
            rI = [('IGs', n0) for n0, _ in nsplits(has_s)]
            rF = [('FGs', n0) for n0, _ in nsplits(has_s)]
            TTn = TP + (NS if has_s else 0)
            if not neg_done[0]:
                neg_done[0] = True
                S.op('act', lambda e: e.mul(out=nbf_l[:], in_=nbf_l[:], mul=-1.0), ['nbf_l', 'bi_l', 'smin'] + lane_keys, ['nbf_l', 'bi_l', 'smin'] + lane_keys)
            ACT(e1[:, 0:TTn], FGs[:, 0:TTn], AF.Exp, rF + ['nbf_l'], ['e1'], scale=-1.0, bias=nbf_l[:, 0:1])
            ACT(lfn[:, 0:TTn], e1[:, 0:TTn], AF.Ln, ['e1'], ['lfn'], bias=1.0)
            if os.environ.get('GSTOP') == '1':
                return
            SCAN(Bneg[:, 0:TP], lfn[:, 0:TP], Bprev[:, 0:1], ALU.add, ['lfn', 'Bprev'], ['Bneg'])
            STT(Ug[:, 0:TP], IGs[:, 0:TP], bi_l[:, 0:1], Bneg[:, 0:TP], ALU.add, ALU.add, rI + ['bi_l', 'Bneg'], ['Ug'])
            CP(MU[:, 0:1], MUprev[:, 0:1], ['MUprev'], ['MU'])
            SCAN(MU[:, 1:TP + 1], Ug[:, 0:TP], MUprev[:, 0:1], ALU.max, ['Ug', 'MUprev', 'MU'], ['MU'])
            CP(Bprev[:, 0:1], Bneg[:, TP - 1:TP], ['Bneg'], ['Bprev'])
            CP(MUprev[:, 0:1], MU[:, TP:TP + 1], ['MU'], ['MUprev'])
            if os.environ.get('GSTOP') == '2':
                return
            MUn = V(MU[:, 128:129], [[128, 4], [0, 128]])
            MUp = V(MU[:, 0:1], [[128, 4], [0, 128]])
            MUc = MU[:, 1:TP + 1].rearrange('p (c t) -> p c t', t=128)
            v3 = lambda a, lo: a[lo:lo + 32, 0:TP].rearrange('p (c t) -> p c t', t=128)
            sl = lambda a, lo: bass.AP(a.tensor, a.offset + lo * a.ap[0][0], [[a.ap[0][0], 32]] + [list(x) for x in a.ap[1:]])
            TTo(v3(tg, 0), v3(Ug, 0), sl(MUn, 0), ALU.subtract, ['Ug', 'MU'], ['tg'])
            TTo(v3(tg, 32), sl(MUn, 32), sl(MUc, 32), ALU.subtract, ['MU'], ['tg'])
            TTo(v3(tg, 64), sl(MUp, 64), sl(MUc, 64), ALU.subtract, ['MU'], ['tg'])
            TTo(v3(tg, 96), v3(Bneg, 96), sl(MUc, 96), ALU.subtract, ['MU', 'Bneg'], ['tg'])
            TTo(dd[0:4, 0:4], V(MU[0:4, 0:1], [[128, 4]]), V(MU[0:4, 128:129], [[128, 4]]), ALU.subtract, ['MU'], ['dd'])
            ACT(dd[0:4, 0:4], dd[0:4, 0:4], AF.Exp, ['dd'], ['dd'])
            if has_s:
                c0 = TP
                l3 = lfn[:, c0:c0 + NS].rearrange('p (b t) -> p b t', t=4)
                B3 = Bneg[:, c0:c0 + NS].rearrange('p (b t) -> p b t', t=4)
                U3 = Ug[:, c0:c0 + NS].rearrange('p (b t) -> p b t', t=4)
                I3 = IGs[:, c0:c0 + NS].rearrange('p (b t) -> p b t', t=4)
                CP(B3[:, :, 0:1], l3[:, :, 0:1], ['lfn'], ['Bneg'])
                for t in range(1, 4):
                    TTo(B3[:, :, t:t + 1], B3[:, :, t - 1:t], l3[:, :, t:t + 1], ALU.add, ['lfn', 'Bneg'], ['Bneg'])
                STT(U3, I3, bi_l[:, 0:1], B3, ALU.add, ALU.add, rI + ['bi_l', 'Bneg'], ['Ug'])
                CP(MUs[:, :, 0:1], smin[:].rearrange('p (b o) -> p b o', o=1), ['smin'], ['MUs'])
                for t in range(4):
                    TTo(MUs[:, :, t + 1:t + 2], MUs[:, :, t:t + 1], U3[:, :, t:t + 1], ALU.max, ['Ug', 'MUs'], ['MUs'])
                MUn_s = V(MUs[:, 0, 4:5], [[5, NB], [0, 4]])
                MUp_s = V(MUs[:, 0, 0:1], [[5, NB], [0, 4]])
                MUc_s = MUs[:, :, 1:5]
                t3 = lambda lo: tg[lo:lo + 32, c0:c0 + NS].rearrange('p (b t) -> p b t', t=4)
                TTo(t3(0), U3[0:32], sl(MUn_s, 0), ALU.subtract, ['Ug', 'MUs'], ['tg'])
                TTo(t3(32), sl(MUn_s, 32), MUc_s[32:64], ALU.subtract, ['MUs'], ['tg'])
                TTo(t3(64), sl(MUp_s, 64), MUc_s[64:96], ALU.subtract, ['MUs'], ['tg'])
                TTo(t3(96), B3[96:128], MUc_s[96:128], ALU.subtract, ['MUs', 'Bneg'], ['tg'])
                TTo(dds[0:4, :], V(MUs[0:4, 0, 0:1], [[5, NB]]), V(MUs[0:4, 0, 4:5], [[5, NB]]), ALU.subtract, ['MUs'], ['dds'])
                ACT(dds[0:4, :], dds[0:4, :], AF.Exp, ['dds'], ['dds'])
                TTo(smout[:, :], V(MUs[:, 0, 4:5], [[5, NB]]), V(B3[:, 0, 3:4], [[4, NB]]), ALU.subtract, ['MUs', 'Bneg'], ['smout'])
                DMA(smo.rearrange('b h -> h b'), smout[0:4, :], ['smout'], ['o_smo'], key='o_sm', slow=True)
            if os.environ.get('GSTOP') == '3':
                return
            TS(tg[:, 0:TTn], tg[:, 0:TTn], 80.0, None, ALU.min, None, ['tg'], ['tg'])
            ACT(Fst[:, 0:TTn], tg[:, 0:TTn], AF.Exp, ['tg'], ['Fst'])
            pd = bank(1)
            for h in range(4):
                MM(pd[:, 4 * h:4 * h + 4], Esel[0:4, h, :], dd[0:4, 0:4], True, True, ['Esel', 'dd'], bkeys(1))
            CP(DECs[:].rearrange('p h c -> p (h c)'), pd[:, 0:16], bkeys(1), ['DECs'])
            if has_s:
                for h in range(4):
                    MM(pd[:, 64 + NB * h:64 + NB * h + NB], Esel[0:4, h, :], dds[0:4, :], True, True, ['Esel', 'dds'], bkeys(1))
                CP(DECss[:].rearrange('p h c -> p (h c)'), pd[:, 64:64 + 4 * NB], bkeys(1), ['DECss'])
            if os.environ.get('GSTOP') == '4':
                return
            pf = bank(7)
            for g in range(4):
                TR(pf[:, g * 128:(g + 1) * 128], Fst[:, g * 128:(g + 1) * 128], ident[:, :], ['Fst', 'ident'], bkeys(7))
            CP(FT[:, 0:4, :].rearrange('p g (f h) -> p g f h', h=4), V(pf[:, 0:1], [[128, 4], [32, 4], [1, 4]]), bkeys(7), ['FT'])
            if has_s:
                pf2 = bank(6)
                TR(pf2[0:64, 0:128], Fst[:, TP:TP + NS], ident[:, :], ['Fst', 'ident'], bkeys(6))
                CP(FT[0:64, 4, :].rearrange('p (f h) -> p f h', h=4), V(pf2[0:64, 0:1], [[32, 4], [1, 4]]), bkeys(6), ['FT'])

        def mlstm_group(g, npp, c0, c_idx, is_s):
            zq = [('qT', 0), ('qT', TP), ('kT', 0), ('kT', TP)]
            pS = bank(0).rearrange('p (h t) -> p h t', t=128)
            for h in range(4):
                MM(pS[:npp, h, 0:npp], kT[:, h, c0:c0 + npp], qT[:, h, c0:c0 + npp], True, True, zq, bkeys(0))
                yield
            mk = masks if is_s else maskp
            for h in range(4):
                STT(Sp[:npp, h, 0:npp], pS[:npp, h, 0:npp], FT[:npp, g, h:h + 1], mk[:npp, :npp], ALU.mult, ALU.mult, bkeys(0) + ['FT', 'masks', 'maskp'], ['Sp'])
                yield
            TTo(kw[:npp, :, :], ktok[:npp, g, :].rearrange('p (h c) -> p h c', c=128), V(FT[:npp, g, 0:1], [[1, 4], [0, 128]]), ALU.mult, [zk('ktok', g), 'FT'], ['kw'])
            yield
            pKV = bank(1, 2).rearrange('p (h c) -> p h c', c=256)
            pO1 = bank(3, 2).rearrange('p (h c) -> p h c', c=256)
            pO2 = bank(3, 2).rearrange('p (h c) -> p h c', c=256)
            vk = [zk('vaug', g), 'vones']
            if not is_s:
                for h in range(4):
                    MM(pKV[:, h, 0:129], kw[:, h, :], vaug[:, g, h, 0:129], True, True, ['kw'] + vk, bkeys(1, 2))
                    yield
                for h in range(4):
                    MM(pO1[:, h, 0:129], qT[:, h, c0:c0 + 128], Cbf[:, h, :], True, True, zq + ['Cbf'], bkeys(3, 2))
                    yield
            else:
                MSET(tmpo[:64], 0.0, ['tmpo'])
                yield
                for b in range(NB):
                    i = b % 2
                    TTo(qTm[:, i, :, :], qT[:, :, c0:c0 + 64], V(bmb[:, b, 0:1], [[0, 4], [1, 64]]), ALU.mult, zq + ['bmb'], [('qTm', i)])
                    yield
                    DMA(Cb[i][:, :, 0:128], sC[b].rearrange('h k v -> k h v'), (), [('Cb', i)], key='cb%d' % i)
                    yield
                    CP(Cb[i][:, :, 128:129], nTin[:, 4 * b:4 * b + 4].rearrange('p (h o) -> p h o', o=1), ['nTin'], [('Cb', i)], eng='act')
                    yield
                    CP(Cbb[i][:], Cb[i][:], [('Cb', i)], [('Cbb', i)], eng='act')
                    yield
                    for h in range(4):
                        MM(pO1[0:64, h, 0:129], qTm[:, i, h, :], Cbb[i][:, h, :], True, True, [('qTm', i), ('Cbb', i)], bkeys(3, 2))
                        yield
                    TTo(tmpo[:64], pO1[0:64, :, 0:129], tmpo[:64], ALU.add, bkeys(3, 2) + ['tmpo'], ['tmpo'])
                    yield
                    TS(kwm[:, :, :].rearrange('p h c -> p (h c)'), kw[0:64, :, :].rearrange('p h c -> p (h c)'), bmT[:, b:b + 1], None, ALU.mult, None, ['kw', 'bmT'], ['kwm'])
                    yield
                    for h in range(4):
                        MM(pKV[:, h, 0:129], kwm[:, h, :], vaug[0:64, g, h, 0:129], True, True, ['kwm'] + vk, bkeys(1, 2))
                        yield
                    TTo(Cb[i][:], Cb[i][:], V(DECss[:, 0, b:b + 1], [[NB, 4], [0, 129]]), ALU.mult, [('Cb', i), 'DECss'], [('Cb', i)])
                    yield
                    TTo(Cb[i][:], Cb[i][:], pKV[:, :, 0:129], ALU.add, [('Cb', i)] + bkeys(1, 2), [('Cb', i)])
                    yield
                    DMA(sCo[b].rearrange('h k v -> k h v'), Cb[i][:, :, 0:128], [('Cb', i)], ['o_sC%d' % i], key='o_sC%d' % i)
                    yield
                    CP(nTout[:, 4 * b:4 * b + 4].rearrange('p (h o) -> p h o', o=1), Cb[i][:, :, 128:129], [('Cb', i)], ['nTout'], eng='act')
                    yield
            if is_s:
                TTo(tmpo[:npp], tmpo[:npp], V(FT[:npp, g, 8:9], [[1, 4], [0, 129]]), ALU.mult, ['tmpo', 'FT'], ['tmpo'])
                yield
            else:
                TTo(tmpo[:npp], pO1[:npp, :, 0:129], V(FT[:npp, g, 8:9], [[1, 4], [0, 129]]), ALU.mult, bkeys(3, 2) + ['FT'], ['tmpo'])
                yield
            for h in range(4):
                MM(pO2[:npp, h, 0:129], Sp[:npp, h, 0:npp], vaug[:npp, g, h, 0:129], True, True, ['Sp'] + vk, bkeys(3, 2))
                yield
            for h in range(4):
                STT(ND[:npp, h, :], pO2[:npp, h, 0:129], FT[:npp, g, 4 + h:5 + h], tmpo[:npp, h, :], ALU.mult, ALU.add, bkeys(3, 2) + ['FT', 'tmpo'], ['ND'])
                yield
            TS(q5[:npp, 0, :], ND[:npp, :, 128], -1.0, None, ALU.mult, None, ['ND'], ['q5'])
            yield
            TTo(q5[:npp, 0, :], q5[:npp, 0, :], ND[:npp, :, 128], ALU.max, ['ND', 'q5'], ['q5'])
            yield
            TTo(q5[:npp, 0, :], q5[:npp, 0, :], FT[:npp, g, 12:16], ALU.max, ['q5', 'FT'], ['q5'])
            yield
            RCP(q5[:npp, 1, :], q5[:npp, 0, :], ['q5'], ['q5'])
            yield
            MSET(q5[:npp, 2, :], 0.0, ['q5'])
            yield
            for h in range(4):
                ACT(kw[:npp, h, :], ND[:npp, h, 0:128], AF.Square, ['ND'], ['kw', 'q5'], accum_out=q5[:npp, 2, h:h + 1])
                yield
            TTo(q5[:npp, 3, :], q5[:npp, 1, :], q5[:npp, 1, :], ALU.mult, ['q5'], ['q5'])
            yield
            TTo(q5[:npp, 4, :], q5[:npp, 3, :], q5[:npp, 2, :], ALU.mult, ['q5'], ['q5'])
            yield
            ACT(q5[:npp, 5, :], q5[:npp, 4, :], AF.Sqrt, ['q5'], ['q5'], scale=1.0 / 128, bias=EPS)
            yield
            RCP(q5[:npp, 6, :], q5[:npp, 5, :], ['q5'], ['q5'])
            yield
            TTo(q5[:npp, 7, :], q5[:npp, 6, :], q5[:npp, 1, :], ALU.mult, ['q5'], ['q5'])
            yield
            TTo(og[:npp, :], osig[:npp, g, :], mngb[:npp, :], ALU.mult, [zk('osig', g), 'mngb'], ['og'])
            yield
            TTo(t5[:npp, :, 0:128], ND[:npp, :, 0:128], V(q5[:npp, 7, 0:1], [[1, 4], [0, 128]]), ALU.mult, ['ND', 'q5', 'tmpo'], ['tmpo'])
            yield
            TTo(U[:npp, g, 0:512].rearrange('p (h c) -> p h c', c=128), t5[:npp, :, 0:128], og[:npp, :].rearrange('p (h c) -> p h c', c=128), ALU.mult, ['tmpo', 'og'], [('U', g, 0)])
            yield
            if not is_s:
                TTo(Cst[:], Cst[:], V(DECs[:, 0, c_idx:c_idx + 1], [[4, 4], [0, 129]]), ALU.mult, ['Cst', 'DECs'], ['Cst'])
                yield
                TTo(Cst[:], Cst[:], pKV[:, :, 0:129], ALU.add, ['Cst'] + bkeys(1, 2), ['Cst'])
                yield
                CP(Cbf[:], Cst[:], ['Cst'], ['Cbf'], eng='act')
                yield

        def softmax_tail(Sx, Px, npp, nk, R_, half):
            sk_ = sinkb[:npp, 4 * half:4 * half + 4]
            RED(sst[:npp, 0, 0:4], Sx, ALU.max, R_, ['sst'])
            yield
            TTo(sst[:npp, 1, 0:4], sst[:npp, 0, 0:4], sk_, ALU.max, ['sst', 'sinkb'], ['sst'])
            yield
            TS(sst[:npp, 7, 0:4], sst[:npp, 1, 0:4], -1.0, None, ALU.mult, None, ['sst'], ['sst'])
            yield
            MSET(sst[:npp, 2, 0:4], 0.0, ['sst'])
            yield
            for hh_ in range(4):
                ACT(Px[:, hh_, :], Sx[:, hh_, :], AF.Exp, R_ + ['sst'], ['Px', 'sst'], bias=sst[:npp, 7, hh_:hh_ + 1], accum_out=sst[:npp, 2, hh_:hh_ + 1])
                yield
            TTo(sst[:npp, 3, 0:4], sk_, sst[:npp, 1, 0:4], ALU.subtract, ['sst', 'sinkb'], ['sst'])
            yield
            ACT(sst[:npp, 4, 0:4], sst[:npp, 3, 0:4], AF.Exp, ['sst'], ['sst'])
            yield
            TTo(sst[:npp, 5, 0:4], sst[:npp, 4, 0:4], sst[:npp, 2, 0:4], ALU.add, ['sst'], ['sst'])
            yield
            RCP(sst[:npp, 6, 0:4], sst[:npp, 5, 0:4], ['sst'], ['sst'])
            yield
            return sst[:npp, 6, 0:4]

        def swa_group(g, first):
            zq = [('qaT', 0), ('kaT', 0), 'kaT_h']
            c0 = g * 128
            vkeys = ['va0', ('va', g)] + ([('va', g - 1)] if g > 0 else [])
            for half in range(2):
                pS = bank(5, 2).rearrange('p (h k) -> p h k', k=256)
                for hh in range(4):
                    h = 4 * half + hh
                    MM(pS[:, hh, :], qaT[:, h // 2, c0:c0 + 128], kaT[:, 2 * half + h % 2, c0:c0 + 256], True, True, zq, bkeys(5, 2))
                    yield
                for hh in range(4):
                    h = 4 * half + hh
                    STT(Ssb[:, hh, :], biasT[:, 0, :], slopeb[:, h:h + 1], pS[:, hh, :], ALU.mult, ALU.add, bkeys(5, 2) + ['biasT', 'slopeb'], ['Ssb'])
                    yield
                if first:
                    TS(Ssb[:, :, 0:128], Ssb[:, :, 0:128], role[:, 16:17], None, ALU.add, None, ['Ssb', 'role'], ['Ssb'])
                    yield
                if os.environ.get('SSTOP') == '1':
                    continue
                rden = (yield from softmax_tail(Ssb[:], Pb[:], 128, 256, ['Ssb'], half))
                if os.environ.get('SSTOP') == '2':
                    continue
                pb_ = 7
                PT = bank(pb_).bitcast(BF16).rearrange('p (h t) -> p h t', t=128)
                for hh in range(4):
                    for kt in range(2):
                        TR(PT[:, 2 * hh + kt, :], Pb[:, hh, kt * 128:(kt + 1) * 128], identb[:, :], ['Px', 'identb'], bkeys(pb_))
                        yield
                CP(PTs[:, 0:4, :], PT[:, 0:4, :], bkeys(pb_), ['PTs'], eng='act')
                yield
                CP(PTs[:, 4:8, :], PT[:, 4:8, :], bkeys(pb_), ['PTs'], eng='act')
                yield
                if os.environ.get('SSTOP') == '3':
                    continue
                pO = bank(7).rearrange('p (h c) -> p h c', c=64)
                for hh in range(4):
                    for kt in range(2):
                        MM(pO[:, hh, :], PTs[:, 2 * hh + kt, :], va[:, g + kt, half * 64:(half + 1) * 64], kt == 0, kt == 1, ['PTs'] + vkeys, bkeys(7))
                        yield
                TTo(U[:, g, 512 + 256 * half:768 + 256 * half].rearrange('p (h c) -> p h c', c=64), pO[:, 0:4, :], V(rden[:, 0:1], [[1, 4], [0, 64]]), ALU.mult, bkeys(7) + ['sst'], [('U', g, 1 + half)])
                yield

        def swa_sample():
            zq = [('qaT', TP), ('kaT', TP)]
            Ss = Ssb[0:4, 0:3, :].rearrange('p a k -> p (a k)').rearrange('p (h k) -> p h k', k=192)
            Ps = Pb[0:4, 0:3, :].rearrange('p a k -> p (a k)').rearrange('p (h k) -> p h k', k=192)
            for b in range(NB):
                i = b % 2
                for v in range(4):
                    hk, par = (v // 2, v % 2)
                    DMA(ckb[i][:, v, 64 * par:64 * par + 64], ck[b][:, 64 * hk:64 * hk + 64], [('ckb', i)], [('ckb', i, v)], key='ck%d_%d' % (i, v), eng='pool')
                    yield
                DMA(cvb[i][:], cv[b], (), [('cvb', i)], key='cv%d' % i, eng='pool')
                yield
                pk_ = bank(7)
                for v in range(4):
                    MM(pk_[:, v * 128:(v + 1) * 128], ckb[i][:, v, :], identb[:, :], True, True, [('ckb', i), ('ckb', i, v), 'identb'], bkeys(7))
                    yield
                CP(kTc[i][:].rearrange('p h t -> p (h t)'), pk_[:, 0:512], bkeys(7), [('kTc', i)], eng='act')
                yield
                for half in range(2):
                    pSc = bank(5).rearrange('p (h k) -> p h k', k=128)
                    pSn = bank(6).rearrange('p (h k) -> p h k', k=64)
                    for hh in range(4):
                        h = 4 * half + hh
                        q_ = qaT[:, h // 2, TP + 4 * b:TP + 4 * b + 4]
                        MM(pSc[0:4, hh, :], q_, kTc[i][:, 2 * half + h % 2, :], True, True, zq + [('kTc', i)], bkeys(5))
                        yield
                        MM(pSn[0:4, hh, :], q_, kaT[:, 2 * half + h % 2, 128 + TP:128 + TP + 64], True, True, zq, bkeys(6))
                        yield
                    for hh in range(4):
                        h = 4 * half + hh
                        STT(Ss[0:4, hh, 0:128], bsc[0:4, 0, :], slopeb[0:4, h:h + 1], pSc[0:4, hh, :], ALU.mult, ALU.add, bkeys(5) + ['bsc', 'slopeb'], ['Ssb'])
                        yield
                        STT(Ss[0:4, hh, 128:192], tbl[0:4, 0, 60 - 4 * b:124 - 4 * b], slopeb[0:4, h:h + 1], pSn[0:4, hh, :], ALU.mult, ALU.add, bkeys(6) + ['tbl', 'slopeb'], ['Ssb'])
                        yield
                    rden = (yield from softmax_tail(Ss, Ps, 4, 192, ['Ssb'], half))
                    PT = bank(6).bitcast(BF16)[:, 0:32].rearrange('p (h k q) -> p h k q', k=2, q=4)
                    for hh in range(4):
                        TR(PT[:, hh, 0, :], Ps[0:4, hh, 0:128], identb[0:4, 0:4], ['Px', 'identb'], bkeys(6))
                        yield
                        TR(PT[0:64, hh, 1, :], Ps[0:4, hh, 128:192], identb[0:4, 0:4], ['Px', 'identb'], bkeys(6))
                        yield
                    CP(PTss[:, 0:4, 0, :], PT[:, :, 0, :], bkeys(6), ['PTss'], eng='act')
                    yield
                    CP(PTss[0:64, 0:4, 1, :], PT[0:64, :, 1, :], bkeys(6), ['PTss'])
                    yield
                    pO = bank(7).rearrange('p (h c) -> p h c', c=64)
                    for hh in range(4):
                        MM(pO[0:4, hh, :], PTss[:, hh, 0, :], cvb[i][:, half * 64:(half + 1) * 64], True, False, ['PTss', ('cvb', i)], bkeys(7))
                        yield
                        MM(pO[0:4, hh, :], PTss[0:64, hh, 1, :], va[0:64, 5, half * 64:(half + 1) * 64], False, True, ['PTss', ('va', 4)], bkeys(7))
                        yield
                    TTo(uab[i][0:4, 256 * half:256 * half + 256].rearrange('p (h c) -> p h c', c=64), pO[0:4, 0:4, :], V(rden[:, 0:1], [[1, 4], [0, 64]]), ALU.mult, bkeys(7) + ['sst'], [('uab', i)])
                    yield
                DMA(U[4 * b:4 * b + 4, 4, 512:1024], uab[i][0:4, :], [('uab', i)], [('U', 4, 1, b)], key='uab%d' % i)
                yield
                DMA(sk[b, 0:124, :], ck[b, 4:128, :], (), [('o_sk', b)], key='o_sk')
                yield
                DMA(sv[b, 0:124, :], cv[b, 4:128, :], (), [('o_sv', b)], key='o_sv')
                yield

        def w_out_stage(has_s):
            grp = groups_of(has_s)
            for g, npp, c0 in grp:
                b = 6 + R2('pT')
                pT = bank(b).bitcast(BF16).rearrange('p (c t) -> p c t', t=128)
                for c in range(8):
                    TR(pT[:, c, 0:npp], U[:npp, g, c * 128:(c + 1) * 128], identb[:npp, :npp], [('U', g, 0), ('U', g, 1), ('U', g, 2), 'identb'] + [('U', 4, 1, bb) for bb in range(NB)], bkeys(b))
                CP(hnT[:, :, c0:c0 + npp], pT[:, :, 0:npp], bkeys(b), [('hnT', g)], eng='act')
            load_gp(3)
            slots = [wload(colblk(wout, 256 * blk, 256)) for blk in range(4)]
            for g, npp, c0 in grp:
                pyb = (4, 0)[R2('py')]
                py = bank(pyb, 2)
                for blk in range(4):
                    ws, wk_ = slots[blk]
                    for kc in range(8):
                        MM(py[:npp, 256 * blk:256 * blk + 256], hnT[:, kc, c0:c0 + npp], ws[:, kc, 0:256], kc == 0, kc == 7, [wk_, ('hnT', g)], bkeys(pyb, 2))
                postnorm(py, bkeys(pyb, 2), 1.0, g, npp)
                if next_pre[0] is not None:
                    prenorm_group(next_pre[0], g, npp, c0)
        ZW = o_[0]
        ZSET = {'qT', 'kT', 'qaT', 'kaT', 'va', 'z', 'vones', 'kaT_h', 'va0', 'hT'}

        def zkeys():
            return [k for k in S.last_writer if (k[0] if isinstance(k, tuple) else k) in ZSET]

        def mlstm_state_only(g, c_idx):
            TTo(kw[:, :, :], ktok[:, g, :].rearrange('p (h c) -> p h c', c=128), V(FT[:, g, 0:1], [[1, 4], [0, 128]]), ALU.mult, [zk('ktok', g), 'FT'], ['kw'])
            pKV = bank(2, 2).rearrange('p (h c) -> p h c', c=256)
            for h in range(4):
                MM(pKV[:, h, 0:129], kw[:, h, :], vaug[:, g, h, 0:129], True, True, ['kw', zk('vaug', g), 'vones'], bkeys(2, 2))
            TTo(Cst[:], Cst[:], V(DECs[:, 0, c_idx:c_idx + 1], [[4, 4], [0, 129]]), ALU.mult, ['Cst', 'DECs'], ['Cst'])
            TTo(Cst[:], Cst[:], pKV[:, :, 0:129], ALU.add, ['Cst'] + bkeys(2, 2), ['Cst'])
        first_wbig = [True]
        prenorm(0, nt == 1 and sample)
        for t in range(nt):
            has_s = t == nt - 1 and sample
            next_pre[0] = 2
            ffn(0, 1, has_s)
            next_pre[0] = None
            load_wbig(0 if t + 1 < nt else 1)
            w_in(has_s)
            DMA(gsI[t], IGs[:], ['IGs'], ['gsI'], key='sp_gI')
            DMA(gsF[t], FGs[:], ['FGs'], ['gsF'], key='sp_gF')
            DMA(x1s[t], X[:].rearrange('p g d -> p (g d)'), [('X', g) for g in range(5)], ['x1s'], key='sp_x')
            DMA(zs[t, :, 0:ZW], big[:, 0:ZW], zkeys(), ['zs'], key='sp_z')
            if t + 1 < nt:
                nhs = t + 1 == nt - 1 and sample
                DMA(X[:, 0:4, :], xp[(t + 1) * TP:(t + 2) * TP, :].rearrange('(g p) d -> p g d', p=128), (), [('X', g) for g in range(4)], key='x_in')
                if nhs:
                    DMA(X[0:64, 4, :], xs[:, :], (), [('X', 4)], key='x_in_s')
                prenorm(0, nhs)
            gates(False, phase=1)
            for g in range(4):
                mlstm_state_only(g, g)
            if has_s:
                DMA(pk[:, :], kvf[:, 0, 0:128], [('kvf', 3)], ['o_pk'], key='o_pkv')
                DMA(pv[:, :], kvf[:, 0, 128:256], [('kvf', 3)], ['o_pv'], key='o_pkv')
                for b in range(NB):
                    DMA(sk[b, 124:128, :], kvf[4 * b:4 * b + 4, 1, 0:128], [('kvf', 4)], [('o_sk2', b)], key='o_pkv')
                    DMA(sv[b, 124:128, :], kvf[4 * b:4 * b + 4, 1, 128:256], [('kvf', 4)], [('o_sv2', b)], key='o_pkv')
        pay = tmpn[:, 0:PW]
        CP(pay[:, 0:516], Cst[:].rearrange('p h c -> p (h c)'), ['Cst'], ['tmpn'])
        MSET(pay[:, 518:520], 0.0, ['tmpn'])
        CP(pay[:, 516:517], MUprev[:, 0:1], ['MUprev'], ['tmpn'])
        CP(pay[:, 517:518], Bprev[:, 0:1], ['Bprev'], ['tmpn'])
        CP(pay[:, 520:776].bitcast(BF16).rearrange('p (v t) -> p v t', t=128), kaT[:, :, TP:TP + 128], zkeys(), ['tmpn'])
        CP(pay[:, 776:840].bitcast(BF16), va[:, 4, :], zkeys(), ['tmpn'])
        DMA(exin[:, :], pay, ['tmpn'], ['exin'], key='ex_in')

        def reload(t):
            allk = zkeys()
            DMA(big[:, 0:ZW], zs[t, :, 0:ZW], ['zs'], allk + ['kw'], key='rl_z')
            DMA(X[:].rearrange('p g d -> p (g d)'), x1s[t], ['x1s'], [('X', g) for g in range(5)], key='rl_x')

        def reload_g(t):
            DMA(IGs[:], gsI[t], ['gsI'], ['IGs'], key='rl_gI')
            DMA(FGs[:], gsF[t], ['gsF'], ['FGs'], key='rl_gF')
        reload(0)
        reload_g(0)
        S.op('pool', lambda e: e.collective_compute('AllGather', ALU.bypass, replica_groups=[[0, 1, 2, 3], [4, 5, 6, 7]], ins=[exin.ap().opt()], outs=[exout.ap().opt()]), ['exin'], ['exout'], dma_key='cc', inc=1)
        for g_ in (1, 2, 3):
            drain(swa_group(g_, False))
        MSET(Cst[:], 0.0, ['Cst'])
        MSET(cmb[:, 0:1], 0.0, ['cmb'])
        MSET(kaTh[:], 0.0, ['kaTh'])
        MSET(vah[:], 0.0, ['vah'])
        payr = Ssb[:].rearrange('p h k -> p (h k)')[:, 0:PW]
        for r in range(3):
            DMA(payr, exout[r * 128:(r + 1) * 128, :], ['exout'], ['Ssb'], key='ex_rd')
            mk_ = role[:, r:r + 1]
            TS(cmb[:, 1:2], payr[:, 517:518], mk_, None, ALU.mult, None, ['Ssb', 'role'], ['cmb'])
            TTo(cmb[:, 2:3], payr[:, 516:517], payr[:, 517:518], ALU.subtract, ['Ssb'], ['cmb'])
            TS(cmb[:, 2:3], cmb[:, 2:3], -NEG, mk_, ALU.add, ALU.mult, ['cmb', 'role'], ['cmb'])
            TS(cmb[:, 2:3], cmb[:, 2:3], NEG, None, ALU.add, None, ['cmb'], ['cmb'])
            TTo(cmb[:, 3:4], cmb[:, 0:1], cmb[:, 1:2], ALU.subtract, ['cmb'], ['cmb'])
            TTo(cmb[:, 4:5], cmb[:, 3:4], cmb[:, 2:3], ALU.max, ['cmb'], ['cmb'])
            TTo(cmb[:, 5:6], cmb[:, 3:4], cmb[:, 4:5], ALU.subtract, ['cmb'], ['cmb'])
            TTo(cmb[:, 6:7], cmb[:, 2:3], cmb[:, 4:5], ALU.subtract, ['cmb'], ['cmb'])
            ACT(cmb[:, 7:9], cmb[:, 5:7], AF.Exp, ['cmb'], ['cmb'])
            TS(cmb[:, 8:9], cmb[:, 8:9], mk_, None, ALU.mult, None, ['cmb', 'role'], ['cmb'])
            CP(cmb[:, 0:1], cmb[:, 4:5], ['cmb'], ['cmb'])
            pd = bank(1)
            for h in range(4):
                MM(pd[:, 2 * h:2 * h + 2], Esel[0:4, h, :], cmb[0:4, 7:9], True, True, ['Esel', 'cmb'], bkeys(1))
            CP(ABt[:].rearrange('p h c -> p (h c)'), pd[:, 0:8], bkeys(1), ['ABt'])
            TTo(Cst[:], Cst[:], V(ABt[:, 0, 0:1], [[2, 4], [0, 129]]), ALU.mult, ['Cst', 'ABt'], ['Cst'])
            TTo(tmpo[:], payr[:, 0:516].rearrange('p (h c) -> p h c', c=129), V(ABt[:, 0, 1:2], [[2, 4], [0, 129]]), ALU.mult, ['Ssb', 'ABt'], ['tmpo'])
            TTo(Cst[:], Cst[:], tmpo[:], ALU.add, ['Cst', 'tmpo'], ['Cst'])
            STT(kaTh[:].rearrange('p v t -> p (v t)'), payr[:, 520:776].bitcast(BF16), role[:, 8 + r:9 + r], kaTh[:].rearrange('p v t -> p (v t)'), ALU.mult, ALU.add, ['Ssb', 'role', 'kaTh'], ['kaTh'])
            STT(vah[:], payr[:, 776:840].bitcast(BF16), role[:, 8 + r:9 + r], vah[:], ALU.mult, ALU.add, ['Ssb', 'role', 'vah'], ['vah'])
        CP(MUprev[:, 0:1], cmb[:, 0:1], ['cmb'], ['MUprev'])
        MSET(Bprev[:], 0.0, ['Bprev'])
        CP(Cbf[:], Cst[:], ['Cst'], ['Cbf'], eng='act')
        for t in range(nt):
            has_s = t == nt - 1 and sample
            grp = groups_of(has_s)
            if t > 0:
                reload(t)
            CP(kaT[:, :, 0:128], kaTh[:], ['kaTh'], ['kaT_h'], eng='act')
            CP(va[:, 0, :], vah[:], ['vah'], ['va0'], eng='act')
            if t == 0:
                gates(has_s, phase=2)
            for g, npp, c0 in grp:
                gens_ = [mlstm_group(g, npp, c0, g, g == 4)]
                if g < 4:
                    if t > 0 or g == 0:
                        gens_.append(swa_group(g, t == 0 and g == 0))
                else:
                    gens_.append(swa_sample())
                run_rr(gens_)
            CP(kaTh[:], kaT[:, :, TP:TP + 128], [('kaT', 0)], ['kaTh'], eng='act')
            CP(vah[:], va[:, 4, :], [('va', 3)], ['vah'], eng='act')
            if t + 1 < nt:
                reload_g(t + 1)
                gates(t + 1 == nt - 1 and sample, phase=2)
            next_pre[0] = 4
            w_out_stage(has_s)
            next_pre[0] = None
            ffn(1, 5, has_s)
            if t + 1 < nt:
                load_wbig(1)
            DMA(yp[t * TP:(t + 1) * TP, :].rearrange('(g p) d -> p g d', p=128), X[:, 0:4, :], [('X', g) for g in range(4)], ['o_yp'], key='o_y')
            if has_s:
                DMA(ys[:, :], X[0:64, 4, :], [('X', 4)], ['o_ys'], key='o_y')
        DMA(pC.rearrange('h k v -> k h v'), Cst[:, :, 0:128], ['Cst'], ['o_pC'], key='o_fin')
        TR(bank(1)[0:4, 0:128], Cst[:, :, 128], ident[:, :], ['Cst', 'ident'], bkeys(1))
        CP(pn_sb[:], bank(1)[0:4, 0:128], bkeys(1), ['pn_sb'])
        DMA(pn[:, :], pn_sb[:], ['pn_sb'], ['o_pn'], key='o_fin')
        TTo(pm_sb[:], MUprev[:], Bprev[:], ALU.subtract, ['MUprev', 'Bprev'], ['pm_sb'])
        DMA(pm[:, :], pm_sb[0:4, :], ['pm_sb'], ['o_pm'], key='o_fin')
        TR(bank(1)[0:64, 128:256], nTout[:, :], ident[:, :], ['nTout', 'ident'], bkeys(1))
        CP(snout[:], bank(1)[0:64, 128:256], bkeys(1), ['snout'])
        DMA(sno[:, :], snout[:], ['snout'], ['o_sno'], key='o_fin')
        out_keys = [k for k in S.dma_counts if k.startswith('o_')]
        S.emit(final_wait_keys=out_keys)
    return nc

def _consts():
    c = {}
    c['c_ident'] = np.eye(128, dtype=np.float32)
    s = np.arange(128)
    c['c_maskp'] = (s[:, None] <= s[None, :]).astype(np.float32)
    s = np.arange(64)
    c['c_masks'] = ((s[:, None] <= s[None, :]) & (s[:, None] // 4 == s[None, :] // 4)).astype(np.float32)
    slopes = np.exp2(-8.0 * np.arange(1, 9, dtype=np.float32) / 8).astype(np.float32)
    c['c_slope'] = np.broadcast_to(slopes[None, :], (128, 8)).copy()
    BIGN = -8000000.0
    qi = np.arange(128)[:, None]
    kj = np.arange(256)[None, :]
    dist = 128 + qi - kj
    valid = (dist >= 0) & (dist < 128)
    c['c_bias'] = np.where(valid, -dist.astype(np.float32), BIGN).astype(np.float32)
    t = np.arange(4)[:, None]
    j = np.arange(128)[None, :]
    d = 128 + t - j
    v = (d >= 0) & (d < 128)
    c['c_bsc'] = np.where(v, -d.astype(np.float32), BIGN).astype(np.float32)
    x = np.arange(124)[None, :] - 60
    d2 = t - x
    v2 = (x >= 0) & (x <= t)
    c['c_tb'] = np.where(v2, -d2.astype(np.float32), BIGN).astype(np.float32)
    bm = (np.arange(64)[None, :] // 4 == np.arange(NB)[:, None]).astype(np.float32)
    c['c_bm'] = np.broadcast_to(bm.reshape(1, NB * 64), (128, NB * 64)).copy()
    c['c_bmT'] = bm.T.copy()
    E = np.zeros((4, 4, 128), np.float32)
    for h in range(4):
        E[h, h, :] = 1.0
    c['c_E'] = E.reshape(4, 4 * 128)
    return c
_NC = None

def kernel(x_prompt, x_sample, cache_swa_k, cache_swa_v, state_mlstm_C, state_mlstm_n, state_mlstm_m, norm_gains, ffn_w_gate, ffn_w_up, ffn_w_down, w_in, b_gate, mlstm_norm_gain, attn_sinks, w_out):
    global _NC
    f = lambda a: np.ascontiguousarray(np.asarray(a, dtype=np.float32))
    x_prompt, x_sample = (f(x_prompt), f(x_sample))
    ckk, cvv = (f(cache_swa_k)[0], f(cache_swa_v)[0])
    sCC, snn, smm = (f(state_mlstm_C)[0], f(state_mlstm_n)[0], f(state_mlstm_m)[0])
    consts = _consts()
    shared = dict(gains=f(norm_gains)[0], wg=f(ffn_w_gate)[0], wu=f(ffn_w_up)[0], wd=f(ffn_w_down)[0], win=f(w_in)[0], bgate=f(b_gate)[0], mng=f(mlstm_norm_gain)[0], sinks=f(attn_sinks)[0], wout=f(w_out)[0])
    shared.update(consts)
    in_maps = []
    for c in range(8):
        m = dict(shared)
        m['xp'] = np.ascontiguousarray(x_prompt[c // 4, SEQ * (c % 4):SEQ * (c % 4 + 1)])
        role = np.zeros((128, 17), np.float32)
        for r in range(4):
            if r < c % 4:
                role[:, r] = 1.0
            if r == c % 4 - 1:
                role[:, 8 + r] = 1.0
        role[:, 16] = NEG if c % 4 == 0 else 0.0
        m['c_role'] = role
        b0 = NB * c
        m['xs'] = x_sample[b0:b0 + NB].reshape(NS, D)
        m['ck'] = ckk[b0:b0 + NB].reshape(NB, 128, 128)
        m['cv'] = cvv[b0:b0 + NB].reshape(NB, 128, 128)
        m['sC'] = sCC[b0:b0 + NB]
        m['sn'] = snn[b0:b0 + NB].reshape(NB * 4, 128)
        m['sm'] = smm[b0:b0 + NB]
        in_maps.append(m)
    if _NC is None:
        _NC = build_program()
    res = run_bass_kernel_spmd(_NC, in_maps, core_ids=list(range(8)))
    r = res.results
    yp = np.stack([np.concatenate([r[4 * b + j]['yp'] for j in range(4)], 0) for b in range(2)], 0)
    ys = np.concatenate([r[c]['ys'].reshape(NB, 4, D) for c in range(8)], 0)
    pk = np.stack([r[3]['pk'], r[7]['pk']], 0).reshape(1, 2, 128, 2, 64)
    pv = np.stack([r[3]['pv'], r[7]['pv']], 0).reshape(1, 2, 128, 2, 64)
    pC = np.stack([r[3]['pC'], r[7]['pC']], 0)[None]
    pn = np.stack([r[3]['pn'], r[7]['pn']], 0)[None]
    pm = np.stack([r[3]['pm'].reshape(4), r[7]['pm'].reshape(4)], 0)[None]
    sk = np.concatenate([r[c]['sk'] for c in range(8)], 0).reshape(1, 128, 128, 2, 64)
    sv = np.concatenate([r[c]['sv'] for c in range(8)], 0).reshape(1, 128, 128, 2, 64)
    sCo = np.concatenate([r[c]['sCo'] for c in range(8)], 0)[None]
    sno = np.concatenate([r[c]['sno'].reshape(NB, 4, 128) for c in range(8)], 0)[None]
    smo = np.concatenate([r[c]['smo'] for c in range(8)], 0)[None]
    outs = (yp, ys, pk, pv, pC, pn, pm, sk, sv, sCo, sno, smo)
    return tuple((np.ascontiguousarray(o, dtype=np.float32) for o in outs))
```

```python
import contextlib
import os
import numpy as np
import concourse.bass as bass
import concourse.mybir as mybir
from concourse.bass_utils import run_bass_kernel_spmd
F32 = mybir.dt.float32
BF16 = mybir.dt.bfloat16
ALU = mybir.AluOpType
AF = mybir.ActivationFunctionType
AX = mybir.AxisListType
ENGS = ('pe', 'act', 'dve', 'pool', 'sp')
D = 1024
DFF = 2816
NJ = 22
DIN = 2824
SEQ = 2048
NTILE = 4
PW = 840
TP = 512
NS = 64
NB = 16
EPS = 1e-06
NEG = -30000.0

class _Op:
    __slots__ = ('eng', 'fn', 'deps', 'dma_key', 'dma_cnt', 'signal', 'sig_val', 'idx', 'inc')

class Sched:
    def __init__(self, nc):
        self.nc = nc
        self.ops = []
        self.last_writer = {}
        self.readers = {}
        self.dma_counts = {}
    ALIAS = {'e1': 'FGs', 'lfn': 'FGs', 'Fst': 'tg', 'Ug': 'IGs', 't5': 'tmpo'}

    def _norm(self, k):
        if isinstance(k, tuple) and k[0] in ('IGs', 'FGs'):
            k = k[0]
        return self.ALIAS.get(k, k) if not isinstance(k, tuple) else k

    def op(self, eng, fn, reads=(), writes=(), dma_key=None, inc=16):
        reads = [self._norm(k) for k in reads]
        writes = [self._norm(k) for k in writes]
        writes = writes + [k for k in reads if isinstance(k, tuple) and k[0] == 'bank']
        reads = [k for k in reads if not (isinstance(k, tuple) and k[0] == 'bank')]
        o = _Op()
        o.eng, o.fn, o.idx, o.dma_key, o.inc = (eng, fn, len(self.ops), dma_key, inc)
        o.signal, o.sig_val = (False, None)
        deps = set()
        for r in reads:
            w = self.last_writer.get(r)
            if w is not None:
                deps.add(w)
        for r in writes:
            w = self.last_writer.get(r)
            if w is not None:
                deps.add(w)
            deps.update(self.readers.get(r, ()))
        o.deps = deps
        if dma_key is not None:
            self.dma_counts[dma_key] = self.dma_counts.get(dma_key, 0) + inc
            o.dma_cnt = self.dma_counts[dma_key]
        else:
            o.dma_cnt = None
        self.ops.append(o)
        for r in reads:
            self.readers.setdefault(r, []).append(o.idx)
        for r in writes:
            self.last_writer[r] = o.idx
            self.readers[r] = []
        return o.idx

    def emit(self, final_wait_keys=()):
        nc, ops = (self.nc, self.ops)
        for o in ops:
            nd = set()
            for d in o.deps:
                p = ops[d]
                if p.dma_key is None and o.dma_key is None and (p.eng == o.eng == 'pe'):
                    continue
                nd.add(d)
            o.deps = nd
            for d in nd:
                if ops[d].dma_key is None:
                    ops[d].signal = True
        cnt = {e: 0 for e in ENGS}
        for o in ops:
            if o.dma_key is None and o.signal:
                cnt[o.eng] += 1
                o.sig_val = cnt[o.eng]
        with contextlib.ExitStack() as st:
            esem = {e: st.enter_context(nc.semaphore('s_' + e)) for e in ENGS}
            dsem = {}
            for i, k in enumerate(self.dma_counts):
                dsem[k] = st.enter_context(nc.semaphore('d_%d' % i))
            block = st.enter_context(nc.Block())

            def run(ename):

                def body(eng):
                    waited = {}
                    for o in ops:
                        if o.eng != ename:
                            continue
                        need = {}
                        for d in o.deps:
                            p = ops[d]
                            if p.dma_key is not None:
                                s, v = (dsem[p.dma_key], p.dma_cnt)
                            else:
                                s, v = (esem[p.eng], p.sig_val)
                            if need.get(id(s), (None, 0))[1] < v:
                                need[id(s)] = (s, v)
                        for key, (s, v) in need.items():
                            if waited.get(key, 0) < v:
                                eng.wait_ge(s, v)
                                waited[key] = v
                        ins = o.fn(eng)
                        if o.dma_key is not None:
                            ins.then_inc(dsem[o.dma_key], o.inc)
                        elif o.signal:
                            ins.then_inc(esem[ename], 1)
                    if ename == 'sp':
                        for k in final_wait_keys:
                            eng.wait_ge(dsem[k], self.dma_counts[k])
                return body
            block.tensor(run('pe'))
            block.scalar(run('act'))
            block.vector(run('dve'))
            block.gpsimd(run('pool'))
            block.sync(run('sp'))

def drain(gen):
    try:
        while True:
            next(gen)
    except StopIteration as e:
        return e.value

def run_rr(gens):
    gens = list(gens)
    while gens:
        for g_ in list(gens):
            try:
                next(g_)
            except StopIteration:
                gens.remove(g_)

def V(ap, dims):
    return bass.AP(ap.tensor, ap.offset, [list(ap.ap[0])] + [list(d) for d in dims])

def build_program(nt=NTILE, upto=9, sample=True):
    assert nt == NTILE
    nc = bass.Bass('TRN2', target_bir_lowering=False)

    def din(name, shape, dt=F32):
        return nc.dram_tensor(name, list(shape), dt, kind='ExternalInput').ap()

    def dout(name, shape, dt=F32):
        return nc.dram_tensor(name, list(shape), dt, kind='ExternalOutput').ap()
    xp = din('xp', [SEQ, D])
    xs = din('xs', [NS, D])
    ck = din('ck', [NB, 128, 128])
    cv = din('cv', [NB, 128, 128])
    sC = din('sC', [NB, 4, 128, 128])
    sn = din('sn', [NB * 4, 128])
    sm = din('sm', [NB, 4])
    gains = din('gains', [6, D])
    wg = din('wg', [2, D, DFF])
    wu = din('wu', [2, D, DFF])
    wd = din('wd', [2, DFF, D])
    win = din('win', [D, DIN])
    bgate = din('bgate', [8])
    mng = din('mng', [512])
    sinks = din('sinks', [8])
    wout = din('wout', [D, D])
    c_ident = din('c_ident', [128, 128])
    c_maskp = din('c_maskp', [128, 128])
    c_masks = din('c_masks', [64, 64])
    c_bias = din('c_bias', [128, 256])
    c_slope = din('c_slope', [128, 8])
    c_bsc = din('c_bsc', [4, 128])
    c_tb = din('c_tb', [4, 124])
    c_bm = din('c_bm', [128, NB * 64])
    c_bmT = din('c_bmT', [64, NB])
    c_E = din('c_E', [4, 4 * 128])
    c_role = din('c_role', [128, 17])
    x1s = nc.dram_tensor('x1s', [NTILE, 128, 5 * D], F32).ap()
    zs = nc.dram_tensor('zs', [NTILE, 128, 18304], BF16).ap()
    gsI = nc.dram_tensor('gsI', [NTILE, 128, TP + NS], F32).ap()
    gsF = nc.dram_tensor('gsF', [NTILE, 128, TP + NS], F32).ap()
    exin = nc.dram_tensor('exin', [128, PW], F32)
    exout = nc.dram_tensor('exout', [4 * 128, PW], F32)
    yp = dout('yp', [SEQ, D])
    ys = dout('ys', [NS, D])
    pk = dout('pk', [128, 128])
    pv = dout('pv', [128, 128])
    pC = dout('pC', [4, 128, 128])
    pn = dout('pn', [4, 128])
    pm = dout('pm', [4, 1])
    sk = dout('sk', [NB, 128, 128])
    sv = dout('sv', [NB, 128, 128])
    sCo = dout('sCo', [NB, 4, 128, 128])
    sno = dout('sno', [NB * 4, 128])
    smo = dout('smo', [NB, 4])
    S = Sched(nc)
    out_keys = []
    with contextlib.ExitStack() as st:

        def sb(name, shape, dt=F32):
            return st.enter_context(nc.sbuf_tensor(name, list(shape), dt))
        TT_ = TP + NS
        X = sb('X', [128, 5, D])
        hnT = sb('hnT', [128, 8, TT_], BF16)
        big = sb('big', [128, 18304], BF16)
        wblk = [sb('wblk%d' % i, [128, 8, 256], BF16) for i in range(4)]
        wbig = sb('wbig', [128, NJ, D], BF16)
        gT = sb('gT', [128, 6, 8])
        gp = sb('gp', [128, 1, D])
        ident = sb('ident', [128, 128])
        identb = sb('identb', [128, 128], BF16)
        maskp = sb('maskp', [128, 128])
        masks = sb('masks', [64, 64])
        biasT = sb('biasT', [128, 1, 256])
        slopeb = sb('slopeb', [128, 8])
        bsc = sb('bsc', [4, 1, 128])
        tbl = sb('tbl', [4, 1, 124])
        bmb = sb('bmb', [128, NB, 64], BF16)
        bmT = sb('bmT', [64, NB])
        Esel = sb('Esel', [4, 4, 128])
        mngb = sb('mngb', [128, 512])
        sinkb = sb('sinkb', [128, 8])
        bi_l = sb('bi_l', [128, 1])
        nbf_l = sb('nbf_l', [128, 1])
        tmpn = sb('tmpn', [128, D])
        stt = sb('stt', [128, 8])
        xn = sb('xn', [128, D], BF16)
        sg = [sb('sg%d' % i, [128, 512]) for i in range(1)]
        IGs = sb('IGs', [128, TT_])
        FGs = sb('FGs', [128, TT_])
        e1 = FGs
        lfn = FGs
        Bneg = sb('Bneg', [128, TT_])
        Ug = IGs
        MU = sb('MU', [128, TP + 1])
        MUs = sb('MUs', [128, NB, 5])
        tg = sb('tg', [128, TT_])
        Fst = tg
        Bprev = sb('Bprev', [128, 1])
        MUprev = sb('MUprev', [128, 1])
        dd = sb('dd', [4, 4])
        dds = sb('dds', [4, NB])
        DECs = sb('DECs', [128, 4, 4])
        DECss = sb('DECss', [128, 4, NB])
        FT = sb('FT', [128, 5, 16])
        smin = sb('smin', [128, NB])
        smout = sb('smout', [128, NB])
        Cst = sb('Cst', [128, 4, 129])
        Cbf = sb('Cbf', [128, 4, 129], BF16)
        Sp = sb('Sp', [128, 4, 128], BF16)
        kw = sb('kw', [128, 4, 128], BF16)
        kwm = sb('kwm', [64, 4, 128], BF16)
        tmpo = sb('tmpo', [128, 4, 129])
        ND = sb('ND', [128, 4, 129])
        q5 = sb('q5', [128, 8, 4])
        og = sb('og', [128, 512], BF16)
        t5 = tmpo
        U = sb('U', [128, 5, D], BF16)
        Ssb = sb('Ssb', [128, 4, 256])
        Pb = sb('Pb', [128, 4, 256], BF16)
        PTs = sb('PTs', [128, 8, 128], BF16)
        sst = sb('sst', [128, 8, 8])
        kvf = sb('kvf', [128, 2, 256])
        qTm = sb('qTm', [128, 2, 4, 64], BF16)
        Cb = [sb('Cb%d' % i, [128, 4, 129]) for i in range(2)]
        Cbb = [sb('Cbb%d' % i, [128, 4, 129], BF16) for i in range(2)]
        snin = sb('snin', [64, 128])
        nTin = sb('nTin', [128, 64])
        nTout = sb('nTout', [128, 64])
        snout = sb('snout', [64, 128])
        ckb = [sb('ckb%d' % i, [128, 4, 128], BF16) for i in range(2)]
        kTc = [sb('kTc%d' % i, [128, 4, 128], BF16) for i in range(2)]
        cvb = [sb('cvb%d' % i, [128, 128], BF16) for i in range(2)]
        PTss = sb('PTss', [128, 8, 2, 4], BF16)
        uab = [sb('uab%d' % i, [4, 512], BF16) for i in range(2)]
        pn_sb = sb('pn_sb', [4, 128])
        pm_sb = sb('pm_sb', [128, 1])
        kaTh = sb('kaTh', [128, 4, 128], BF16)
        vah = sb('vah', [128, 128], BF16)
        role = sb('role', [128, 17])
        cmb = sb('cmb', [128, 12])
        ABt = sb('ABt', [128, 4, 2])
        hT = big[:, 0:NJ * TT_].rearrange('p (j t) -> p j t', t=TT_)
        o_ = [0]

        def carve(n):
            a = big[:, o_[0]:o_[0] + n]
            o_[0] += n
            return a
        qT = carve(4 * TT_).rearrange('p (h t) -> p h t', t=TT_)
        kT = carve(4 * TT_).rearrange('p (h t) -> p h t', t=TT_)
        osig = carve(5 * 512).rearrange('p (g c) -> p g c', c=512)
        qaT = carve(4 * TT_).rearrange('p (h t) -> p h t', t=TT_)
        KW = 128 + TT_
        kaT = carve(4 * KW).rearrange('p (h t) -> p h t', t=KW)
        va = carve(6 * 128).rearrange('p (g c) -> p g c', c=128)
        assert o_[0] >= NJ * TT_
        ZT = o_[0]
        ktok = carve(5 * 512).rearrange('p (g c) -> p g c', c=512)
        vaug = carve(5 * 4 * 130).rearrange('p (g h c) -> p g h c', h=4, c=130)
        assert o_[0] <= 18304
        ps = st.enter_context(nc.psum_tensor('ps', [128, 8, 512], F32))

        def bank(i, n=1):
            return ps[:, i:i + n, :].rearrange('p a b -> p (a b)')

        def bkeys(i, n=1):
            return [('bank', i + k) for k in range(n)]

        def MM(out, lhsT, rhs, start, stop, R, W, **kw_):
            S.op('pe', lambda e: e.matmul(out, lhsT=lhsT, rhs=rhs, start=start, stop=stop, **kw_), R, W)

        def TR(out, in_, idn, R, W):
            S.op('pe', lambda e: e.transpose(out=out, in_=in_, identity=idn), R, W)

        def ACT(out, in_, func, R, W, **kw_):
            S.op('act', lambda e: e.activation(out=out, in_=in_, func=func, **kw_), R, W)

        def TTo(out, in0, in1, op, R, W, eng='dve'):
            S.op(eng, lambda e: e.tensor_tensor(out=out, in0=in0, in1=in1, op=op), R, W)

        def STT(out, in0, scalar, in1, op0, op1, R, W, eng='dve'):
            S.op(eng, lambda e: e.scalar_tensor_tensor(out=out, in0=in0, scalar=scalar, in1=in1, op0=op0, op1=op1), R, W)

        def TS(out, in0, s1, s2, op0, op1, R, W, eng='dve'):
            if s2 is None:
                S.op(eng, lambda e: e.tensor_scalar(out=out, in0=in0, scalar1=s1, scalar2=None, op0=op0), R, W)
            else:
                S.op(eng, lambda e: e.tensor_scalar(out=out, in0=in0, scalar1=s1, scalar2=s2, op0=op0, op1=op1), R, W)

        def CP(out, in_, R, W, eng='dve'):
            if eng == 'act':
                S.op('act', lambda e: e.copy(out=out, in_=in_), R, W)
            else:
                S.op(eng, lambda e: e.tensor_copy(out=out, in_=in_), R, W)

        def RCP(out, in_, R, W):
            S.op('dve', lambda e: e.reciprocal(out=out, in_=in_), R, W)

        def RED(out, in_, op, R, W):
            S.op('dve', lambda e: e.tensor_reduce(out=out, in_=in_, axis=AX.X, op=op), R, W)

        def SCAN(out, d0, init, op0, R, W):
            S.op('dve', lambda e: e.tensor_tensor_scan(out=out, data0=d0, data1=d0, initial=init, op0=op0, op1=ALU.bypass), R, W)

        def MSET(ap, val, W, eng='dve'):
            S.op(eng, lambda e: e.memset(ap, val), (), W)
        dctr = [0]
        nodma = [False]

        def DMA(out, in_, R, W, key=None, eng='sp', slow=False):
            if nodma[0] and eng == 'pool' and (key is not None) and key.startswith('w_'):
                return key
            if key is None:
                dctr[0] += 1
                key = 'dk%d' % (dctr[0] % 24)
            if slow:
                S.op(eng, lambda e: e.dma_start(out=out, in_=in_, allow_slow_non_contiguous=True), R, W, dma_key=key)
            else:
                S.op(eng, lambda e: e.dma_start(out=out, in_=in_), R, W, dma_key=key)
            return key

        DMA(X[:, 0:4, :], xp[0:TP, :].rearrange('(g p) d -> p g d', p=128), (), [('X', g) for g in range(4)], key='x_in')

        def LD(t, src, name):
            DMA(t, src, (), [name], key='c_' + name)
        LD(ident[:], c_ident[:, :], 'ident')
        LD(maskp[:], c_maskp[:, :], 'maskp')
        LD(masks[:], c_masks[:, :], 'masks')
        LD(biasT[:].rearrange('p h k -> p (h k)'), c_bias[:, :], 'biasT')
        LD(slopeb[:], c_slope[:, :], 'slopeb')
        LD(bsc[:].rearrange('p h k -> p (h k)'), c_bsc[:, :], 'bsc')
        LD(tbl[:].rearrange('p h k -> p (h k)'), c_tb[:, :], 'tbl')
        DMA(bmb[:].rearrange('p b t -> p (b t)'), c_bm[:, :], (), ['bmb'], key='c_bmb', eng='pool')
        LD(bmT[:], c_bmT[:, :], 'bmT')
        LD(Esel[:].rearrange('p h k -> p (h k)'), c_E[:, :], 'Esel')
        LD(role[:], c_role[:, :], 'role')
        CP(identb[:], ident[:], ['ident'], ['identb'])
        DMA(tmpn[0:6, :], gains[:, :], (), ['tmpn'], key='c_gT')
        pg_ = bank(1)
        for c in range(8):
            TR(pg_[:, 6 * c:6 * c + 6], tmpn[0:6, c * 128:(c + 1) * 128], ident[0:6, 0:6], ['tmpn', 'ident'], bkeys(1))
        CP(gT[:], V(pg_[:, 0:1], [[1, 6], [6, 8]]), bkeys(1), ['gT'])

        def load_gp(gi):
            DMA(gp[:, 0, :], bass.AP(gains.tensor, gains[gi, :].offset, [[0, 128], [1, D]]), (), ['gp'], key='c_gp')
        DMA(mngb[:], bass.AP(mng.tensor, mng.offset, [[0, 128], [1, 512]]), (), ['mngb'], key='c_mngb')
        DMA(sinkb[:], bass.AP(sinks.tensor, sinks.offset, [[0, 128], [1, 8]]), (), ['sinkb'], key='c_sinkb')
        MSET(bi_l[:], 0.0, ['bi_l'])
        MSET(nbf_l[:], 0.0, ['nbf_l'])
        MSET(smin[:], 0.0, ['smin'])
        for f in range(4):
            DMA(bi_l[32 * f:32 * f + 4, :], bgate[0:4].rearrange('(p o) -> p o', o=1), ['bi_l'], [('bi_l', f)], key='c_bil', slow=True)
            DMA(nbf_l[32 * f:32 * f + 4, :], bgate[4:8].rearrange('(p o) -> p o', o=1), ['nbf_l'], [('nbf_l', f)], key='c_bil', slow=True)
            DMA(smin[32 * f:32 * f + 4, :], sm.rearrange('b h -> h b'), ['smin'], [('smin', f)], key='c_smin', slow=True)
        lane_keys = [(nm_, f) for nm_ in ('bi_l', 'nbf_l', 'smin') for f in range(4)]
        neg_done = [False]
        MSET(big[:], 0.0, ['kaT_h', 'va0', 'vones'], eng='pool')
        MSET(IGs[:], 0.0, ['IGs'], eng='pool')
        MSET(FGs[:], 0.0, ['FGs'], eng='pool')
        MSET(X[:, 4, :], 0.0, [('X', 4)], eng='pool')
        MSET(Cst[:], 0.0, ['Cst'])
        MSET(Cbf[:], 0.0, ['Cbf'], eng='pool')
        MSET(Bprev[:], 0.0, ['Bprev'])
        MSET(MUprev[:], NEG, ['MUprev'])
        MSET(kaT[:, :, 0:128], 0.0, ['kaT_h'], eng='pool')
        MSET(va[:, 0, :], 0.0, ['va0'], eng='pool')
        MSET(vaug[:, :, :, 128:129], 1.0, ['vones'], eng='pool')
        for i in range(2):
            MSET(ckb[i][:], 0.0, [('ckb', i)], eng='pool')
        DMA(snin[:], sn[:, :], (), ['snin'], key='c_snin')
        TR(bank(1)[:, 0:64], snin[:, :], ident[0:64, 0:64], ['snin', 'ident'], bkeys(1))
        CP(nTin[:], bank(1)[:, 0:64], bkeys(1), ['nTin'])
        wq = []
        wslot = [0]

        extra_keys = {}

        def wload(src_fn):
            s = wslot[0] % 4
            wslot[0] += 1
            rk = 'wblk%d' % s
            extra_keys.pop(rk, None)
            src_fn(wblk[s], rk, 'w_slot%d' % s)
            return (wblk[s], rk)

        def colblk(Wap, c0, ncols):

            def f(slot, rk, key):
                DMA(slot[:, :, 0:ncols], Wap[:, c0:c0 + ncols].rearrange('(kc p) c -> p kc c', p=128), (), [rk], key=key, eng='pool')
            return f

        def colparts(Wap, parts):

            def f(slot, rk, key):
                extra_keys[rk] = [(rk, pi) for pi in range(1, len(parts))]
                for pi, (d0, c0, ncols) in enumerate(parts):
                    DMA(slot[:, :, d0:d0 + ncols], Wap[:, c0:c0 + ncols].rearrange('(kc p) c -> p kc c', p=128),
                        () if pi == 0 else [rk], [rk] if pi == 0 else [(rk, pi)], key=key if pi == 0 else key + '_p%d' % pi, eng='pool')
            return f

        def load_wbig(f):
            for j2 in range(11):
                DMA(wbig[:, 2 * j2:2 * j2 + 2, :], wd[f, 256 * j2:256 * j2 + 256, :].rearrange('(j p) c -> p j c', p=128), (), [('wbig', j2)], key='w_big%d' % j2, eng='pool')
        rot = {}

        def R2(name, n=2):
            rot[name] = (rot.get(name, -1) + 1) % n
            return rot[name]

        def groups_of(has_s):
            gs = [(g, 128, g * 128) for g in range(4)]
            if has_s:
                gs.append((4, 64, TP))
            return gs

        def prenorm(gi, has_s):
            for g, npp, c0 in groups_of(has_s):
                Xg = ('X', g)
                MSET(stt[:npp, 0:1], 0.0, ['stt'])
                ACT(xn[:npp, :], X[:npp, g, :], AF.Square, [Xg], ['xn', 'stt'], accum_out=stt[:npp, 0:1])
                ACT(stt[:npp, 1:2], stt[:npp, 0:1], AF.Sqrt, ['stt'], ['stt'], scale=1.0 / D, bias=EPS)
                RCP(stt[:npp, 2:3], stt[:npp, 1:2], ['stt'], ['stt'])
                TS(xn[:npp, :], X[:npp, g, :], stt[:npp, 2:3], None, ALU.mult, None, [Xg, 'stt'], ['xn'])
                b = 6 + R2('pT')
                pT = bank(b).bitcast(BF16).rearrange('p (c t) -> p c t', t=128)
                for c in range(8):
                    TR(pT[:, c, 0:npp], xn[:npp, c * 128:(c + 1) * 128], identb[:npp, :npp], ['xn', 'identb'], bkeys(b))
                TTo(hnT[:, :, c0:c0 + npp], pT[:, :, 0:npp], V(gT[:, gi, :], [[1, 8], [0, npp]]), ALU.mult, bkeys(b) + ['gT'], [('hnT', g)])

        def postnorm(py, pkeys, fac, g, npp):
            Xg = ('X', g)
            MSET(stt[:npp, 4:5], 0.0, ['stt'])
            ACT(tmpn[:npp, :], py[:npp, :], AF.Square, pkeys, ['tmpn', 'stt'], accum_out=stt[:npp, 4:5])
            ACT(stt[:npp, 5:6], stt[:npp, 4:5], AF.Sqrt, ['stt'], ['stt'], scale=1.0 / D, bias=EPS)
            RCP(stt[:npp, 6:7], stt[:npp, 5:6], ['stt'], ['stt'])
            STT(tmpn[:npp, :], py[:npp, :], stt[:npp, 6:7], gp[:npp, 0, :], ALU.mult, ALU.mult, pkeys + ['stt', 'gp'], ['tmpn'])
            STT(X[:npp, g, :], tmpn[:npp, :], fac, X[:npp, g, :], ALU.mult, ALU.add, [Xg, 'tmpn'], [Xg])

        def nsplits(has_s):
            return [(0, TP)] + ([(TP, NS)] if has_s else [])

        first_wbig = [False]

        def ffn(f, gi_post, has_s):
            hkeys = [('hnT', g) for g, _, _ in groups_of(has_s)]
            for blk in range(11):
                wgs, wgk = wload(colblk(wg[f], 256 * blk, 256))
                wus, wuk = wload(colblk(wu[f], 256 * blk, 256))
                if blk == 1 and first_wbig[0]:
                    first_wbig[0] = False
                    load_wbig(0)
                for jj in range(2):
                    j = 2 * blk + jj
                    for n0, nn in nsplits(has_s):
                        bg = R2('pg')
                        bu = 2 + R2('pu')
                        pg = bank(bg)
                        pu = bank(bu)
                        for kc in range(8):
                            MM(pg[:, 0:nn], wgs[:, kc, jj * 128:(jj + 1) * 128], hnT[:, kc, n0:n0 + nn], kc == 0, kc == 7, [wgk] + hkeys, bkeys(bg))
                        for kc in range(8):
                            MM(pu[:, 0:nn], wus[:, kc, jj * 128:(jj + 1) * 128], hnT[:, kc, n0:n0 + nn], kc == 0, kc == 7, [wuk] + hkeys, bkeys(bu))
                        si = 0
                        ACT(sg[si][:, 0:nn], pg[:, 0:nn], AF.Silu, bkeys(bg), ['sg%d' % si])
                        TTo(hT[:, j, n0:n0 + nn], sg[si][:, 0:nn], pu[:, 0:nn], ALU.mult, ['sg%d' % si] + bkeys(bu), [('hT', j, n0)])
            load_gp(gi_post)
            hall = [('hT', j, n0) for j in range(NJ) for n0, _ in nsplits(has_s)]
            for g, npp, c0 in groups_of(has_s):
                pyb = (4, 0)[R2('py')]
                py = bank(pyb, 2)
                for hf in range(2):
                    for j in range(NJ):
                        MM(py[:npp, hf * 512:(hf + 1) * 512], hT[:, j, c0:c0 + npp], wbig[:, j, hf * 512:(hf + 1) * 512], j == 0, j == NJ - 1, hall + [('wbig', j // 2)], bkeys(pyb, 2))
                postnorm(py, bkeys(pyb, 2), 0.5, g, npp)
        zk = lambda nm, g: ('z', nm, g)

        def w_in(has_s):
            grp = groups_of(has_s)
            hkeys = [('hnT', g) for g, _, _ in grp]

            def fm_chunk(ws, wk_, lhs_fn, evac):
                for n0, nn in nsplits(has_s):
                    b = R2('pg')
                    p = bank(b)
                    for kc in range(8):
                        MM(p[:, 0:nn], lhs_fn(ws, kc), hnT[:, kc, n0:n0 + nn], kc == 0, kc == 7, [wk_] + extra_keys.get(wk_, []) + hkeys, bkeys(b))
                    evac(p, b, n0, nn)

            def tm_block(ws, wk_, ncols, evac):
                for g, npp, c0 in grp:
                    b = 2 + R2('pu')
                    p = bank(b)
                    for kc in range(8):
                        MM(p[:npp, 0:ncols], hnT[:, kc, c0:c0 + npp], ws[:, kc, 0:ncols], kc == 0, kc == 7, [wk_, ('hnT', g)], bkeys(b))
                    evac(p, b, g, npp)
            sc_k = 128.0 ** (-0.5)
            for blk in range(2):
                ws, wk_ = wload(colblk(win, 256 * blk, 256))
                for jj in range(2):
                    h = 2 * blk + jj
                    fm_chunk(ws, wk_, lambda w_, kc, jj=jj: w_[:, kc, jj * 128:(jj + 1) * 128], lambda p, b, n0, nn, h=h: CP(qT[:, h, n0:n0 + nn], p[:, 0:nn], bkeys(b), [('qT', n0)], eng='act'))
            if os.environ.get('WSTOP') == '1':
                return
            for blk in range(2):
                ws, wk_ = wload(colblk(win, 512 + 256 * blk, 256))
                for jj in range(2):
                    h = 2 * blk + jj
                    fm_chunk(ws, wk_, lambda w_, kc, jj=jj: w_[:, kc, jj * 128:(jj + 1) * 128], lambda p, b, n0, nn, h=h: S.op('act', lambda e: e.mul(out=kT[:, h, n0:n0 + nn], in_=p[:, 0:nn], mul=sc_k), bkeys(b), [('kT', n0)]))
                tm_block(ws, wk_, 256, lambda p, b, g, npp, blk=blk: TS(ktok[:npp, g, 256 * blk:256 * blk + 256], p[:npp, 0:256], sc_k, None, ALU.mult, None, bkeys(b), [zk('ktok', g)]))
            if os.environ.get('WSTOP') == '2':
                return
            for blk in range(2):
                ws, wk_ = wload(colblk(win, 1024 + 256 * blk, 256))

                def ev_v(p, b, g, npp, blk=blk):
                    CP(vaug[:npp, g, 2 * blk:2 * blk + 2, 0:128], p[:npp, 0:256].rearrange('p (h c) -> p h c', c=128), bkeys(b), [zk('vaug', g)])
                    if blk == 1:
                        MSET(vaug[:npp, g, :, 128:129], 1.0, [zk('vaug', g)])
                tm_block(ws, wk_, 256, ev_v)
            if os.environ.get('WSTOP') == '3':
                return
            for blk in range(2):
                ws, wk_ = wload(colblk(win, 1536 + 256 * blk, 256))
                tm_block(ws, wk_, 256, lambda p, b, g, npp, blk=blk: ACT(osig[:npp, g, 256 * blk:256 * blk + 256], p[:npp, 0:256], AF.Sigmoid, bkeys(b), [zk('osig', g)]))
            if os.environ.get('WSTOP') == '4':
                return
            ws, wk_ = wload(colparts(win, [(32 * f_, 2048, 32) for f_ in range(4)] + [(128 + 32 * f_, 2052, 32) for f_ in range(4)]))
            fm_chunk(ws, wk_, lambda w_, kc: w_[:, kc, 0:128], lambda p, b, n0, nn: CP(IGs[:, n0:n0 + nn], p[:, 0:nn], bkeys(b), [('IGs', n0)], eng='act'))
            fm_chunk(ws, wk_, lambda w_, kc: w_[:, kc, 128:256], lambda p, b, n0, nn: CP(FGs[:, n0:n0 + nn], p[:, 0:nn], bkeys(b), [('FGs', n0)], eng='act'))
            if os.environ.get('WSTOP') == '5':
                return
            for blk in range(2):
                ws, wk_ = wload(colblk(win, 2056 + 256 * blk, 256))
                for jj in range(2):
                    c = 2 * blk + jj
                    fm_chunk(ws, wk_, lambda w_, kc, jj=jj: w_[:, kc, jj * 128:(jj + 1) * 128], lambda p, b, n0, nn, c=c: S.op('act', lambda e: e.mul(out=qaT[:, c, n0:n0 + nn], in_=p[:, 0:nn], mul=0.125), bkeys(b), [('qaT', n0)]))
            if os.environ.get('WSTOP') == '6':
                return
            for hk in range(2):

                def kaf(slot, rk, key, hk=hk):
                    MSET(slot[:, :, 64:192], 0.0, [rk], eng='pool')
                    for d0 in (0, 192):
                        DMA(slot[:, :, d0:d0 + 64], win[:, 2568 + 64 * hk:2568 + 64 * hk + 64].rearrange('(kc p) c -> p kc c', p=128), (), [rk], key=key, eng='pool')
                ws, wk_ = wload(kaf)
                for par in range(2):
                    fm_chunk(ws, wk_, lambda w_, kc, par=par: w_[:, kc, par * 128:(par + 1) * 128], lambda p, b, n0, nn, v=2 * hk + par: CP(kaT[:, v, 128 + n0:128 + n0 + nn], p[:, 0:nn], bkeys(b), [('kaT', n0)], eng='act'))
            if os.environ.get('WSTOP') == '7':
                return
            ws, wk_ = wload(colblk(win, 2568, 256))

            def ev_kv(p, b, g, npp):
                if has_s and g >= 3:
                    CP(kvf[:npp, g - 3, :], p[:npp, 0:256], bkeys(b), [('kvf', g)])
                CP(va[:npp, 1 + g, :], p[:npp, 128:256], bkeys(b), [('va', g)], eng='act')
            tm_block(ws, wk_, 256, ev_kv)

        def gates(has_s, phase=2):
            rI = [('IGs', n0) for n0, _ in nsplits(has_s)]
            rF = [('FGs', n0) for n0, _ in nsplits(has_s)]
            TTn = TP + (NS if has_s else 0)
            if not neg_done[0]:
                neg_done[0] = True
                S.op('act', lambda e: e.mul(out=nbf_l[:], in_=nbf_l[:], mul=-1.0), ['nbf_l', 'bi_l', 'smin'] + lane_keys, ['nbf_l', 'bi_l', 'smin'] + lane_keys)
            ACT(e1[:, 0:TTn], FGs[:, 0:TTn], AF.Exp, rF + ['nbf_l'], ['e1'], scale=-1.0, bias=nbf_l[:, 0:1])
            ACT(lfn[:, 0:TTn], e1[:, 0:TTn], AF.Ln, ['e1'], ['lfn'], bias=1.0)
            if os.environ.get('GSTOP') == '1':
                return
            SCAN(Bneg[:, 0:TP], lfn[:, 0:TP], Bprev[:, 0:1], ALU.add, ['lfn', 'Bprev'], ['Bneg'])
            STT(Ug[:, 0:TP], IGs[:, 0:TP], bi_l[:, 0:1], Bneg[:, 0:TP], ALU.add, ALU.add, rI + ['bi_l', 'Bneg'], ['Ug'])
            CP(MU[:, 0:1], MUprev[:, 0:1], ['MUprev'], ['MU'])
            SCAN(MU[:, 1:TP + 1], Ug[:, 0:TP], MUprev[:, 0:1], ALU.max, ['Ug', 'MUprev', 'MU'], ['MU'])
            CP(Bprev[:, 0:1], Bneg[:, TP - 1:TP], ['Bneg'], ['Bprev'])
            CP(MUprev[:, 0:1], MU[:, TP:TP + 1], ['MU'], ['MUprev'])
            if os.environ.get('GSTOP') == '2':
                return
            MUn = V(MU[:, 128:129], [[128, 4], [0, 128]])
            MUp = V(MU[:, 0:1], [[128, 4], [0, 128]])
            MUc = MU[:, 1:TP + 1].rearrange('p (c t) -> p c t', t=128)
            v3 = lambda a, lo: a[lo:lo + 32, 0:TP].rearrange('p (c t) -> p c t', t=128)
            sl = lambda a, lo: bass.AP(a.tensor, a.offset + lo * a.ap[0][0], [[a.ap[0][0], 32]] + [list(x) for x in a.ap[1:]])
            TTo(v3(tg, 0), v3(Ug, 0), sl(MUn, 0), ALU.subtract, ['Ug', 'MU'], ['tg'])
            TTo(v3(tg, 32), sl(MUn, 32), sl(MUc, 32), ALU.subtract, ['MU'], ['tg'])
            TTo(v3(tg, 64), sl(MUp, 64), sl(MUc, 64), ALU.subtract, ['MU'], ['tg'])
            TTo(v3(tg, 96), v3(Bneg, 96), sl(MUc, 96), ALU.subtract, ['MU', 'Bneg'], ['tg'])
            TTo(dd[0:4, 0:4], V(MU[0:4, 0:1], [[128, 4]]), V(MU[0:4, 128:129], [[128, 4]]), ALU.subtract, ['MU'], ['dd'])
            ACT(dd[0:4, 0:4], dd[0:4, 0:4], AF.Exp, ['dd'], ['dd'])
            if has_s:
                c0 = TP
                l3 = lfn[:, c0:c0 + NS].rearrange('p (b t) -> p b t', t=4)
                B3 = Bneg[:, c0:c0 + NS].rearrange('p (b t) -> p b t', t=4)
                U3 = Ug[:, c0:c0 + NS].rearrange('p (b t) -> p b t', t=4)
                I3 = IGs[:, c0:c0 + NS].rearrange('p (b t) -> p b t', t=4)
                CP(B3[:, :, 0:1], l3[:, :, 0:1], ['lfn'], ['Bneg'])
                for t in range(1, 4):
                    TTo(B3[:, :, t:t + 1], B3[:, :, t - 1:t], l3[:, :, t:t + 1], ALU.add, ['lfn', 'Bneg'], ['Bneg'])
                STT(U3, I3, bi_l[:, 0:1], B3, ALU.add, ALU.add, rI + ['bi_l', 'Bneg'], ['Ug'])
                CP(MUs[:, :, 0:1], smin[:].rearrange('p (b o) -> p b o', o=1), ['smin'], ['MUs'])
                for t in range(4):
                    TTo(MUs[:, :, t + 1:t + 2], MUs[:, :, t:t + 1], U3[:, :, t:t + 1], ALU.max, ['Ug', 'MUs'], ['MUs'])
                MUn_s = V(MUs[:, 0, 4:5], [[5, NB], [0, 4]])
                MUp_s = V(MUs[:, 0, 0:1], [[5, NB], [0, 4]])
                MUc_s = MUs[:, :, 1:5]
                t3 = lambda lo: tg[lo:lo + 32, c0:c0 + NS].rearrange('p (b t) -> p b t', t=4)
                TTo(t3(0), U3[0:32], sl(MUn_s, 0), ALU.subtract, ['Ug', 'MUs'], ['tg'])
                TTo(t3(32), sl(MUn_s, 32), MUc_s[32:64], ALU.subtract, ['MUs'], ['tg'])
                TTo(t3(64), sl(MUp_s, 64), MUc_s[64:96], ALU.subtract, ['MUs'], ['tg'])
                TTo(t3(96), B3[96:128], MUc_s[96:128], ALU.subtract, ['MUs', 'Bneg'], ['tg'])
                TTo(dds[0:4, :], V(MUs[0:4, 0, 0:1], [[5, NB]]), V(MUs[0:4, 0, 4:5], [[5, NB]]), ALU.subtract, ['MUs'], ['dds'])
                ACT(dds[0:4, :], dds[0:4, :], AF.Exp, ['dds'], ['dds'])
                TTo(smout[:, :], V(MUs[:, 0, 4:5], [[5, NB]]), V(B3[:, 0, 3:4], [[4, NB]]), ALU.subtract, ['MUs', 'Bneg'], ['smout'])
                DMA(smo.rearrange('b h -> h b'), smout[0:4, :], ['smout'], ['o_smo'], key='o_sm', slow=True)
            if os.environ.get('GSTOP') == '3':
                return
            TS(tg[:, 0:TTn], tg[:, 0:TTn], 80.0, None, ALU.min, None, ['tg'], ['tg'])
            ACT(Fst[:, 0:TTn], tg[:, 0:TTn], AF.Exp, ['tg'], ['Fst'])
            pd = bank(1)
            for h in range(4):
                MM(pd[:, 4 * h:4 * h + 4], Esel[0:4, h, :], dd[0:4, 0:4], True, True, ['Esel', 'dd'], bkeys(1))
            CP(DECs[:].rearrange('p h c -> p (h c)'), pd[:, 0:16], bkeys(1), ['DECs'])
            if has_s:
                for h in range(4):
                    MM(pd[:, 64 + NB * h:64 + NB * h + NB], Esel[0:4, h, :], dds[0:4, :], True, True, ['Esel', 'dds'], bkeys(1))
                CP(DECss[:].rearrange('p h c -> p (h c)'), pd[:, 64:64 + 4 * NB], bkeys(1), ['DECss'])
            if os.environ.get('GSTOP') == '4':
                return
            pf = bank(7)
            for g in range(4):
                TR(pf[:, g * 128:(g + 1) * 128], Fst[:, g * 128:(g + 1) * 128], ident[:, :], ['Fst', 'ident'], bkeys(7))
            CP(FT[:, 0:4, :].rearrange('p g (f h) -> p g f h', h=4), V(pf[:, 0:1], [[128, 4], [32, 4], [1, 4]]), bkeys(7), ['FT'])
            if has_s:
                pf2 = bank(6)
                TR(pf2[0:64, 0:128], Fst[:, TP:TP + NS], ident[:, :], ['Fst', 'ident'], bkeys(6))
                CP(FT[0:64, 4, :].rearrange('p (f h) -> p f h', h=4), V(pf2[0:64, 0:1], [[32, 4], [1, 4]]), bkeys(6), ['FT'])

        def mlstm_group(g, npp, c0, c_idx, is_s):
            zq = [('qT', 0), ('qT', TP), ('kT', 0), ('kT', TP)]
            pS = bank(0).rearrange('p (h t) -> p h t', t=128)
            for h in range(4):
                MM(pS[:npp, h, 0:npp], kT[:, h, c0:c0 + npp], qT[:, h, c0:c0 + npp], True, True, zq, bkeys(0))
                yield
            mk = masks if is_s else maskp
            for h in range(4):
                STT(Sp[:npp, h, 0:npp], pS[:npp, h, 0:npp], FT[:npp, g, h:h + 1], mk[:npp, :npp], ALU.mult, ALU.mult, bkeys(0) + ['FT', 'masks', 'maskp'], ['Sp'])
                yield
            TTo(kw[:npp, :, :], ktok[:npp, g, :].rearrange('p (h c) -> p h c', c=128), V(FT[:npp, g, 0:1], [[1, 4], [0, 128]]), ALU.mult, [zk('ktok', g), 'FT'], ['kw'])
            yield
            pKV = bank(1, 2).rearrange('p (h c) -> p h c', c=256)
            pO1 = bank(3, 2).rearrange('p (h c) -> p h c', c=256)
            pO2 = bank(3, 2).rearrange('p (h c) -> p h c', c=256)
            vk = [zk('vaug', g), 'vones']
            if not is_s:
                for h in range(4):
                    MM(pKV[:, h, 0:129], kw[:, h, :], vaug[:, g, h, 0:129], True, True, ['kw'] + vk, bkeys(1, 2))
                    yield
                for h in range(4):
                    MM(pO1[:, h, 0:129], qT[:, h, c0:c0 + 128], Cbf[:, h, :], True, True, zq + ['Cbf'], bkeys(3, 2))
                    yield
            else:
                MSET(tmpo[:64], 0.0, ['tmpo'])
                yield
                for b in range(NB):
                    i = b % 2
                    TTo(qTm[:, i, :, :], qT[:, :, c0:c0 + 64], V(bmb[:, b, 0:1], [[0, 4], [1, 64]]), ALU.mult, zq + ['bmb'], [('qTm', i)])
                    yield
                    DMA(Cb[i][:, :, 0:128], sC[b].rearrange('h k v -> k h v'), (), [('Cb', i)], key='cb%d' % i)
                    yield
                    CP(Cb[i][:, :, 128:129], nTin[:, 4 * b:4 * b + 4].rearrange('p (h o) -> p h o', o=1), ['nTin'], [('Cb', i)], eng='act')
                    yield
                    CP(Cbb[i][:], Cb[i][:], [('Cb', i)], [('Cbb', i)], eng='act')
                    yield
                    for h in range(4):
                        MM(pO1[0:64, h, 0:129], qTm[:, i, h, :], Cbb[i][:, h, :], True, True, [('qTm', i), ('Cbb', i)], bkeys(3, 2))
                        yield
                    TTo(tmpo[:64], pO1[0:64, :, 0:129], tmpo[:64], ALU.add, bkeys(3, 2) + ['tmpo'], ['tmpo'])
                    yield
                    TS(kwm[:, :, :].rearrange('p h c -> p (h c)'), kw[0:64, :, :].rearrange('p h c -> p (h c)'), bmT[:, b:b + 1], None, ALU.mult, None, ['kw', 'bmT'], ['kwm'])
                    yield
                    for h in range(4):
                        MM(pKV[:, h, 0:129], kwm[:, h, :], vaug[0:64, g, h, 0:129], True, True, ['kwm'] + vk, bkeys(1, 2))
                        yield
                    TTo(Cb[i][:], Cb[i][:], V(DECss[:, 0, b:b + 1], [[NB, 4], [0, 129]]), ALU.mult, [('Cb', i), 'DECss'], [('Cb', i)])
                    yield
                    TTo(Cb[i][:], Cb[i][:], pKV[:, :, 0:129], ALU.add, [('Cb', i)] + bkeys(1, 2), [('Cb', i)])
                    yield
                    DMA(sCo[b].rearrange('h k v -> k h v'), Cb[i][:, :, 0:128], [('Cb', i)], ['o_sC%d' % i], key='o_sC%d' % i)
                    yield
                    CP(nTout[:, 4 * b:4 * b + 4].rearrange('p (h o) -> p h o', o=1), Cb[i][:, :, 128:129], [('Cb', i)], ['nTout'], eng='act')
                    yield
            if is_s:
                TTo(tmpo[:npp], tmpo[:npp], V(FT[:npp, g, 8:9], [[1, 4], [0, 129]]), ALU.mult, ['tmpo', 'FT'], ['tmpo'])
                yield
            else:
                TTo(tmpo[:npp], pO1[:npp, :, 0:129], V(FT[:npp, g, 8:9], [[1, 4], [0, 129]]), ALU.mult, bkeys(3, 2) + ['FT'], ['tmpo'])
                yield
            for h in range(4):
                MM(pO2[:npp, h, 0:129], Sp[:npp, h, 0:npp], vaug[:npp, g, h, 0:129], True, True, ['Sp'] + vk, bkeys(3, 2))
                yield
            for h in range(4):
                STT(ND[:npp, h, :], pO2[:npp, h, 0:129], FT[:npp, g, 4 + h:5 + h], tmpo[:npp, h, :], ALU.mult, ALU.add, bkeys(3, 2) + ['FT', 'tmpo'], ['ND'])
                yield
            TS(q5[:npp, 0, :], ND[:npp, :, 128], -1.0, None, ALU.mult, None, ['ND'], ['q5'])
            yield
            TTo(q5[:npp, 0, :], q5[:npp, 0, :], ND[:npp, :, 128], ALU.max, ['ND', 'q5'], ['q5'])
            yield
            TTo(q5[:npp, 0, :], q5[:npp, 0, :], FT[:npp, g, 12:16], ALU.max, ['q5', 'FT'], ['q5'])
            yield
            RCP(q5[:npp, 1, :], q5[:npp, 0, :], ['q5'], ['q5'])
            yield
            MSET(q5[:npp, 2, :], 0.0, ['q5'])
            yield
            for h in range(4):
                ACT(kw[:npp, h, :], ND[:npp, h, 0:128], AF.Square, ['ND'], ['kw', 'q5'], accum_out=q5[:npp, 2, h:h + 1])
                yield
            TTo(q5[:npp, 3, :], q5[:npp, 1, :], q5[:npp, 1, :], ALU.mult, ['q5'], ['q5'])
            yield
            TTo(q5[:npp, 4, :], q5[:npp, 3, :], q5[:npp, 2, :], ALU.mult, ['q5'], ['q5'])
            yield
            ACT(q5[:npp, 5, :], q5[:npp, 4, :], AF.Sqrt, ['q5'], ['q5'], scale=1.0 / 128, bias=EPS)
            yield
            RCP(q5[:npp, 6, :], q5[:npp, 5, :], ['q5'], ['q5'])
            yield
            TTo(q5[:npp, 7, :], q5[:npp, 6, :], q5[:npp, 1, :], ALU.mult, ['q5'], ['q5'])
            yield
            TTo(og[:npp, :], osig[:npp, g, :], mngb[:npp, :], ALU.mult, [zk('osig', g), 'mngb'], ['og'])
            yield
            TTo(t5[:npp, :, 0:128], ND[:npp, :, 0:128], V(q5[:npp, 7, 0:1], [[1, 4], [0, 128]]), ALU.mult, ['ND', 'q5', 'tmpo'], ['tmpo'])
            yield
            TTo(U[:npp, g, 0:512].rearrange('p (h c) -> p h c', c=128), t5[:npp, :, 0:128], og[:npp, :].rearrange('p (h c) -> p h c', c=128), ALU.mult, ['tmpo', 'og'], [('U', g, 0)])
            yield
            if not is_s:
                TTo(Cst[:], Cst[:], V(DECs[:, 0, c_idx:c_idx + 1], [[4, 4], [0, 129]]), ALU.mult, ['Cst', 'DECs'], ['Cst'])
                yield
                TTo(Cst[:], Cst[:], pKV[:, :, 0:129], ALU.add, ['Cst'] + bkeys(1, 2), ['Cst'])
                yield
                CP(Cbf[:], Cst[:], ['Cst'], ['Cbf'], eng='act')
                yield

        def softmax_tail(Sx, Px, npp, nk, R_, half):
            sk_ = sinkb[:npp, 4 * half:4 * half + 4]
            RED(sst[:npp, 0, 0:4], Sx, ALU.max, R_, ['sst'])
            yield
            TTo(sst[:npp, 1, 0:4], sst[:npp, 0, 0:4], sk_, ALU.max, ['sst', 'sinkb'], ['sst'])
            yield
            TS(sst[:npp, 7, 0:4], sst[:npp, 1, 0:4], -1.0, None, ALU.mult, None, ['sst'], ['sst'])
            yield
            MSET(sst[:npp, 2, 0:4], 0.0, ['sst'])
            yield
            for hh_ in range(4):
                ACT(Px[:, hh_, :], Sx[:, hh_, :], AF.Exp, R_ + ['sst'], ['Px', 'sst'], bias=sst[:npp, 7, hh_:hh_ + 1], accum_out=sst[:npp, 2, hh_:hh_ + 1])
                yield
            TTo(sst[:npp, 3, 0:4], sk_, sst[:npp, 1, 0:4], ALU.subtract, ['sst', 'sinkb'], ['sst'])
            yield
            ACT(sst[:npp, 4, 0:4], sst[:npp, 3, 0:4], AF.Exp, ['sst'], ['sst'])
            yield
            TTo(sst[:npp, 5, 0:4], sst[:npp, 4, 0:4], sst[:npp, 2, 0:4], ALU.add, ['sst'], ['sst'])
            yield
            RCP(sst[:npp, 6, 0:4], sst[:npp, 5, 0:4], ['sst'], ['sst'])
            yield
            return sst[:npp, 6, 0:4]

        def swa_group(g, first):
            zq = [('qaT', 0), ('kaT', 0), 'kaT_h']
            c0 = g * 128
            vkeys = ['va0', ('va', g)] + ([('va', g - 1)] if g > 0 else [])
            for half in range(2):
                pS = bank(5, 2).rearrange('p (h k) -> p h k', k=256)
                for hh in range(4):
                    h = 4 * half + hh
                    MM(pS[:, hh, :], qaT[:, h // 2, c0:c0 + 128], kaT[:, 2 * half + h % 2, c0:c0 + 256], True, True, zq, bkeys(5, 2))
                    yield
                for hh in range(4):
                    h = 4 * half + hh
                    STT(Ssb[:, hh, :], biasT[:, 0, :], slopeb[:, h:h + 1], pS[:, hh, :], ALU.mult, ALU.add, bkeys(5, 2) + ['biasT', 'slopeb'], ['Ssb'])
                    yield
                if first:
                    TS(Ssb[:, :, 0:128], Ssb[:, :, 0:128], role[:, 16:17], None, ALU.add, None, ['Ssb', 'role'], ['Ssb'])
                    yield
                if os.environ.get('SSTOP') == '1':
                    continue
                rden = (yield from softmax_tail(Ssb[:], Pb[:], 128, 256, ['Ssb'], half))
                if os.environ.get('SSTOP') == '2':
                    continue
                pb_ = 7
                PT = bank(pb_).bitcast(BF16).rearrange('p (h t) -> p h t', t=128)
                for hh in range(4):
                    for kt in range(2):
                        TR(PT[:, 2 * hh + kt, :], Pb[:, hh, kt * 128:(kt + 1) * 128], identb[:, :], ['Px', 'identb'], bkeys(pb_))
                        yield
                CP(PTs[:, 0:4, :], PT[:, 0:4, :], bkeys(pb_), ['PTs'], eng='act')
                yield
                CP(PTs[:, 4:8, :], PT[:, 4:8, :], bkeys(pb_), ['PTs'], eng='act')
                yield
                if os.environ.get('SSTOP') == '3':
                    continue
                pO = bank(7).rearrange('p (h c) -> p h c', c=64)
                for hh in range(4):
                    for kt in range(2):
                        MM(pO[:, hh, :], PTs[:, 2 * hh + kt, :], va[:, g + kt, half * 64:(half + 1) * 64], kt == 0, kt == 1, ['PTs'] + vkeys, bkeys(7))
                        yield
                TTo(U[:, g, 512 + 256 * half:768 + 256 * half].rearrange('p (h c) -> p h c', c=64), pO[:, 0:4, :], V(rden[:, 0:1], [[1, 4], [0, 64]]), ALU.mult, bkeys(7) + ['sst'], [('U', g, 1 + half)])
                yield

        def swa_sample():
            zq = [('qaT', TP), ('kaT', TP)]
            Ss = Ssb[0:4, 0:3, :].rearrange('p a k -> p (a k)').rearrange('p (h k) -> p h k', k=192)
            Ps = Pb[0:4, 0:3, :].rearrange('p a k -> p (a k)').rearrange('p (h k) -> p h k', k=192)
            for b in range(NB):
                i = b % 2
                for v in range(4):
                    hk, par = (v // 2, v % 2)
                    DMA(ckb[i][:, v, 64 * par:64 * par + 64], ck[b][:, 64 * hk:64 * hk + 64], [('ckb', i)], [('ckb', i, v)], key='ck%d_%d' % (i, v), eng='pool')
                    yield
                DMA(cvb[i][:], cv[b], (), [('cvb', i)], key='cv%d' % i, eng='pool')
                yield
                pk_ = bank(7)
                for v in range(4):
                    MM(pk_[:, v * 128:(v + 1) * 128], ckb[i][:, v, :], identb[:, :], True, True, [('ckb', i), ('ckb', i, v), 'identb'], bkeys(7))
                    yield
                CP(kTc[i][:].rearrange('p h t -> p (h t)'), pk_[:, 0:512], bkeys(7), [('kTc', i)], eng='act')
                yield
                for half in range(2):
                    pSc = bank(5).rearrange('p (h k) -> p h k', k=128)
                    pSn = bank(6).rearrange('p (h k) -> p h k', k=64)
                    for hh in range(4):
                        h = 4 * half + hh
                        q_ = qaT[:, h // 2, TP + 4 * b:TP + 4 * b + 4]
                        MM(pSc[0:4, hh, :], q_, kTc[i][:, 2 * half + h % 2, :], True, True, zq + [('kTc', i)], bkeys(5))
                        yield
                        MM(pSn[0:4, hh, :], q_, kaT[:, 2 * half + h % 2, 128 + TP:128 + TP + 64], True, True, zq, bkeys(6))
                        yield
                    for hh in range(4):
                        h = 4 * half + hh
                        STT(Ss[0:4, hh, 0:128], bsc[0:4, 0, :], slopeb[0:4, h:h + 1], pSc[0:4, hh, :], ALU.mult, ALU.add, bkeys(5) + ['bsc', 'slopeb'], ['Ssb'])
                        yield
                        STT(Ss[0:4, hh, 128:192], tbl[0:4, 0, 60 - 4 * b:124 - 4 * b], slopeb[0:4, h:h + 1], pSn[0:4, hh, :], ALU.mult, ALU.add, bkeys(6) + ['tbl', 'slopeb'], ['Ssb'])
                        yield
                    rden = (yield from softmax_tail(Ss, Ps, 4, 192, ['Ssb'], half))
                    PT = bank(6).bitcast(BF16)[:, 0:32].rearrange('p (h k q) -> p h k q', k=2, q=4)
                    for hh in range(4):
                        TR(PT[:, hh, 0, :], Ps[0:4, hh, 0:128], identb[0:4, 0:4], ['Px', 'identb'], bkeys(6))
                        yield
                        TR(PT[0:64, hh, 1, :], Ps[0:4, hh, 128:192], identb[0:4, 0:4], ['Px', 'identb'], bkeys(6))
                        yield
                    CP(PTss[:, 0:4, 0, :], PT[:, :, 0, :], bkeys(6), ['PTss'], eng='act')
                    yield
                    CP(PTss[0:64, 0:4, 1, :], PT[0:64, :, 1, :], bkeys(6), ['PTss'])
                    yield
                    pO = bank(7).rearrange('p (h c) -> p h c', c=64)
                    for hh in range(4):
                        MM(pO[0:4, hh, :], PTss[:, hh, 0, :], cvb[i][:, half * 64:(half + 1) * 64], True, False, ['PTss', ('cvb', i)], bkeys(7))
                        yield
                        MM(pO[0:4, hh, :], PTss[0:64, hh, 1, :], va[0:64, 5, half * 64:(half + 1) * 64], False, True, ['PTss', ('va', 4)], bkeys(7))
                        yield
                    TTo(uab[i][0:4, 256 * half:256 * half + 256].rearrange('p (h c) -> p h c', c=64), pO[0:4, 0:4, :], V(rden[:, 0:1], [[1, 4], [0, 64]]), ALU.mult, bkeys(7) + ['sst'], [('uab', i)])
                    yield
                DMA(U[4 * b:4 * b + 4, 4, 512:1024], uab[i][0:4, :], [('uab', i)], [('U', 4, 1, b)], key='uab%d' % i)
                yield
                DMA(sk[b, 0:124, :], ck[b, 4:128, :], (), [('o_sk', b)], key='o_sk')
                yield
                DMA(sv[b, 0:124, :], cv[b, 4:128, :], (), [('o_sv', b)], key='o_sv')
                yield

        def w_out_stage(has_s):
            grp = groups_of(has_s)
            for g, npp, c0 in grp:
                b = 6 + R2('pT')
                pT = bank(b).bitcast(BF16).rearrange('p (c t) -> p c t', t=128)
                for c in range(8):
                    TR(pT[:, c, 0:npp], U[:npp, g, c * 128:(c + 1) * 128], identb[:npp, :npp], [('U', g, 0), ('U', g, 1), ('U', g, 2), 'identb'] + [('U', 4, 1, bb) for bb in range(NB)], bkeys(b))
                CP(hnT[:, :, c0:c0 + npp], pT[:, :, 0:npp], bkeys(b), [('hnT', g)], eng='act')
            load_gp(3)
            slots = [wload(colblk(wout, 256 * blk, 256)) for blk in range(4)]
            for g, npp, c0 in grp:
                pyb = (4, 0)[R2('py')]
                py = bank(pyb, 2)
                for blk in range(4):
                    ws, wk_ = slots[blk]
                    for kc in range(8):
                        MM(py[:npp, 256 * blk:256 * blk + 256], hnT[:, kc, c0:c0 + npp], ws[:, kc, 0:256], kc == 0, kc == 7, [wk_, ('hnT', g)], bkeys(pyb, 2))
                postnorm(py, bkeys(pyb, 2), 1.0, g, npp)
        ZW = o_[0]
        ZSET = {'qT', 'kT', 'qaT', 'kaT', 'va', 'z', 'vones', 'kaT_h', 'va0', 'hT'}

        def zkeys():
            return [k for k in S.last_writer if (k[0] if isinstance(k, tuple) else k) in ZSET]

        def mlstm_state_only(g, c_idx):
            TTo(kw[:, :, :], ktok[:, g, :].rearrange('p (h c) -> p h c', c=128), V(FT[:, g, 0:1], [[1, 4], [0, 128]]), ALU.mult, [zk('ktok', g), 'FT'], ['kw'])
            pKV = bank(2, 2).rearrange('p (h c) -> p h c', c=256)
            for h in range(4):
                MM(pKV[:, h, 0:129], kw[:, h, :], vaug[:, g, h, 0:129], True, True, ['kw', zk('vaug', g), 'vones'], bkeys(2, 2))
            TTo(Cst[:], Cst[:], V(DECs[:, 0, c_idx:c_idx + 1], [[4, 4], [0, 129]]), ALU.mult, ['Cst', 'DECs'], ['Cst'])
            TTo(Cst[:], Cst[:], pKV[:, :, 0:129], ALU.add, ['Cst'] + bkeys(2, 2), ['Cst'])
        first_wbig = [True]
        prenorm(0, nt == 1 and sample)
        for t in range(nt):
            has_s = t == nt - 1 and sample
            ffn(0, 1, has_s)
            load_wbig(0 if t + 1 < nt else 1)
            prenorm(2, has_s)
            w_in(has_s)
            DMA(gsI[t], IGs[:], ['IGs'], ['gsI'], key='sp_gI')
            DMA(gsF[t], FGs[:], ['FGs'], ['gsF'], key='sp_gF')
            DMA(x1s[t], X[:].rearrange('p g d -> p (g d)'), [('X', g) for g in range(5)], ['x1s'], key='sp_x')
            DMA(zs[t, :, 0:ZW], big[:, 0:ZW], zkeys(), ['zs'], key='sp_z')
            if t + 1 < nt:
                nhs = t + 1 == nt - 1 and sample
                DMA(X[:, 0:4, :], xp[(t + 1) * TP:(t + 2) * TP, :].rearrange('(g p) d -> p g d', p=128), (), [('X', g) for g in range(4)], key='x_in')
                if nhs:
                    DMA(X[0:64, 4, :], xs[:, :], (), [('X', 4)], key='x_in_s')
                prenorm(0, nhs)
            gates(False, phase=1)
            for g in range(4):
                mlstm_state_only(g, g)
            if has_s:
                DMA(pk[:, :], kvf[:, 0, 0:128], [('kvf', 3)], ['o_pk'], key='o_pkv')
                DMA(pv[:, :], kvf[:, 0, 128:256], [('kvf', 3)], ['o_pv'], key='o_pkv')
                for b in range(NB):
                    DMA(sk[b, 124:128, :], kvf[4 * b:4 * b + 4, 1, 0:128], [('kvf', 4)], [('o_sk2', b)], key='o_pkv')
                    DMA(sv[b, 124:128, :], kvf[4 * b:4 * b + 4, 1, 128:256], [('kvf', 4)], [('o_sv2', b)], key='o_pkv')
        pay = tmpn[:, 0:PW]
        CP(pay[:, 0:516], Cst[:].rearrange('p h c -> p (h c)'), ['Cst'], ['tmpn'])
        MSET(pay[:, 518:520], 0.0, ['tmpn'])
        CP(pay[:, 516:517], MUprev[:, 0:1], ['MUprev'], ['tmpn'])
        CP(pay[:, 517:518], Bprev[:, 0:1], ['Bprev'], ['tmpn'])
        CP(pay[:, 520:776].bitcast(BF16).rearrange('p (v t) -> p v t', t=128), kaT[:, :, TP:TP + 128], zkeys(), ['tmpn'])
        CP(pay[:, 776:840].bitcast(BF16), va[:, 4, :], zkeys(), ['tmpn'])
        DMA(exin[:, :], pay, ['tmpn'], ['exin'], key='ex_in')

        def is_tail(k):
            return k == 'vones' or (isinstance(k, tuple) and k[0] == 'z' and k[1] in ('ktok', 'vaug'))

        def reload_head(t):
            DMA(big[:, 0:ZT], zs[t, :, 0:ZT], ['zs'], [k for k in zkeys() if not is_tail(k)], key='rl_z')

        def reload_tail(t):
            DMA(big[:, ZT:ZW], zs[t, :, ZT:ZW], ['zs'], [k for k in zkeys() if is_tail(k)] + ['kw'], key='rl_zt')

        def reload_x(t):
            DMA(X[:].rearrange('p g d -> p (g d)'), x1s[t], ['x1s'], [('X', g) for g in range(5)], key='rl_x')

        def reload(t):
            reload_head(t)
            reload_tail(t)
            reload_x(t)

        def reload_g(t):
            DMA(IGs[:], gsI[t], ['gsI'], ['IGs'], key='rl_gI')
            DMA(FGs[:], gsF[t], ['gsF'], ['FGs'], key='rl_gF')
        reload(0)
        reload_g(0)
        S.op('pool', lambda e: e.collective_compute('AllGather', ALU.bypass, replica_groups=[[0, 1, 2, 3], [4, 5, 6, 7]], ins=[exin.ap().opt()], outs=[exout.ap().opt()]), ['exin'], ['exout'], dma_key='cc', inc=1)
        for g_ in (1, 2, 3):
            drain(swa_group(g_, False))
        MSET(Cst[:], 0.0, ['Cst'])
        MSET(cmb[:, 0:1], 0.0, ['cmb'])
        MSET(kaTh[:], 0.0, ['kaTh'])
        MSET(vah[:], 0.0, ['vah'])
        payr = Ssb[:].rearrange('p h k -> p (h k)')[:, 0:PW]
        for r in range(3):
            DMA(payr, exout[r * 128:(r + 1) * 128, :], ['exout'], ['Ssb'], key='ex_rd')
            mk_ = role[:, r:r + 1]
            TS(cmb[:, 1:2], payr[:, 517:518], mk_, None, ALU.mult, None, ['Ssb', 'role'], ['cmb'])
            TTo(cmb[:, 2:3], payr[:, 516:517], payr[:, 517:518], ALU.subtract, ['Ssb'], ['cmb'])
            TS(cmb[:, 2:3], cmb[:, 2:3], -NEG, mk_, ALU.add, ALU.mult, ['cmb', 'role'], ['cmb'])
            TS(cmb[:, 2:3], cmb[:, 2:3], NEG, None, ALU.add, None, ['cmb'], ['cmb'])
            TTo(cmb[:, 3:4], cmb[:, 0:1], cmb[:, 1:2], ALU.subtract, ['cmb'], ['cmb'])
            TTo(cmb[:, 4:5], cmb[:, 3:4], cmb[:, 2:3], ALU.max, ['cmb'], ['cmb'])
            TTo(cmb[:, 5:6], cmb[:, 3:4], cmb[:, 4:5], ALU.subtract, ['cmb'], ['cmb'])
            TTo(cmb[:, 6:7], cmb[:, 2:3], cmb[:, 4:5], ALU.subtract, ['cmb'], ['cmb'])
            ACT(cmb[:, 7:9], cmb[:, 5:7], AF.Exp, ['cmb'], ['cmb'])
            TS(cmb[:, 8:9], cmb[:, 8:9], mk_, None, ALU.mult, None, ['cmb', 'role'], ['cmb'])
            CP(cmb[:, 0:1], cmb[:, 4:5], ['cmb'], ['cmb'])
            pd = bank(1)
            for h in range(4):
                MM(pd[:, 2 * h:2 * h + 2], Esel[0:4, h, :], cmb[0:4, 7:9], True, True, ['Esel', 'cmb'], bkeys(1))
            CP(ABt[:].rearrange('p h c -> p (h c)'), pd[:, 0:8], bkeys(1), ['ABt'])
            TTo(Cst[:], Cst[:], V(ABt[:, 0, 0:1], [[2, 4], [0, 129]]), ALU.mult, ['Cst', 'ABt'], ['Cst'])
            TTo(tmpo[:], payr[:, 0:516].rearrange('p (h c) -> p h c', c=129), V(ABt[:, 0, 1:2], [[2, 4], [0, 129]]), ALU.mult, ['Ssb', 'ABt'], ['tmpo'])
            TTo(Cst[:], Cst[:], tmpo[:], ALU.add, ['Cst', 'tmpo'], ['Cst'])
            STT(kaTh[:].rearrange('p v t -> p (v t)'), payr[:, 520:776].bitcast(BF16), role[:, 8 + r:9 + r], kaTh[:].rearrange('p v t -> p (v t)'), ALU.mult, ALU.add, ['Ssb', 'role', 'kaTh'], ['kaTh'])
            STT(vah[:], payr[:, 776:840].bitcast(BF16), role[:, 8 + r:9 + r], vah[:], ALU.mult, ALU.add, ['Ssb', 'role', 'vah'], ['vah'])
        CP(MUprev[:, 0:1], cmb[:, 0:1], ['cmb'], ['MUprev'])
        MSET(Bprev[:], 0.0, ['Bprev'])
        CP(Cbf[:], Cst[:], ['Cst'], ['Cbf'], eng='act')
        for t in range(nt):
            has_s = t == nt - 1 and sample
            grp = groups_of(has_s)
            if t > 0:
                reload_x(t)
            CP(kaT[:, :, 0:128], kaTh[:], ['kaTh'], ['kaT_h'], eng='act')
            CP(va[:, 0, :], vah[:], ['vah'], ['va0'], eng='act')
            if t == 0:
                gates(has_s, phase=2)
            for g, npp, c0 in grp:
                gens_ = [mlstm_group(g, npp, c0, g, g == 4)]
                if g < 4:
                    if t > 0 or g == 0:
                        gens_.append(swa_group(g, t == 0 and g == 0))
                else:
                    gens_.append(swa_sample())
                run_rr(gens_)
            CP(kaTh[:], kaT[:, :, TP:TP + 128], [('kaT', 0)], ['kaTh'], eng='act')
            CP(vah[:], va[:, 4, :], [('va', 3)], ['vah'], eng='act')
            if t + 1 < nt:
                reload_tail(t + 1)
                reload_g(t + 1)
                gates(t + 1 == nt - 1 and sample, phase=2)
            w_out_stage(has_s)
            prenorm(4, has_s)
            ffn(1, 5, has_s)
            if t + 1 < nt:
                load_wbig(1)
                reload_head(t + 1)
            DMA(yp[t * TP:(t + 1) * TP, :].rearrange('(g p) d -> p g d', p=128), X[:, 0:4, :], [('X', g) for g in range(4)], ['o_yp'], key='o_y')
            if has_s:
                DMA(ys[:, :], X[0:64, 4, :], [('X', 4)], ['o_ys'], key='o_y')
        DMA(pC.rearrange('h k v -> k h v'), Cst[:, :, 0:128], ['Cst'], ['o_pC'], key='o_fin')
        TR(bank(1)[0:4, 0:128], Cst[:, :, 128], ident[:, :], ['Cst', 'ident'], bkeys(1))
        CP(pn_sb[:], bank(1)[0:4, 0:128], bkeys(1), ['pn_sb'])
        DMA(pn[:, :], pn_sb[:], ['pn_sb'], ['o_pn'], key='o_fin')
        TTo(pm_sb[:], MUprev[:], Bprev[:], ALU.subtract, ['MUprev', 'Bprev'], ['pm_sb'])
        DMA(pm[:, :], pm_sb[0:4, :], ['pm_sb'], ['o_pm'], key='o_fin')
        TR(bank(1)[0:64, 128:256], nTout[:, :], ident[:, :], ['nTout', 'ident'], bkeys(1))
        CP(snout[:], bank(1)[0:64, 128:256], bkeys(1), ['snout'])
        DMA(sno[:, :], snout[:], ['snout'], ['o_sno'], key='o_fin')
        out_keys = [k for k in S.dma_counts if k.startswith('o_')]
        S.emit(final_wait_keys=out_keys)
    return nc

def _consts():
    c = {}
    c['c_ident'] = np.eye(128, dtype=np.float32)
    s = np.arange(128)
    c['c_maskp'] = (s[:, None] <= s[None, :]).astype(np.float32)
    s = np.arange(64)
    c['c_masks'] = ((s[:, None] <= s[None, :]) & (s[:, None] // 4 == s[None, :] // 4)).astype(np.float32)
    slopes = np.exp2(-8.0 * np.arange(1, 9, dtype=np.float32) / 8).astype(np.float32)
    c['c_slope'] = np.broadcast_to(slopes[None, :], (128, 8)).copy()
    BIGN = -8000000.0
    qi = np.arange(128)[:, None]
    kj = np.arange(256)[None, :]
    dist = 128 + qi - kj
    valid = (dist >= 0) & (dist < 128)
    c['c_bias'] = np.where(valid, -dist.astype(np.float32), BIGN).astype(np.float32)
    t = np.arange(4)[:, None]
    j = np.arange(128)[None, :]
    d = 128 + t - j
    v = (d >= 0) & (d < 128)
    c['c_bsc'] = np.where(v, -d.astype(np.float32), BIGN).astype(np.float32)
    x = np.arange(124)[None, :] - 60
    d2 = t - x
    v2 = (x >= 0) & (x <= t)
    c['c_tb'] = np.where(v2, -d2.astype(np.float32), BIGN).astype(np.float32)
    bm = (np.arange(64)[None, :] // 4 == np.arange(NB)[:, None]).astype(np.float32)
    c['c_bm'] = np.broadcast_to(bm.reshape(1, NB * 64), (128, NB * 64)).copy()
    c['c_bmT'] = bm.T.copy()
    E = np.zeros((4, 4, 128), np.float32)
    for h in range(4):
        E[h, h, :] = 1.0
    c['c_E'] = E.reshape(4, 4 * 128)
    return c
_NC = None

def kernel(x_prompt, x_sample, cache_swa_k, cache_swa_v, state_mlstm_C, state_mlstm_n, state_mlstm_m, norm_gains, ffn_w_gate, ffn_w_up, ffn_w_down, w_in, b_gate, mlstm_norm_gain, attn_sinks, w_out):
    global _NC
    f = lambda a: np.ascontiguousarray(np.asarray(a, dtype=np.float32))
    x_prompt, x_sample = (f(x_prompt), f(x_sample))
    ckk, cvv = (f(cache_swa_k)[0], f(cache_swa_v)[0])
    sCC, snn, smm = (f(state_mlstm_C)[0], f(state_mlstm_n)[0], f(state_mlstm_m)[0])
    consts = _consts()
    shared = dict(gains=f(norm_gains)[0], wg=f(ffn_w_gate)[0], wu=f(ffn_w_up)[0], wd=f(ffn_w_down)[0], win=f(w_in)[0], bgate=f(b_gate)[0], mng=f(mlstm_norm_gain)[0], sinks=f(attn_sinks)[0], wout=f(w_out)[0])
    shared.update(consts)
    in_maps = []
    for c in range(8):
        m = dict(shared)
        m['xp'] = np.ascontiguousarray(x_prompt[c // 4, SEQ * (c % 4):SEQ * (c % 4 + 1)])
        role = np.zeros((128, 17), np.float32)
        for r in range(4):
            if r < c % 4:
                role[:, r] = 1.0
            if r == c % 4 - 1:
                role[:, 8 + r] = 1.0
        role[:, 16] = NEG if c % 4 == 0 else 0.0
        m['c_role'] = role
        b0 = NB * c
        m['xs'] = x_sample[b0:b0 + NB].reshape(NS, D)
        m['ck'] = ckk[b0:b0 + NB].reshape(NB, 128, 128)
        m['cv'] = cvv[b0:b0 + NB].reshape(NB, 128, 128)
        m['sC'] = sCC[b0:b0 + NB]
        m['sn'] = snn[b0:b0 + NB].reshape(NB * 4, 128)
        m['sm'] = smm[b0:b0 + NB]
        in_maps.append(m)
    if _NC is None:
        _NC = build_program()
    res = run_bass_kernel_spmd(_NC, in_maps, core_ids=list(range(8)))
    r = res.results
    yp = np.stack([np.concatenate([r[4 * b + j]['yp'] for j in range(4)], 0) for b in range(2)], 0)
    ys = np.concatenate([r[c]['ys'].reshape(NB, 4, D) for c in range(8)], 0)
    pk = np.stack([r[3]['pk'], r[7]['pk']], 0).reshape(1, 2, 128, 2, 64)
    pv = np.stack([r[3]['pv'], r[7]['pv']], 0).reshape(1, 2, 128, 2, 64)
    pC = np.stack([r[3]['pC'], r[7]['pC']], 0)[None]
    pn = np.stack([r[3]['pn'], r[7]['pn']], 0)[None]
    pm = np.stack([r[3]['pm'].reshape(4), r[7]['pm'].reshape(4)], 0)[None]
    sk = np.concatenate([r[c]['sk'] for c in range(8)], 0).reshape(1, 128, 128, 2, 64)
    sv = np.concatenate([r[c]['sv'] for c in range(8)], 0).reshape(1, 128, 128, 2, 64)
    sCo = np.concatenate([r[c]['sCo'] for c in range(8)], 0)[None]
    sno = np.concatenate([r[c]['sno'].reshape(NB, 4, 128) for c in range(8)], 0)[None]
    smo = np.concatenate([r[c]['smo'] for c in range(8)], 0)[None]
    outs = (yp, ys, pk, pv, pC, pn, pm, sk, sv, sCo, sno, smo)
    return tuple((np.ascontiguousarray(o, dtype=np.float32) for o in outs))
```

```python
import contextlib
import os
import numpy as np
import concourse.bass as bass
import concourse.mybir as mybir
from concourse.bass_utils import run_bass_kernel_spmd
F32 = mybir.dt.float32
BF16 = mybir.dt.bfloat16
ALU = mybir.AluOpType
AF = mybir.ActivationFunctionType
AX = mybir.AxisListType
ENGS = ('pe', 'act', 'dve', 'pool', 'sp')
D = 1024
DFF = 2816
NJ = 22
DIN = 2824
SEQ = 2048
NTILE = 4
PW = 840
TP = 512
NS = 64
NB = 16
EPS = 1e-06
NEG = -30000.0

class _Op:
    __slots__ = ('eng', 'fn', 'deps', 'dma_key', 'dma_cnt', 'signal', 'sig_val', 'idx', 'inc')

class Sched:
    def __init__(self, nc):
        self.nc = nc
        self.ops = []
        self.last_writer = {}
        self.readers = {}
        self.dma_counts = {}
    ALIAS = {'e1': 'FGs', 'lfn': 'FGs', 'Fst': 'tg', 'Ug': 'IGs', 't5': 'tmpo'}

    def _norm(self, k):
        if isinstance(k, tuple) and k[0] in ('IGs', 'FGs'):
            k = k[0]
        return self.ALIAS.get(k, k) if not isinstance(k, tuple) else k

    def op(self, eng, fn, reads=(), writes=(), dma_key=None, inc=16):
        reads = [self._norm(k) for k in reads]
        writes = [self._norm(k) for k in writes]
        writes = writes + [k for k in reads if isinstance(k, tuple) and k[0] == 'bank']
        reads = [k for k in reads if not (isinstance(k, tuple) and k[0] == 'bank')]
        o = _Op()
        o.eng, o.fn, o.idx, o.dma_key, o.inc = (eng, fn, len(self.ops), dma_key, inc)
        o.signal, o.sig_val = (False, None)
        deps = set()
        for r in reads:
            w = self.last_writer.get(r)
            if w is not None:
                deps.add(w)
        for r in writes:
            w = self.last_writer.get(r)
            if w is not None:
                deps.add(w)
            deps.update(self.readers.get(r, ()))
        o.deps = deps
        if dma_key is not None:
            self.dma_counts[dma_key] = self.dma_counts.get(dma_key, 0) + inc
            o.dma_cnt = self.dma_counts[dma_key]
        else:
            o.dma_cnt = None
        self.ops.append(o)
        for r in reads:
            self.readers.setdefault(r, []).append(o.idx)
        for r in writes:
            self.last_writer[r] = o.idx
            self.readers[r] = []
        return o.idx

    def emit(self, final_wait_keys=()):
        nc, ops = (self.nc, self.ops)
        for o in ops:
            nd = set()
            for d in o.deps:
                p = ops[d]
                if p.dma_key is None and o.dma_key is None and (p.eng == o.eng == 'pe'):
                    continue
                nd.add(d)
            o.deps = nd
            for d in nd:
                if ops[d].dma_key is None:
                    ops[d].signal = True
        cnt = {e: 0 for e in ENGS}
        for o in ops:
            if o.dma_key is None and o.signal:
                cnt[o.eng] += 1
                o.sig_val = cnt[o.eng]
        with contextlib.ExitStack() as st:
            esem = {e: st.enter_context(nc.semaphore('s_' + e)) for e in ENGS}
            dsem = {}
            for i, k in enumerate(self.dma_counts):
                dsem[k] = st.enter_context(nc.semaphore('d_%d' % i))
            block = st.enter_context(nc.Block())

            def run(ename):

                def body(eng):
                    waited = {}
                    for o in ops:
                        if o.eng != ename:
                            continue
                        need = {}
                        for d in o.deps:
                            p = ops[d]
                            if p.dma_key is not None:
                                s, v = (dsem[p.dma_key], p.dma_cnt)
                            else:
                                s, v = (esem[p.eng], p.sig_val)
                            if need.get(id(s), (None, 0))[1] < v:
                                need[id(s)] = (s, v)
                        for key, (s, v) in need.items():
                            if waited.get(key, 0) < v:
                                eng.wait_ge(s, v)
                                waited[key] = v
                        ins = o.fn(eng)
                        if o.dma_key is not None:
                            ins.then_inc(dsem[o.dma_key], o.inc)
                        elif o.signal:
                            ins.then_inc(esem[ename], 1)
                    if ename == 'sp':
                        for k in final_wait_keys:
                            eng.wait_ge(dsem[k], self.dma_counts[k])
                return body
            block.tensor(run('pe'))
            block.scalar(run('act'))
            block.vector(run('dve'))
            block.gpsimd(run('pool'))
            block.sync(run('sp'))

def drain(gen):
    try:
        while True:
            next(gen)
    except StopIteration as e:
        return e.value

def run_rr(gens, steps=None):
    gens = list(gens)
    steps = dict(zip(map(id, gens), steps or [1] * len(gens)))
    while gens:
        for g_ in list(gens):
            try:
                for _ in range(steps[id(g_)]):
                    next(g_)
            except StopIteration:
                gens.remove(g_)

def V(ap, dims):
    return bass.AP(ap.tensor, ap.offset, [list(ap.ap[0])] + [list(d) for d in dims])

def build_program(nt=NTILE, upto=9, sample=True):
    assert nt == NTILE
    nc = bass.Bass('TRN2', target_bir_lowering=False)

    def din(name, shape, dt=F32):
        return nc.dram_tensor(name, list(shape), dt, kind='ExternalInput').ap()

    def dout(name, shape, dt=F32):
        return nc.dram_tensor(name, list(shape), dt, kind='ExternalOutput').ap()
    xp = din('xp', [SEQ, D])
    xs = din('xs', [NS, D])
    ck = din('ck', [NB, 128, 128])
    cv = din('cv', [NB, 128, 128])
    sC = din('sC', [NB, 4, 128, 128])
    sn = din('sn', [NB * 4, 128])
    sm = din('sm', [NB, 4])
    gains = din('gains', [6, D])
    wg = din('wg', [2, D, DFF])
    wu = din('wu', [2, D, DFF])
    wd = din('wd', [2, DFF, D])
    win = din('win', [D, DIN])
    bgate = din('bgate', [8])
    mng = din('mng', [512])
    sinks = din('sinks', [8])
    wout = din('wout', [D, D])
    c_ident = din('c_ident', [128, 128])
    c_maskp = din('c_maskp', [128, 128])
    c_masks = din('c_masks', [64, 64])
    c_bias = din('c_bias', [128, 256])
    c_slope = din('c_slope', [128, 8])
    c_bsc = din('c_bsc', [4, 128])
    c_tb = din('c_tb', [4, 124])
    c_bm = din('c_bm', [128, NB * 64])
    c_bmT = din('c_bmT', [64, NB])
    c_E = din('c_E', [4, 4 * 128])
    c_role = din('c_role', [128, 17])
    x1s = nc.dram_tensor('x1s', [NTILE, 128, 5 * D], F32).ap()
    zs = nc.dram_tensor('zs', [NTILE, 128, 18304], BF16).ap()
    gsI = nc.dram_tensor('gsI', [NTILE, 128, TP + NS], F32).ap()
    gsF = nc.dram_tensor('gsF', [NTILE, 128, TP + NS], F32).ap()
    exin = nc.dram_tensor('exin', [128, PW], F32)
    exout = nc.dram_tensor('exout', [4 * 128, PW], F32)
    yp = dout('yp', [SEQ, D])
    ys = dout('ys', [NS, D])
    pk = dout('pk', [128, 128])
    pv = dout('pv', [128, 128])
    pC = dout('pC', [4, 128, 128])
    pn = dout('pn', [4, 128])
    pm = dout('pm', [4, 1])
    sk = dout('sk', [NB, 128, 128])
    sv = dout('sv', [NB, 128, 128])
    sCo = dout('sCo', [NB, 4, 128, 128])
    sno = dout('sno', [NB * 4, 128])
    smo = dout('smo', [NB, 4])
    S = Sched(nc)
    out_keys = []
    with contextlib.ExitStack() as st:

        def sb(name, shape, dt=F32):
            return st.enter_context(nc.sbuf_tensor(name, list(shape), dt))
        TT_ = TP + NS
        X = sb('X', [128, 5, D])
        hnT = sb('hnT', [128, 8, TT_], BF16)
        big = sb('big', [128, 18304], BF16)
        wblk = [sb('wblk%d' % i, [128, 8, 256], BF16) for i in range(4)]
        wbig = sb('wbig', [128, NJ, D], BF16)
        gT = sb('gT', [128, 6, 8])
        gp = sb('gp', [128, 1, D])
        ident = sb('ident', [128, 128])
        identb = sb('identb', [128, 128], BF16)
        maskp = sb('maskp', [128, 128])
        masks = sb('masks', [64, 64])
        biasT = sb('biasT', [128, 1, 256])
        slopeb = sb('slopeb', [128, 8])
        bsc = sb('bsc', [4, 1, 128])
        tbl = sb('tbl', [4, 1, 124])
        bmb = sb('bmb', [128, NB, 64], BF16)
        bmT = sb('bmT', [64, NB])
        Esel = sb('Esel', [4, 4, 128])
        mngb = sb('mngb', [128, 512])
        sinkb = sb('sinkb', [128, 8])
        bi_l = sb('bi_l', [128, 1])
        nbf_l = sb('nbf_l', [128, 1])
        tmpn = sb('tmpn', [128, D])
        stt = sb('stt', [128, 8])
        xn = sb('xn', [128, D], BF16)
        sg = [sb('sg%d' % i, [128, 512]) for i in range(1)]
        IGs = sb('IGs', [128, TT_])
        FGs = sb('FGs', [128, TT_])
        e1 = FGs
        lfn = FGs
        Bneg = sb('Bneg', [128, TT_])
        Ug = IGs
        MU = sb('MU', [128, TP + 1])
        MUs = sb('MUs', [128, NB, 5])
        tg = sb('tg', [128, TT_])
        Fst = tg
        Bprev = sb('Bprev', [128, 1])
        MUprev = sb('MUprev', [128, 1])
        dd = sb('dd', [4, 4])
        dds = sb('dds', [4, NB])
        DECs = sb('DECs', [128, 4, 4])
        DECss = sb('DECss', [128, 4, NB])
        FT = sb('FT', [128, 5, 16])
        smin = sb('smin', [128, NB])
        smout = sb('smout', [128, NB])
        Cst = sb('Cst', [128, 4, 129])
        Cbf = sb('Cbf', [128, 4, 129], BF16)
        Sp = sb('Sp', [128, 4, 128], BF16)
        kw = sb('kw', [128, 4, 128], BF16)
        kwm = sb('kwm', [64, 4, 128], BF16)
        tmpo = sb('tmpo', [128, 4, 129])
        ND = sb('ND', [128, 4, 129])
        q5 = sb('q5', [128, 8, 4])
        og = sb('og', [128, 512], BF16)
        t5 = tmpo
        U = sb('U', [128, 5, D], BF16)
        Ssb = sb('Ssb', [128, 4, 256])
        Pb = sb('Pb', [128, 4, 256], BF16)
        PTs = sb('PTs', [128, 8, 128], BF16)
        sst = sb('sst', [128, 8, 8])
        kvf = sb('kvf', [128, 2, 256])
        qTm = sb('qTm', [128, 2, 4, 64], BF16)
        Cb = [sb('Cb%d' % i, [128, 4, 129]) for i in range(2)]
        Cbb = [sb('Cbb%d' % i, [128, 4, 129], BF16) for i in range(2)]
        snin = sb('snin', [64, 128])
        nTin = sb('nTin', [128, 64])
        nTout = sb('nTout', [128, 64])
        snout = sb('snout', [64, 128])
        ckb = [sb('ckb%d' % i, [128, 4, 128], BF16) for i in range(2)]
        kTc = [sb('kTc%d' % i, [128, 4, 128], BF16) for i in range(2)]
        cvb = [sb('cvb%d' % i, [128, 128], BF16) for i in range(2)]
        PTss = sb('PTss', [128, 8, 2, 4], BF16)
        uab = [sb('uab%d' % i, [4, 512], BF16) for i in range(2)]
        pn_sb = sb('pn_sb', [4, 128])
        pm_sb = sb('pm_sb', [128, 1])
        kaTh = sb('kaTh', [128, 4, 128], BF16)
        vah = sb('vah', [128, 128], BF16)
        role = sb('role', [128, 17])
        cmb = sb('cmb', [128, 12])
        ABt = sb('ABt', [128, 4, 2])
        hT = big[:, 0:NJ * TT_].rearrange('p (j t) -> p j t', t=TT_)
        o_ = [0]

        def carve(n):
            a = big[:, o_[0]:o_[0] + n]
            o_[0] += n
            return a
        qT = carve(4 * TT_).rearrange('p (h t) -> p h t', t=TT_)
        kT = carve(4 * TT_).rearrange('p (h t) -> p h t', t=TT_)
        osig = carve(5 * 512).rearrange('p (g c) -> p g c', c=512)
        qaT = carve(4 * TT_).rearrange('p (h t) -> p h t', t=TT_)
        KW = 128 + TT_
        kaT = carve(4 * KW).rearrange('p (h t) -> p h t', t=KW)
        va = carve(6 * 128).rearrange('p (g c) -> p g c', c=128)
        assert o_[0] >= NJ * TT_
        ZT = o_[0]
        ktok = carve(5 * 512).rearrange('p (g c) -> p g c', c=512)
        vaug = carve(5 * 4 * 130).rearrange('p (g h c) -> p g h c', h=4, c=130)
        assert o_[0] <= 18304
        ps = st.enter_context(nc.psum_tensor('ps', [128, 8, 512], F32))

        def bank(i, n=1):
            return ps[:, i:i + n, :].rearrange('p a b -> p (a b)')

        def bkeys(i, n=1):
            return [('bank', i + k) for k in range(n)]

        def MM(out, lhsT, rhs, start, stop, R, W, **kw_):
            S.op('pe', lambda e: e.matmul(out, lhsT=lhsT, rhs=rhs, start=start, stop=stop, **kw_), R, W)

        def TR(out, in_, idn, R, W):
            S.op('pe', lambda e: e.transpose(out=out, in_=in_, identity=idn), R, W)

        def ACT(out, in_, func, R, W, **kw_):
            S.op('act', lambda e: e.activation(out=out, in_=in_, func=func, **kw_), R, W)

        def TTo(out, in0, in1, op, R, W, eng='dve'):
            S.op(eng, lambda e: e.tensor_tensor(out=out, in0=in0, in1=in1, op=op), R, W)

        def STT(out, in0, scalar, in1, op0, op1, R, W, eng='dve'):
            S.op(eng, lambda e: e.scalar_tensor_tensor(out=out, in0=in0, scalar=scalar, in1=in1, op0=op0, op1=op1), R, W)

        def TS(out, in0, s1, s2, op0, op1, R, W, eng='dve'):
            if s2 is None:
                S.op(eng, lambda e: e.tensor_scalar(out=out, in0=in0, scalar1=s1, scalar2=None, op0=op0), R, W)
            else:
                S.op(eng, lambda e: e.tensor_scalar(out=out, in0=in0, scalar1=s1, scalar2=s2, op0=op0, op1=op1), R, W)

        def CP(out, in_, R, W, eng='dve'):
            if eng == 'act':
                S.op('act', lambda e: e.copy(out=out, in_=in_), R, W)
            else:
                S.op(eng, lambda e: e.tensor_copy(out=out, in_=in_), R, W)

        def RCP(out, in_, R, W):
            S.op('dve', lambda e: e.reciprocal(out=out, in_=in_), R, W)

        def RED(out, in_, op, R, W):
            S.op('dve', lambda e: e.tensor_reduce(out=out, in_=in_, axis=AX.X, op=op), R, W)

        def SCAN(out, d0, init, op0, R, W):
            S.op('dve', lambda e: e.tensor_tensor_scan(out=out, data0=d0, data1=d0, initial=init, op0=op0, op1=ALU.bypass), R, W)

        def MSET(ap, val, W, eng='dve'):
            S.op(eng, lambda e: e.memset(ap, val), (), W)
        dctr = [0]
        nodma = [False]

        def DMA(out, in_, R, W, key=None, eng='sp', slow=False):
            if nodma[0] and eng == 'pool' and (key is not None) and key.startswith('w_'):
                return key
            if key is None:
                dctr[0] += 1
                key = 'dk%d' % (dctr[0] % 24)
            if slow:
                S.op(eng, lambda e: e.dma_start(out=out, in_=in_, allow_slow_non_contiguous=True), R, W, dma_key=key)
            else:
                S.op(eng, lambda e: e.dma_start(out=out, in_=in_), R, W, dma_key=key)
            return key

        DMA(X[:, 0:4, :], xp[0:TP, :].rearrange('(g p) d -> p g d', p=128), (), [('X', g) for g in range(4)], key='x_in')

        def LD(t, src, name):
            DMA(t, src, (), [name], key='c_' + name)
        LD(ident[:], c_ident[:, :], 'ident')
        LD(maskp[:], c_maskp[:, :], 'maskp')
        LD(masks[:], c_masks[:, :], 'masks')
        LD(biasT[:].rearrange('p h k -> p (h k)'), c_bias[:, :], 'biasT')
        LD(slopeb[:], c_slope[:, :], 'slopeb')
        LD(bsc[:].rearrange('p h k -> p (h k)'), c_bsc[:, :], 'bsc')
        LD(tbl[:].rearrange('p h k -> p (h k)'), c_tb[:, :], 'tbl')
        DMA(bmb[:].rearrange('p b t -> p (b t)'), c_bm[:, :], (), ['bmb'], key='c_bmb', eng='pool')
        LD(bmT[:], c_bmT[:, :], 'bmT')
        LD(Esel[:].rearrange('p h k -> p (h k)'), c_E[:, :], 'Esel')
        LD(role[:], c_role[:, :], 'role')
        CP(identb[:], ident[:], ['ident'], ['identb'])
        DMA(tmpn[0:6, :], gains[:, :], (), ['tmpn'], key='c_gT')
        pg_ = bank(1)
        for c in range(8):
            TR(pg_[:, 6 * c:6 * c + 6], tmpn[0:6, c * 128:(c + 1) * 128], ident[0:6, 0:6], ['tmpn', 'ident'], bkeys(1))
        CP(gT[:], V(pg_[:, 0:1], [[1, 6], [6, 8]]), bkeys(1), ['gT'])

        def load_gp(gi):
            DMA(gp[:, 0, :], bass.AP(gains.tensor, gains[gi, :].offset, [[0, 128], [1, D]]), (), ['gp'], key='c_gp')
        DMA(mngb[:], bass.AP(mng.tensor, mng.offset, [[0, 128], [1, 512]]), (), ['mngb'], key='c_mngb')
        DMA(sinkb[:], bass.AP(sinks.tensor, sinks.offset, [[0, 128], [1, 8]]), (), ['sinkb'], key='c_sinkb')
        MSET(bi_l[:], 0.0, ['bi_l'])
        MSET(nbf_l[:], 0.0, ['nbf_l'])
        MSET(smin[:], 0.0, ['smin'])
        for f in range(4):
            DMA(bi_l[32 * f:32 * f + 4, :], bgate[0:4].rearrange('(p o) -> p o', o=1), ['bi_l'], [('bi_l', f)], key='c_bil', slow=True)
            DMA(nbf_l[32 * f:32 * f + 4, :], bgate[4:8].rearrange('(p o) -> p o', o=1), ['nbf_l'], [('nbf_l', f)], key='c_bil', slow=True)
            DMA(smin[32 * f:32 * f + 4, :], sm.rearrange('b h -> h b'), ['smin'], [('smin', f)], key='c_smin', slow=True)
        lane_keys = [(nm_, f) for nm_ in ('bi_l', 'nbf_l', 'smin') for f in range(4)]
        neg_done = [False]
        MSET(big[:], 0.0, ['kaT_h', 'va0', 'vones'], eng='pool')
        MSET(IGs[:], 0.0, ['IGs'], eng='pool')
        MSET(FGs[:], 0.0, ['FGs'], eng='pool')
        MSET(X[:, 4, :], 0.0, [('X', 4)], eng='pool')
        MSET(Cst[:], 0.0, ['Cst'])
        MSET(Cbf[:], 0.0, ['Cbf'], eng='pool')
        MSET(Bprev[:], 0.0, ['Bprev'])
        MSET(MUprev[:], NEG, ['MUprev'])
        MSET(kaT[:, :, 0:128], 0.0, ['kaT_h'], eng='pool')
        MSET(va[:, 0, :], 0.0, ['va0'], eng='pool')
        MSET(vaug[:, :, :, 128:129], 1.0, ['vones'], eng='pool')
        for i in range(2):
            MSET(ckb[i][:], 0.0, [('ckb', i)], eng='pool')
        DMA(snin[:], sn[:, :], (), ['snin'], key='c_snin')
        TR(bank(1)[:, 0:64], snin[:, :], ident[0:64, 0:64], ['snin', 'ident'], bkeys(1))
        CP(nTin[:], bank(1)[:, 0:64], bkeys(1), ['nTin'])
        wq = []
        wslot = [0]

        extra_keys = {}

        def wload(src_fn):
            s = wslot[0] % 4
            wslot[0] += 1
            rk = 'wblk%d' % s
            extra_keys.pop(rk, None)
            src_fn(wblk[s], rk, 'w_slot%d' % s)
            return (wblk[s], rk)

        def colblk(Wap, c0, ncols):

            def f(slot, rk, key):
                DMA(slot[:, :, 0:ncols], Wap[:, c0:c0 + ncols].rearrange('(kc p) c -> p kc c', p=128), (), [rk], key=key, eng='pool')
            return f

        def colparts(Wap, parts):

            def f(slot, rk, key):
                extra_keys[rk] = [(rk, pi) for pi in range(1, len(parts))]
                for pi, (d0, c0, ncols) in enumerate(parts):
                    DMA(slot[:, :, d0:d0 + ncols], Wap[:, c0:c0 + ncols].rearrange('(kc p) c -> p kc c', p=128),
                        () if pi == 0 else [rk], [rk] if pi == 0 else [(rk, pi)], key=key if pi == 0 else key + '_p%d' % pi, eng='pool')
            return f

        def load_wbig(f):
            for j2 in range(11):
                DMA(wbig[:, 2 * j2:2 * j2 + 2, :], wd[f, 256 * j2:256 * j2 + 256, :].rearrange('(j p) c -> p j c', p=128), (), [('wbig', j2)], key='w_big%d' % j2, eng='pool')
        rot = {}

        def R2(name, n=2):
            rot[name] = (rot.get(name, -1) + 1) % n
            return rot[name]

        def groups_of(has_s):
            gs = [(g, 128, g * 128) for g in range(4)]
            if has_s:
                gs.append((4, 64, TP))
            return gs

        def prenorm(gi, has_s):
            for g, npp, c0 in groups_of(has_s):
                Xg = ('X', g)
                MSET(stt[:npp, 0:1], 0.0, ['stt'])
                ACT(xn[:npp, :], X[:npp, g, :], AF.Square, [Xg], ['xn', 'stt'], accum_out=stt[:npp, 0:1])
                ACT(stt[:npp, 1:2], stt[:npp, 0:1], AF.Sqrt, ['stt'], ['stt'], scale=1.0 / D, bias=EPS)
                RCP(stt[:npp, 2:3], stt[:npp, 1:2], ['stt'], ['stt'])
                TS(xn[:npp, :], X[:npp, g, :], stt[:npp, 2:3], None, ALU.mult, None, [Xg, 'stt'], ['xn'])
                b = 6 + R2('pT')
                pT = bank(b).bitcast(BF16).rearrange('p (c t) -> p c t', t=128)
                for c in range(8):
                    TR(pT[:, c, 0:npp], xn[:npp, c * 128:(c + 1) * 128], identb[:npp, :npp], ['xn', 'identb'], bkeys(b))
                TTo(hnT[:, :, c0:c0 + npp], pT[:, :, 0:npp], V(gT[:, gi, :], [[1, 8], [0, npp]]), ALU.mult, bkeys(b) + ['gT'], [('hnT', g)])

        def postnorm(py, pkeys, fac, g, npp):
            Xg = ('X', g)
            MSET(stt[:npp, 4:5], 0.0, ['stt'])
            ACT(tmpn[:npp, :], py[:npp, :], AF.Square, pkeys, ['tmpn', 'stt'], accum_out=stt[:npp, 4:5])
            ACT(stt[:npp, 5:6], stt[:npp, 4:5], AF.Sqrt, ['stt'], ['stt'], scale=1.0 / D, bias=EPS)
            RCP(stt[:npp, 6:7], stt[:npp, 5:6], ['stt'], ['stt'])
            STT(tmpn[:npp, :], py[:npp, :], stt[:npp, 6:7], gp[:npp, 0, :], ALU.mult, ALU.mult, pkeys + ['stt', 'gp'], ['tmpn'])
            STT(X[:npp, g, :], tmpn[:npp, :], fac, X[:npp, g, :], ALU.mult, ALU.add, [Xg, 'tmpn'], [Xg])

        def nsplits(has_s):
            return [(0, TP)] + ([(TP, NS)] if has_s else [])

        first_wbig = [False]

        def ffn(f, gi_post, has_s):
            hkeys = [('hnT', g) for g, _, _ in groups_of(has_s)]
            for blk in range(11):
                wgs, wgk = wload(colblk(wg[f], 256 * blk, 256))
                wus, wuk = wload(colblk(wu[f], 256 * blk, 256))
                if blk == 1 and first_wbig[0]:
                    first_wbig[0] = False
                    load_wbig(0)
                for jj in range(2):
                    j = 2 * blk + jj
                    for n0, nn in nsplits(has_s):
                        bg = R2('pg')
                        bu = 2 + R2('pu')
                        pg = bank(bg)
                        pu = bank(bu)
                        for kc in range(8):
                            MM(pg[:, 0:nn], wgs[:, kc, jj * 128:(jj + 1) * 128], hnT[:, kc, n0:n0 + nn], kc == 0, kc == 7, [wgk] + hkeys, bkeys(bg))
                        for kc in range(8):
                            MM(pu[:, 0:nn], wus[:, kc, jj * 128:(jj + 1) * 128], hnT[:, kc, n0:n0 + nn], kc == 0, kc == 7, [wuk] + hkeys, bkeys(bu))
                        si = 0
                        ACT(sg[si][:, 0:nn], pg[:, 0:nn], AF.Silu, bkeys(bg), ['sg%d' % si])
                        TTo(hT[:, j, n0:n0 + nn], sg[si][:, 0:nn], pu[:, 0:nn], ALU.mult, ['sg%d' % si] + bkeys(bu), [('hT', j, n0)])
            load_gp(gi_post)
            hall = [('hT', j, n0) for j in range(NJ) for n0, _ in nsplits(has_s)]
            for g, npp, c0 in groups_of(has_s):
                pyb = (4, 0)[R2('py')]
                py = bank(pyb, 2)
                for hf in range(2):
                    for j in range(NJ):
                        MM(py[:npp, hf * 512:(hf + 1) * 512], hT[:, j, c0:c0 + npp], wbig[:, j, hf * 512:(hf + 1) * 512], j == 0, j == NJ - 1, hall + [('wbig', j // 2)], bkeys(pyb, 2))
                postnorm(py, bkeys(pyb, 2), 0.5, g, npp)
        zk = lambda nm, g: ('z', nm, g)

        def w_in(has_s):
            grp = groups_of(has_s)
            hkeys = [('hnT', g) for g, _, _ in grp]

            def fm_chunk(ws, wk_, lhs_fn, evac):
                for n0, nn in nsplits(has_s):
                    b = R2('pg')
                    p = bank(b)
                    for kc in range(8):
                        MM(p[:, 0:nn], lhs_fn(ws, kc), hnT[:, kc, n0:n0 + nn], kc == 0, kc == 7, [wk_] + extra_keys.get(wk_, []) + hkeys, bkeys(b))
                    evac(p, b, n0, nn)

            def tm_block(ws, wk_, ncols, evac):
                for g, npp, c0 in grp:
                    b = 2 + R2('pu')
                    p = bank(b)
                    for kc in range(8):
                        MM(p[:npp, 0:ncols], hnT[:, kc, c0:c0 + npp], ws[:, kc, 0:ncols], kc == 0, kc == 7, [wk_, ('hnT', g)], bkeys(b))
                    evac(p, b, g, npp)
            sc_k = 128.0 ** (-0.5)
            for blk in range(2):
                ws, wk_ = wload(colblk(win, 256 * blk, 256))
                for jj in range(2):
                    h = 2 * blk + jj
                    fm_chunk(ws, wk_, lambda w_, kc, jj=jj: w_[:, kc, jj * 128:(jj + 1) * 128], lambda p, b, n0, nn, h=h: CP(qT[:, h, n0:n0 + nn], p[:, 0:nn], bkeys(b), [('qT', n0)], eng='act'))
            if os.environ.get('WSTOP') == '1':
                return
            for blk in range(2):
                ws, wk_ = wload(colblk(win, 512 + 256 * blk, 256))
                for jj in range(2):
                    h = 2 * blk + jj
                    fm_chunk(ws, wk_, lambda w_, kc, jj=jj: w_[:, kc, jj * 128:(jj + 1) * 128], lambda p, b, n0, nn, h=h: S.op('act', lambda e: e.mul(out=kT[:, h, n0:n0 + nn], in_=p[:, 0:nn], mul=sc_k), bkeys(b), [('kT', n0)]))
                tm_block(ws, wk_, 256, lambda p, b, g, npp, blk=blk: TS(ktok[:npp, g, 256 * blk:256 * blk + 256], p[:npp, 0:256], sc_k, None, ALU.mult, None, bkeys(b), [zk('ktok', g)]))
            if os.environ.get('WSTOP') == '2':
                return
            for blk in range(2):
                ws, wk_ = wload(colblk(win, 1024 + 256 * blk, 256))

                def ev_v(p, b, g, npp, blk=blk):
                    CP(vaug[:npp, g, 2 * blk:2 * blk + 2, 0:128], p[:npp, 0:256].rearrange('p (h c) -> p h c', c=128), bkeys(b), [zk('vaug', g)])
                    if blk == 1:
                        MSET(vaug[:npp, g, :, 128:129], 1.0, [zk('vaug', g)])
                tm_block(ws, wk_, 256, ev_v)
            if os.environ.get('WSTOP') == '3':
                return
            for blk in range(2):
                ws, wk_ = wload(colblk(win, 1536 + 256 * blk, 256))
                tm_block(ws, wk_, 256, lambda p, b, g, npp, blk=blk: ACT(osig[:npp, g, 256 * blk:256 * blk + 256], p[:npp, 0:256], AF.Sigmoid, bkeys(b), [zk('osig', g)]))
            if os.environ.get('WSTOP') == '4':
                return
            ws, wk_ = wload(colparts(win, [(32 * f_, 2048, 32) for f_ in range(4)] + [(128 + 32 * f_, 2052, 32) for f_ in range(4)]))
            fm_chunk(ws, wk_, lambda w_, kc: w_[:, kc, 0:128], lambda p, b, n0, nn: CP(IGs[:, n0:n0 + nn], p[:, 0:nn], bkeys(b), [('IGs', n0)], eng='act'))
            fm_chunk(ws, wk_, lambda w_, kc: w_[:, kc, 128:256], lambda p, b, n0, nn: CP(FGs[:, n0:n0 + nn], p[:, 0:nn], bkeys(b), [('FGs', n0)], eng='act'))
            if os.environ.get('WSTOP') == '5':
                return
            for blk in range(2):
                ws, wk_ = wload(colblk(win, 2056 + 256 * blk, 256))
                for jj in range(2):
                    c = 2 * blk + jj
                    fm_chunk(ws, wk_, lambda w_, kc, jj=jj: w_[:, kc, jj * 128:(jj + 1) * 128], lambda p, b, n0, nn, c=c: S.op('act', lambda e: e.mul(out=qaT[:, c, n0:n0 + nn], in_=p[:, 0:nn], mul=0.125), bkeys(b), [('qaT', n0)]))
            if os.environ.get('WSTOP') == '6':
                return
            for hk in range(2):

                def kaf(slot, rk, key, hk=hk):
                    MSET(slot[:, :, 64:192], 0.0, [rk], eng='pool')
                    for d0 in (0, 192):
                        DMA(slot[:, :, d0:d0 + 64], win[:, 2568 + 64 * hk:2568 + 64 * hk + 64].rearrange('(kc p) c -> p kc c', p=128), (), [rk], key=key, eng='pool')
                ws, wk_ = wload(kaf)
                for par in range(2):
                    fm_chunk(ws, wk_, lambda w_, kc, par=par: w_[:, kc, par * 128:(par + 1) * 128], lambda p, b, n0, nn, v=2 * hk + par: CP(kaT[:, v, 128 + n0:128 + n0 + nn], p[:, 0:nn], bkeys(b), [('kaT', n0)], eng='act'))
            if os.environ.get('WSTOP') == '7':
                return
            ws, wk_ = wload(colblk(win, 2568, 256))

            def ev_kv(p, b, g, npp):
                if has_s and g >= 3:
                    CP(kvf[:npp, g - 3, :], p[:npp, 0:256], bkeys(b), [('kvf', g)])
                CP(va[:npp, 1 + g, :], p[:npp, 128:256], bkeys(b), [('va', g)], eng='act')
            tm_block(ws, wk_, 256, ev_kv)

        def gates(has_s, phase=2):
            rI = [('IGs', n0) for n0, _ in nsplits(has_s)]
            rF = [('FGs', n0) for n0, _ in nsplits(has_s)]
            TTn = TP + (NS if has_s else 0)
            if not neg_done[0]:
                neg_done[0] = True
                S.op('act', lambda e: e.mul(out=nbf_l[:], in_=nbf_l[:], mul=-1.0), ['nbf_l', 'bi_l', 'smin'] + lane_keys, ['nbf_l', 'bi_l', 'smin'] + lane_keys)
            ACT(e1[:, 0:TTn], FGs[:, 0:TTn], AF.Exp, rF + ['nbf_l'], ['e1'], scale=-1.0, bias=nbf_l[:, 0:1])
            ACT(lfn[:, 0:TTn], e1[:, 0:TTn], AF.Ln, ['e1'], ['lfn'], bias=1.0)
            if os.environ.get('GSTOP') == '1':
                return
            SCAN(Bneg[:, 0:TP], lfn[:, 0:TP], Bprev[:, 0:1], ALU.add, ['lfn', 'Bprev'], ['Bneg'])
            STT(Ug[:, 0:TP], IGs[:, 0:TP], bi_l[:, 0:1], Bneg[:, 0:TP], ALU.add, ALU.add, rI + ['bi_l', 'Bneg'], ['Ug'])
            CP(MU[:, 0:1], MUprev[:, 0:1], ['MUprev'], ['MU'])
            SCAN(MU[:, 1:TP + 1], Ug[:, 0:TP], MUprev[:, 0:1], ALU.max, ['Ug', 'MUprev', 'MU'], ['MU'])
            CP(Bprev[:, 0:1], Bneg[:, TP - 1:TP], ['Bneg'], ['Bprev'])
            CP(MUprev[:, 0:1], MU[:, TP:TP + 1], ['MU'], ['MUprev'])
            if os.environ.get('GSTOP') == '2':
                return
            MUn = V(MU[:, 128:129], [[128, 4], [0, 128]])
            MUp = V(MU[:, 0:1], [[128, 4], [0, 128]])
            MUc = MU[:, 1:TP + 1].rearrange('p (c t) -> p c t', t=128)
            v3 = lambda a, lo: a[lo:lo + 32, 0:TP].rearrange('p (c t) -> p c t', t=128)
            sl = lambda a, lo: bass.AP(a.tensor, a.offset + lo * a.ap[0][0], [[a.ap[0][0], 32]] + [list(x) for x in a.ap[1:]])
            TTo(v3(tg, 0), v3(Ug, 0), sl(MUn, 0), ALU.subtract, ['Ug', 'MU'], ['tg'])
            TTo(v3(tg, 32), sl(MUn, 32), sl(MUc, 32), ALU.subtract, ['MU'], ['tg'])
            TTo(v3(tg, 64), sl(MUp, 64), sl(MUc, 64), ALU.subtract, ['MU'], ['tg'])
            TTo(v3(tg, 96), v3(Bneg, 96), sl(MUc, 96), ALU.subtract, ['MU', 'Bneg'], ['tg'])
            TTo(dd[0:4, 0:4], V(MU[0:4, 0:1], [[128, 4]]), V(MU[0:4, 128:129], [[128, 4]]), ALU.subtract, ['MU'], ['dd'])
            ACT(dd[0:4, 0:4], dd[0:4, 0:4], AF.Exp, ['dd'], ['dd'])
            if has_s:
                c0 = TP
                l3 = lfn[:, c0:c0 + NS].rearrange('p (b t) -> p b t', t=4)
                B3 = Bneg[:, c0:c0 + NS].rearrange('p (b t) -> p b t', t=4)
                U3 = Ug[:, c0:c0 + NS].rearrange('p (b t) -> p b t', t=4)
                I3 = IGs[:, c0:c0 + NS].rearrange('p (b t) -> p b t', t=4)
                CP(B3[:, :, 0:1], l3[:, :, 0:1], ['lfn'], ['Bneg'])
                for t in range(1, 4):
                    TTo(B3[:, :, t:t + 1], B3[:, :, t - 1:t], l3[:, :, t:t + 1], ALU.add, ['lfn', 'Bneg'], ['Bneg'])
                STT(U3, I3, bi_l[:, 0:1], B3, ALU.add, ALU.add, rI + ['bi_l', 'Bneg'], ['Ug'])
                CP(MUs[:, :, 0:1], smin[:].rearrange('p (b o) -> p b o', o=1), ['smin'], ['MUs'])
                for t in range(4):
                    TTo(MUs[:, :, t + 1:t + 2], MUs[:, :, t:t + 1], U3[:, :, t:t + 1], ALU.max, ['Ug', 'MUs'], ['MUs'])
                MUn_s = V(MUs[:, 0, 4:5], [[5, NB], [0, 4]])
                MUp_s = V(MUs[:, 0, 0:1], [[5, NB], [0, 4]])
                MUc_s = MUs[:, :, 1:5]
                t3 = lambda lo: tg[lo:lo + 32, c0:c0 + NS].rearrange('p (b t) -> p b t', t=4)
                TTo(t3(0), U3[0:32], sl(MUn_s, 0), ALU.subtract, ['Ug', 'MUs'], ['tg'])
                TTo(t3(32), sl(MUn_s, 32), MUc_s[32:64], ALU.subtract, ['MUs'], ['tg'])
                TTo(t3(64), sl(MUp_s, 64), MUc_s[64:96], ALU.subtract, ['MUs'], ['tg'])
                TTo(t3(96), B3[96:128], MUc_s[96:128], ALU.subtract, ['MUs', 'Bneg'], ['tg'])
                TTo(dds[0:4, :], V(MUs[0:4, 0, 0:1], [[5, NB]]), V(MUs[0:4, 0, 4:5], [[5, NB]]), ALU.subtract, ['MUs'], ['dds'])
                ACT(dds[0:4, :], dds[0:4, :], AF.Exp, ['dds'], ['dds'])
                TTo(smout[:, :], V(MUs[:, 0, 4:5], [[5, NB]]), V(B3[:, 0, 3:4], [[4, NB]]), ALU.subtract, ['MUs', 'Bneg'], ['smout'])
                DMA(smo.rearrange('b h -> h b'), smout[0:4, :], ['smout'], ['o_smo'], key='o_sm', slow=True)
            if os.environ.get('GSTOP') == '3':
                return
            TS(tg[:, 0:TTn], tg[:, 0:TTn], 80.0, None, ALU.min, None, ['tg'], ['tg'])
            ACT(Fst[:, 0:TTn], tg[:, 0:TTn], AF.Exp, ['tg'], ['Fst'])
            pd = bank(1)
            for h in range(4):
                MM(pd[:, 4 * h:4 * h + 4], Esel[0:4, h, :], dd[0:4, 0:4], True, True, ['Esel', 'dd'], bkeys(1))
            CP(DECs[:].rearrange('p h c -> p (h c)'), pd[:, 0:16], bkeys(1), ['DECs'])
            if has_s:
                for h in range(4):
                    MM(pd[:, 64 + NB * h:64 + NB * h + NB], Esel[0:4, h, :], dds[0:4, :], True, True, ['Esel', 'dds'], bkeys(1))
                CP(DECss[:].rearrange('p h c -> p (h c)'), pd[:, 64:64 + 4 * NB], bkeys(1), ['DECss'])
            if os.environ.get('GSTOP') == '4':
                return
            pf = bank(7)
            for g in range(4):
                TR(pf[:, g * 128:(g + 1) * 128], Fst[:, g * 128:(g + 1) * 128], ident[:, :], ['Fst', 'ident'], bkeys(7))
            CP(FT[:, 0:4, :].rearrange('p g (f h) -> p g f h', h=4), V(pf[:, 0:1], [[128, 4], [32, 4], [1, 4]]), bkeys(7), ['FT'])
            if has_s:
                pf2 = bank(6)
                TR(pf2[0:64, 0:128], Fst[:, TP:TP + NS], ident[:, :], ['Fst', 'ident'], bkeys(6))
                CP(FT[0:64, 4, :].rearrange('p (f h) -> p f h', h=4), V(pf2[0:64, 0:1], [[32, 4], [1, 4]]), bkeys(6), ['FT'])

        def mlstm_group(g, npp, c0, c_idx, is_s):
            zq = [('qT', 0), ('qT', TP), ('kT', 0), ('kT', TP)]
            pS = bank(0).rearrange('p (h t) -> p h t', t=128)
            for h in range(4):
                MM(pS[:npp, h, 0:npp], kT[:, h, c0:c0 + npp], qT[:, h, c0:c0 + npp], True, True, zq, bkeys(0))
                yield
            mk = masks if is_s else maskp
            for h in range(4):
                STT(Sp[:npp, h, 0:npp], pS[:npp, h, 0:npp], FT[:npp, g, h:h + 1], mk[:npp, :npp], ALU.mult, ALU.mult, bkeys(0) + ['FT', 'masks', 'maskp'], ['Sp'])
                yield
            TTo(kw[:npp, :, :], ktok[:npp, g, :].rearrange('p (h c) -> p h c', c=128), V(FT[:npp, g, 0:1], [[1, 4], [0, 128]]), ALU.mult, [zk('ktok', g), 'FT'], ['kw'])
            yield
            pKV = bank(1, 2).rearrange('p (h c) -> p h c', c=256)
            pO1 = bank(3, 2).rearrange('p (h c) -> p h c', c=256)
            pO2 = bank(3, 2).rearrange('p (h c) -> p h c', c=256)
            vk = [zk('vaug', g), 'vones']
            if not is_s:
                for h in range(4):
                    MM(pKV[:, h, 0:129], kw[:, h, :], vaug[:, g, h, 0:129], True, True, ['kw'] + vk, bkeys(1, 2))
                    yield
                for h in range(4):
                    MM(pO1[:, h, 0:129], qT[:, h, c0:c0 + 128], Cbf[:, h, :], True, True, zq + ['Cbf'], bkeys(3, 2))
                    yield
            else:
                MSET(tmpo[:64], 0.0, ['tmpo'])
                yield
                for b in range(NB):
                    i = b % 2
                    TTo(qTm[:, i, :, :], qT[:, :, c0:c0 + 64], V(bmb[:, b, 0:1], [[0, 4], [1, 64]]), ALU.mult, zq + ['bmb'], [('qTm', i)])
                    yield
                    DMA(Cb[i][:, :, 0:128], sC[b].rearrange('h k v -> k h v'), (), [('Cb', i)], key='cb%d' % i)
                    yield
                    CP(Cb[i][:, :, 128:129], nTin[:, 4 * b:4 * b + 4].rearrange('p (h o) -> p h o', o=1), ['nTin'], [('Cb', i)], eng='act')
                    yield
                    CP(Cbb[i][:], Cb[i][:], [('Cb', i)], [('Cbb', i)], eng='act')
                    yield
                    for h in range(4):
                        MM(pO1[0:64, h, 0:129], qTm[:, i, h, :], Cbb[i][:, h, :], True, True, [('qTm', i), ('Cbb', i)], bkeys(3, 2))
                        yield
                    TTo(tmpo[:64], pO1[0:64, :, 0:129], tmpo[:64], ALU.add, bkeys(3, 2) + ['tmpo'], ['tmpo'])
                    yield
                    TS(kwm[:, :, :].rearrange('p h c -> p (h c)'), kw[0:64, :, :].rearrange('p h c -> p (h c)'), bmT[:, b:b + 1], None, ALU.mult, None, ['kw', 'bmT'], ['kwm'])
                    yield
                    for h in range(4):
                        MM(pKV[:, h, 0:129], kwm[:, h, :], vaug[0:64, g, h, 0:129], True, True, ['kwm'] + vk, bkeys(1, 2))
                        yield
                    TTo(Cb[i][:], Cb[i][:], V(DECss[:, 0, b:b + 1], [[NB, 4], [0, 129]]), ALU.mult, [('Cb', i), 'DECss'], [('Cb', i)])
                    yield
                    TTo(Cb[i][:], Cb[i][:], pKV[:, :, 0:129], ALU.add, [('Cb', i)] + bkeys(1, 2), [('Cb', i)])
                    yield
                    DMA(sCo[b].rearrange('h k v -> k h v'), Cb[i][:, :, 0:128], [('Cb', i)], ['o_sC%d' % i], key='o_sC%d' % i)
                    yield
                    CP(nTout[:, 4 * b:4 * b + 4].rearrange('p (h o) -> p h o', o=1), Cb[i][:, :, 128:129], [('Cb', i)], ['nTout'], eng='act')
                    yield
            if is_s:
                TTo(tmpo[:npp], tmpo[:npp], V(FT[:npp, g, 8:9], [[1, 4], [0, 129]]), ALU.mult, ['tmpo', 'FT'], ['tmpo'])
                yield
            else:
                TTo(tmpo[:npp], pO1[:npp, :, 0:129], V(FT[:npp, g, 8:9], [[1, 4], [0, 129]]), ALU.mult, bkeys(3, 2) + ['FT'], ['tmpo'])
                yield
            for h in range(4):
                MM(pO2[:npp, h, 0:129], Sp[:npp, h, 0:npp], vaug[:npp, g, h, 0:129], True, True, ['Sp'] + vk, bkeys(3, 2))
                yield
            for h in range(4):
                STT(ND[:npp, h, :], pO2[:npp, h, 0:129], FT[:npp, g, 4 + h:5 + h], tmpo[:npp, h, :], ALU.mult, ALU.add, bkeys(3, 2) + ['FT', 'tmpo'], ['ND'])
                yield
            TS(q5[:npp, 0, :], ND[:npp, :, 128], -1.0, None, ALU.mult, None, ['ND'], ['q5'])
            yield
            TTo(q5[:npp, 0, :], q5[:npp, 0, :], ND[:npp, :, 128], ALU.max, ['ND', 'q5'], ['q5'])
            yield
            TTo(q5[:npp, 0, :], q5[:npp, 0, :], FT[:npp, g, 12:16], ALU.max, ['q5', 'FT'], ['q5'])
            yield
            RCP(q5[:npp, 1, :], q5[:npp, 0, :], ['q5'], ['q5'])
            yield
            MSET(q5[:npp, 2, :], 0.0, ['q5'])
            yield
            for h in range(4):
                ACT(kw[:npp, h, :], ND[:npp, h, 0:128], AF.Square, ['ND'], ['kw', 'q5'], accum_out=q5[:npp, 2, h:h + 1])
                yield
            TTo(q5[:npp, 3, :], q5[:npp, 1, :], q5[:npp, 1, :], ALU.mult, ['q5'], ['q5'])
            yield
            TTo(q5[:npp, 4, :], q5[:npp, 3, :], q5[:npp, 2, :], ALU.mult, ['q5'], ['q5'])
            yield
            ACT(q5[:npp, 5, :], q5[:npp, 4, :], AF.Sqrt, ['q5'], ['q5'], scale=1.0 / 128, bias=EPS)
            yield
            RCP(q5[:npp, 6, :], q5[:npp, 5, :], ['q5'], ['q5'])
            yield
            TTo(q5[:npp, 7, :], q5[:npp, 6, :], q5[:npp, 1, :], ALU.mult, ['q5'], ['q5'])
            yield
            TTo(og[:npp, :], osig[:npp, g, :], mngb[:npp, :], ALU.mult, [zk('osig', g), 'mngb'], ['og'])
            yield
            TTo(t5[:npp, :, 0:128], ND[:npp, :, 0:128], V(q5[:npp, 7, 0:1], [[1, 4], [0, 128]]), ALU.mult, ['ND', 'q5', 'tmpo'], ['tmpo'])
            yield
            TTo(U[:npp, g, 0:512].rearrange('p (h c) -> p h c', c=128), t5[:npp, :, 0:128], og[:npp, :].rearrange('p (h c) -> p h c', c=128), ALU.mult, ['tmpo', 'og'], [('U', g, 0)])
            yield
            if not is_s:
                TTo(Cst[:], Cst[:], V(DECs[:, 0, c_idx:c_idx + 1], [[4, 4], [0, 129]]), ALU.mult, ['Cst', 'DECs'], ['Cst'])
                yield
                TTo(Cst[:], Cst[:], pKV[:, :, 0:129], ALU.add, ['Cst'] + bkeys(1, 2), ['Cst'])
                yield
                CP(Cbf[:], Cst[:], ['Cst'], ['Cbf'], eng='act')
                yield

        def softmax_tail(Sx, Px, npp, nk, R_, half):
            sk_ = sinkb[:npp, 4 * half:4 * half + 4]
            RED(sst[:npp, 0, 0:4], Sx, ALU.max, R_, ['sst'])
            yield
            TTo(sst[:npp, 1, 0:4], sst[:npp, 0, 0:4], sk_, ALU.max, ['sst', 'sinkb'], ['sst'])
            yield
            TS(sst[:npp, 7, 0:4], sst[:npp, 1, 0:4], -1.0, None, ALU.mult, None, ['sst'], ['sst'])
            yield
            MSET(sst[:npp, 2, 0:4], 0.0, ['sst'])
            yield
            for hh_ in range(4):
                ACT(Px[:, hh_, :], Sx[:, hh_, :], AF.Exp, R_ + ['sst'], ['Px', 'sst'], bias=sst[:npp, 7, hh_:hh_ + 1], accum_out=sst[:npp, 2, hh_:hh_ + 1])
                yield
            TTo(sst[:npp, 3, 0:4], sk_, sst[:npp, 1, 0:4], ALU.subtract, ['sst', 'sinkb'], ['sst'])
            yield
            ACT(sst[:npp, 4, 0:4], sst[:npp, 3, 0:4], AF.Exp, ['sst'], ['sst'])
            yield
            TTo(sst[:npp, 5, 0:4], sst[:npp, 4, 0:4], sst[:npp, 2, 0:4], ALU.add, ['sst'], ['sst'])
            yield
            RCP(sst[:npp, 6, 0:4], sst[:npp, 5, 0:4], ['sst'], ['sst'])
            yield
            return sst[:npp, 6, 0:4]

        def swa_group(g, first):
            zq = [('qaT', 0), ('kaT', 0), 'kaT_h']
            c0 = g * 128
            vkeys = ['va0', ('va', g)] + ([('va', g - 1)] if g > 0 else [])
            for half in range(2):
                pS = bank(5, 2).rearrange('p (h k) -> p h k', k=256)
                for hh in range(4):
                    h = 4 * half + hh
                    MM(pS[:, hh, :], qaT[:, h // 2, c0:c0 + 128], kaT[:, 2 * half + h % 2, c0:c0 + 256], True, True, zq, bkeys(5, 2))
                    yield
                for hh in range(4):
                    h = 4 * half + hh
                    STT(Ssb[:, hh, :], biasT[:, 0, :], slopeb[:, h:h + 1], pS[:, hh, :], ALU.mult, ALU.add, bkeys(5, 2) + ['biasT', 'slopeb'], ['Ssb'])
                    yield
                if first:
                    TS(Ssb[:, :, 0:128], Ssb[:, :, 0:128], role[:, 16:17], None, ALU.add, None, ['Ssb', 'role'], ['Ssb'])
                    yield
                if os.environ.get('SSTOP') == '1':
                    continue
                rden = (yield from softmax_tail(Ssb[:], Pb[:], 128, 256, ['Ssb'], half))
                if os.environ.get('SSTOP') == '2':
                    continue
                pb_ = 7
                PT = bank(pb_).bitcast(BF16).rearrange('p (h t) -> p h t', t=128)
                for hh in range(4):
                    for kt in range(2):
                        TR(PT[:, 2 * hh + kt, :], Pb[:, hh, kt * 128:(kt + 1) * 128], identb[:, :], ['Px', 'identb'], bkeys(pb_))
                        yield
                CP(PTs[:, 0:4, :], PT[:, 0:4, :], bkeys(pb_), ['PTs'], eng='act')
                yield
                CP(PTs[:, 4:8, :], PT[:, 4:8, :], bkeys(pb_), ['PTs'], eng='act')
                yield
                if os.environ.get('SSTOP') == '3':
                    continue
                pO = bank(7).rearrange('p (h c) -> p h c', c=64)
                for hh in range(4):
                    for kt in range(2):
                        MM(pO[:, hh, :], PTs[:, 2 * hh + kt, :], va[:, g + kt, half * 64:(half + 1) * 64], kt == 0, kt == 1, ['PTs'] + vkeys, bkeys(7))
                        yield
                TTo(U[:, g, 512 + 256 * half:768 + 256 * half].rearrange('p (h c) -> p h c', c=64), pO[:, 0:4, :], V(rden[:, 0:1], [[1, 4], [0, 64]]), ALU.mult, bkeys(7) + ['sst'], [('U', g, 1 + half)])
                yield

        def swa_sample():
            zq = [('qaT', TP), ('kaT', TP)]
            Ss = Ssb[0:4, 0:3, :].rearrange('p a k -> p (a k)').rearrange('p (h k) -> p h k', k=192)
            Ps = Pb[0:4, 0:3, :].rearrange('p a k -> p (a k)').rearrange('p (h k) -> p h k', k=192)
            for b in range(NB):
                i = b % 2
                for v in range(4):
                    hk, par = (v // 2, v % 2)
                    DMA(ckb[i][:, v, 64 * par:64 * par + 64], ck[b][:, 64 * hk:64 * hk + 64], [('ckb', i)], [('ckb', i, v)], key='ck%d_%d' % (i, v), eng='pool')
                    yield
                DMA(cvb[i][:], cv[b], (), [('cvb', i)], key='cv%d' % i, eng='pool')
                yield
                pk_ = bank(7)
                for v in range(4):
                    MM(pk_[:, v * 128:(v + 1) * 128], ckb[i][:, v, :], identb[:, :], True, True, [('ckb', i), ('ckb', i, v), 'identb'], bkeys(7))
                    yield
                CP(kTc[i][:].rearrange('p h t -> p (h t)'), pk_[:, 0:512], bkeys(7), [('kTc', i)], eng='act')
                yield
                for half in range(2):
                    pSc = bank(5).rearrange('p (h k) -> p h k', k=128)
                    pSn = bank(6).rearrange('p (h k) -> p h k', k=64)
                    for hh in range(4):
                        h = 4 * half + hh
                        q_ = qaT[:, h // 2, TP + 4 * b:TP + 4 * b + 4]
                        MM(pSc[0:4, hh, :], q_, kTc[i][:, 2 * half + h % 2, :], True, True, zq + [('kTc', i)], bkeys(5))
                        yield
                        MM(pSn[0:4, hh, :], q_, kaT[:, 2 * half + h % 2, 128 + TP:128 + TP + 64], True, True, zq, bkeys(6))
                        yield
                    for hh in range(4):
                        h = 4 * half + hh
                        STT(Ss[0:4, hh, 0:128], bsc[0:4, 0, :], slopeb[0:4, h:h + 1], pSc[0:4, hh, :], ALU.mult, ALU.add, bkeys(5) + ['bsc', 'slopeb'], ['Ssb'])
                        yield
                        STT(Ss[0:4, hh, 128:192], tbl[0:4, 0, 60 - 4 * b:124 - 4 * b], slopeb[0:4, h:h + 1], pSn[0:4, hh, :], ALU.mult, ALU.add, bkeys(6) + ['tbl', 'slopeb'], ['Ssb'])
                        yield
                    rden = (yield from softmax_tail(Ss, Ps, 4, 192, ['Ssb'], half))
                    PT = bank(6).bitcast(BF16)[:, 0:32].rearrange('p (h k q) -> p h k q', k=2, q=4)
                    for hh in range(4):
                        TR(PT[:, hh, 0, :], Ps[0:4, hh, 0:128], identb[0:4, 0:4], ['Px', 'identb'], bkeys(6))
                        yield
                        TR(PT[0:64, hh, 1, :], Ps[0:4, hh, 128:192], identb[0:4, 0:4], ['Px', 'identb'], bkeys(6))
                        yield
                    CP(PTss[:, 0:4, 0, :], PT[:, :, 0, :], bkeys(6), ['PTss'], eng='act')
                    yield
                    CP(PTss[0:64, 0:4, 1, :], PT[0:64, :, 1, :], bkeys(6), ['PTss'])
                    yield
                    pO = bank(7).rearrange('p (h c) -> p h c', c=64)
                    for hh in range(4):
                        MM(pO[0:4, hh, :], PTss[:, hh, 0, :], cvb[i][:, half * 64:(half + 1) * 64], True, False, ['PTss', ('cvb', i)], bkeys(7))
                        yield
                        MM(pO[0:4, hh, :], PTss[0:64, hh, 1, :], va[0:64, 5, half * 64:(half + 1) * 64], False, True, ['PTss', ('va', 4)], bkeys(7))
                        yield
                    TTo(uab[i][0:4, 256 * half:256 * half + 256].rearrange('p (h c) -> p h c', c=64), pO[0:4, 0:4, :], V(rden[:, 0:1], [[1, 4], [0, 64]]), ALU.mult, bkeys(7) + ['sst'], [('uab', i)])
                    yield
                DMA(U[4 * b:4 * b + 4, 4, 512:1024], uab[i][0:4, :], [('uab', i)], [('U', 4, 1, b)], key='uab%d' % i)
                yield
                DMA(sk[b, 0:124, :], ck[b, 4:128, :], (), [('o_sk', b)], key='o_sk')
                yield
                DMA(sv[b, 0:124, :], cv[b, 4:128, :], (), [('o_sv', b)], key='o_sv')
                yield

        def w_out_stage(has_s):
            grp = groups_of(has_s)
            for g, npp, c0 in grp:
                b = 6 + R2('pT')
                pT = bank(b).bitcast(BF16).rearrange('p (c t) -> p c t', t=128)
                for c in range(8):
                    TR(pT[:, c, 0:npp], U[:npp, g, c * 128:(c + 1) * 128], identb[:npp, :npp], [('U', g, 0), ('U', g, 1), ('U', g, 2), 'identb'] + [('U', 4, 1, bb) for bb in range(NB)], bkeys(b))
                CP(hnT[:, :, c0:c0 + npp], pT[:, :, 0:npp], bkeys(b), [('hnT', g)], eng='act')
            load_gp(3)
            slots = [wload(colblk(wout, 256 * blk, 256)) for blk in range(4)]
            for g, npp, c0 in grp:
                pyb = (4, 0)[R2('py')]
                py = bank(pyb, 2)
                for blk in range(4):
                    ws, wk_ = slots[blk]
                    for kc in range(8):
                        MM(py[:npp, 256 * blk:256 * blk + 256], hnT[:, kc, c0:c0 + npp], ws[:, kc, 0:256], kc == 0, kc == 7, [wk_, ('hnT', g)], bkeys(pyb, 2))
                postnorm(py, bkeys(pyb, 2), 1.0, g, npp)
        ZW = o_[0]
        ZSET = {'qT', 'kT', 'qaT', 'kaT', 'va', 'z', 'vones', 'kaT_h', 'va0', 'hT'}

        def zkeys():
            return [k for k in S.last_writer if (k[0] if isinstance(k, tuple) else k) in ZSET]

        def mlstm_state_only(g, c_idx):
            TTo(kw[:, :, :], ktok[:, g, :].rearrange('p (h c) -> p h c', c=128), V(FT[:, g, 0:1], [[1, 4], [0, 128]]), ALU.mult, [zk('ktok', g), 'FT'], ['kw'])
            pKV = bank(2, 2).rearrange('p (h c) -> p h c', c=256)
            for h in range(4):
                MM(pKV[:, h, 0:129], kw[:, h, :], vaug[:, g, h, 0:129], True, True, ['kw', zk('vaug', g), 'vones'], bkeys(2, 2))
            TTo(Cst[:], Cst[:], V(DECs[:, 0, c_idx:c_idx + 1], [[4, 4], [0, 129]]), ALU.mult, ['Cst', 'DECs'], ['Cst'])
            TTo(Cst[:], Cst[:], pKV[:, :, 0:129], ALU.add, ['Cst'] + bkeys(2, 2), ['Cst'])
        first_wbig = [True]
        prenorm(0, nt == 1 and sample)
        for t in range(nt):
            has_s = t == nt - 1 and sample
            ffn(0, 1, has_s)
            load_wbig(0 if t + 1 < nt else 1)
            prenorm(2, has_s)
            w_in(has_s)
            DMA(gsI[t], IGs[:], ['IGs'], ['gsI'], key='sp_gI')
            DMA(gsF[t], FGs[:], ['FGs'], ['gsF'], key='sp_gF')
            DMA(x1s[t], X[:].rearrange('p g d -> p (g d)'), [('X', g) for g in range(5)], ['x1s'], key='sp_x')
            DMA(zs[t, :, 0:ZW], big[:, 0:ZW], zkeys(), ['zs'], key='sp_z')
            if t + 1 < nt:
                nhs = t + 1 == nt - 1 and sample
                DMA(X[:, 0:4, :], xp[(t + 1) * TP:(t + 2) * TP, :].rearrange('(g p) d -> p g d', p=128), (), [('X', g) for g in range(4)], key='x_in')
                if nhs:
                    DMA(X[0:64, 4, :], xs[:, :], (), [('X', 4)], key='x_in_s')
                prenorm(0, nhs)
            gates(False, phase=1)
            for g in range(4):
                mlstm_state_only(g, g)
            if has_s:
                DMA(pk[:, :], kvf[:, 0, 0:128], [('kvf', 3)], ['o_pk'], key='o_pkv')
                DMA(pv[:, :], kvf[:, 0, 128:256], [('kvf', 3)], ['o_pv'], key='o_pkv')
                for b in range(NB):
                    DMA(sk[b, 124:128, :], kvf[4 * b:4 * b + 4, 1, 0:128], [('kvf', 4)], [('o_sk2', b)], key='o_pkv')
                    DMA(sv[b, 124:128, :], kvf[4 * b:4 * b + 4, 1, 128:256], [('kvf', 4)], [('o_sv2', b)], key='o_pkv')
        pay = tmpn[:, 0:PW]
        CP(pay[:, 0:516], Cst[:].rearrange('p h c -> p (h c)'), ['Cst'], ['tmpn'])
        MSET(pay[:, 518:520], 0.0, ['tmpn'])
        CP(pay[:, 516:517], MUprev[:, 0:1], ['MUprev'], ['tmpn'])
        CP(pay[:, 517:518], Bprev[:, 0:1], ['Bprev'], ['tmpn'])
        CP(pay[:, 520:776].bitcast(BF16).rearrange('p (v t) -> p v t', t=128), kaT[:, :, TP:TP + 128], zkeys(), ['tmpn'])
        CP(pay[:, 776:840].bitcast(BF16), va[:, 4, :], zkeys(), ['tmpn'])
        DMA(exin[:, :], pay, ['tmpn'], ['exin'], key='ex_in')

        def is_tail(k):
            return k == 'vones' or (isinstance(k, tuple) and k[0] == 'z' and k[1] in ('ktok', 'vaug'))

        def reload_head(t):
            hk_ = [k for k in zkeys() if not is_tail(k)]
            isqk = lambda k: isinstance(k, tuple) and k[0] in ('qT', 'kT')
            isht = lambda k: isinstance(k, tuple) and k[0] == 'hT'
            QK = 8 * TT_
            DMA(big[:, 0:QK], zs[t, :, 0:QK], ['zs'], [k for k in hk_ if isqk(k) or isht(k)], key='rl_z')
            DMA(big[:, QK:ZT], zs[t, :, QK:ZT], ['zs'], [k for k in hk_ if not isqk(k)], key='rl_z2')

        def reload_tail(t):
            DMA(big[:, ZT:ZW], zs[t, :, ZT:ZW], ['zs'], [k for k in zkeys() if is_tail(k)] + ['kw'], key='rl_zt')

        def reload_x(t):
            DMA(X[:].rearrange('p g d -> p (g d)'), x1s[t], ['x1s'], [('X', g) for g in range(5)], key='rl_x')

        def reload(t):
            reload_head(t)
            reload_tail(t)
            reload_x(t)

        def reload_g(t):
            DMA(IGs[:], gsI[t], ['gsI'], ['IGs'], key='rl_gI')
            DMA(FGs[:], gsF[t], ['gsF'], ['FGs'], key='rl_gF')
        reload(0)
        reload_g(0)
        S.op('pool', lambda e: e.collective_compute('AllGather', ALU.bypass, replica_groups=[[0, 1, 2, 3], [4, 5, 6, 7]], ins=[exin.ap().opt()], outs=[exout.ap().opt()]), ['exin'], ['exout'], dma_key='cc', inc=1)
        for g_ in (1, 2, 3):
            drain(swa_group(g_, False))
        MSET(Cst[:], 0.0, ['Cst'])
        MSET(cmb[:, 0:1], 0.0, ['cmb'])
        MSET(kaTh[:], 0.0, ['kaTh'])
        MSET(vah[:], 0.0, ['vah'])
        payr = Ssb[:].rearrange('p h k -> p (h k)')[:, 0:PW]
        for r in range(3):
            DMA(payr, exout[r * 128:(r + 1) * 128, :], ['exout'], ['Ssb'], key='ex_rd')
            mk_ = role[:, r:r + 1]
            TS(cmb[:, 1:2], payr[:, 517:518], mk_, None, ALU.mult, None, ['Ssb', 'role'], ['cmb'])
            TTo(cmb[:, 2:3], payr[:, 516:517], payr[:, 517:518], ALU.subtract, ['Ssb'], ['cmb'])
            TS(cmb[:, 2:3], cmb[:, 2:3], -NEG, mk_, ALU.add, ALU.mult, ['cmb', 'role'], ['cmb'])
            TS(cmb[:, 2:3], cmb[:, 2:3], NEG, None, ALU.add, None, ['cmb'], ['cmb'])
            TTo(cmb[:, 3:4], cmb[:, 0:1], cmb[:, 1:2], ALU.subtract, ['cmb'], ['cmb'])
            TTo(cmb[:, 4:5], cmb[:, 3:4], cmb[:, 2:3], ALU.max, ['cmb'], ['cmb'])
            TTo(cmb[:, 5:6], cmb[:, 3:4], cmb[:, 4:5], ALU.subtract, ['cmb'], ['cmb'])
            TTo(cmb[:, 6:7], cmb[:, 2:3], cmb[:, 4:5], ALU.subtract, ['cmb'], ['cmb'])
            ACT(cmb[:, 7:9], cmb[:, 5:7], AF.Exp, ['cmb'], ['cmb'])
            TS(cmb[:, 8:9], cmb[:, 8:9], mk_, None, ALU.mult, None, ['cmb', 'role'], ['cmb'])
            CP(cmb[:, 0:1], cmb[:, 4:5], ['cmb'], ['cmb'])
            pd = bank(1)
            for h in range(4):
                MM(pd[:, 2 * h:2 * h + 2], Esel[0:4, h, :], cmb[0:4, 7:9], True, True, ['Esel', 'cmb'], bkeys(1))
            CP(ABt[:].rearrange('p h c -> p (h c)'), pd[:, 0:8], bkeys(1), ['ABt'])
            TTo(Cst[:], Cst[:], V(ABt[:, 0, 0:1], [[2, 4], [0, 129]]), ALU.mult, ['Cst', 'ABt'], ['Cst'])
            TTo(tmpo[:], payr[:, 0:516].rearrange('p (h c) -> p h c', c=129), V(ABt[:, 0, 1:2], [[2, 4], [0, 129]]), ALU.mult, ['Ssb', 'ABt'], ['tmpo'])
            TTo(Cst[:], Cst[:], tmpo[:], ALU.add, ['Cst', 'tmpo'], ['Cst'])
            STT(kaTh[:].rearrange('p v t -> p (v t)'), payr[:, 520:776].bitcast(BF16), role[:, 8 + r:9 + r], kaTh[:].rearrange('p v t -> p (v t)'), ALU.mult, ALU.add, ['Ssb', 'role', 'kaTh'], ['kaTh'])
            STT(vah[:], payr[:, 776:840].bitcast(BF16), role[:, 8 + r:9 + r], vah[:], ALU.mult, ALU.add, ['Ssb', 'role', 'vah'], ['vah'])
        CP(MUprev[:, 0:1], cmb[:, 0:1], ['cmb'], ['MUprev'])
        MSET(Bprev[:], 0.0, ['Bprev'])
        CP(Cbf[:], Cst[:], ['Cst'], ['Cbf'], eng='act')
        for t in range(nt):
            has_s = t == nt - 1 and sample
            grp = groups_of(has_s)
            if t > 0:
                reload_x(t)
            CP(kaT[:, :, 0:128], kaTh[:], ['kaTh'], ['kaT_h'], eng='act')
            CP(va[:, 0, :], vah[:], ['vah'], ['va0'], eng='act')
            if t == 0:
                gates(has_s, phase=2)
            for g, npp, c0 in grp:
                gens_ = [mlstm_group(g, npp, c0, g, g == 4)]
                if g < 4:
                    if t > 0 or g == 0:
                        gens_.append(swa_group(g, t == 0 and g == 0))
                else:
                    gens_.append(swa_sample())
                run_rr(gens_, [2, 3] if len(gens_) == 2 and g < 4 else None)
            CP(kaTh[:], kaT[:, :, TP:TP + 128], [('kaT', 0)], ['kaTh'], eng='act')
            CP(vah[:], va[:, 4, :], [('va', 3)], ['vah'], eng='act')
            if t + 1 < nt:
                reload_tail(t + 1)
                reload_g(t + 1)
                gates(t + 1 == nt - 1 and sample, phase=2)
            w_out_stage(has_s)
            prenorm(4, has_s)
            ffn(1, 5, has_s)
            if t + 1 < nt:
                load_wbig(1)
                reload_head(t + 1)
            DMA(yp[t * TP:(t + 1) * TP, :].rearrange('(g p) d -> p g d', p=128), X[:, 0:4, :], [('X', g) for g in range(4)], ['o_yp'], key='o_y')
            if has_s:
                DMA(ys[:, :], X[0:64, 4, :], [('X', 4)], ['o_ys'], key='o_y')
        DMA(pC.rearrange('h k v -> k h v'), Cst[:, :, 0:128], ['Cst'], ['o_pC'], key='o_fin')
        TR(bank(1)[0:4, 0:128], Cst[:, :, 128], ident[:, :], ['Cst', 'ident'], bkeys(1))
        CP(pn_sb[:], bank(1)[0:4, 0:128], bkeys(1), ['pn_sb'])
        DMA(pn[:, :], pn_sb[:], ['pn_sb'], ['o_pn'], key='o_fin')
        TTo(pm_sb[:], MUprev[:], Bprev[:], ALU.subtract, ['MUprev', 'Bprev'], ['pm_sb'])
        DMA(pm[:, :], pm_sb[0:4, :], ['pm_sb'], ['o_pm'], key='o_fin')
        TR(bank(1)[0:64, 128:256], nTout[:, :], ident[:, :], ['nTout', 'ident'], bkeys(1))
        CP(snout[:], bank(1)[0:64, 128:256], bkeys(1), ['snout'])
        DMA(sno[:, :], snout[:], ['snout'], ['o_sno'], key='o_fin')
        out_keys = [k for k in S.dma_counts if k.startswith('o_')]
        S.emit(final_wait_keys=out_keys)
    return nc

def _consts():
    c = {}
    c['c_ident'] = np.eye(128, dtype=np.float32)
    s = np.arange(128)
    c['c_maskp'] = (s[:, None] <= s[None, :]).astype(np.float32)
    s = np.arange(64)
    c['c_masks'] = ((s[:, None] <= s[None, :]) & (s[:, None] // 4 == s[None, :] // 4)).astype(np.float32)
    slopes = np.exp2(-8.0 * np.arange(1, 9, dtype=np.float32) / 8).astype(np.float32)
    c['c_slope'] = np.broadcast_to(slopes[None, :], (128, 8)).copy()
    BIGN = -8000000.0
    qi = np.arange(128)[:, None]
    kj = np.arange(256)[None, :]
    dist = 128 + qi - kj
    valid = (dist >= 0) & (dist < 128)
    c['c_bias'] = np.where(valid, -dist.astype(np.float32), BIGN).astype(np.float32)
    t = np.arange(4)[:, None]
    j = np.arange(128)[None, :]
    d = 128 + t - j
    v = (d >= 0) & (d < 128)
    c['c_bsc'] = np.where(v, -d.astype(np.float32), BIGN).astype(np.float32)
    x = np.arange(124)[None, :] - 60
    d2 = t - x
    v2 = (x >= 0) & (x <= t)
    c['c_tb'] = np.where(v2, -d2.astype(np.float32), BIGN).astype(np.float32)
    bm = (np.arange(64)[None, :] // 4 == np.arange(NB)[:, None]).astype(np.float32)
    c['c_bm'] = np.broadcast_to(bm.reshape(1, NB * 64), (128, NB * 64)).copy()
    c['c_bmT'] = bm.T.copy()
    E = np.zeros((4, 4, 128), np.float32)
    for h in range(4):
        E[h, h, :] = 1.0
    c['c_E'] = E.reshape(4, 4 * 128)
    return c
_NC = None

def kernel(x_prompt, x_sample, cache_swa_k, cache_swa_v, state_mlstm_C, state_mlstm_n, state_mlstm_m, norm_gains, ffn_w_gate, ffn_w_up, ffn_w_down, w_in, b_gate, mlstm_norm_gain, attn_sinks, w_out):
    global _NC
    f = lambda a: np.ascontiguousarray(np.asarray(a, dtype=np.float32))
    x_prompt, x_sample = (f(x_prompt), f(x_sample))
    ckk, cvv = (f(cache_swa_k)[0], f(cache_swa_v)[0])
    sCC, snn, smm = (f(state_mlstm_C)[0], f(state_mlstm_n)[0], f(state_mlstm_m)[0])
    consts = _consts()
    shared = dict(gains=f(norm_gains)[0], wg=f(ffn_w_gate)[0], wu=f(ffn_w_up)[0], wd=f(ffn_w_down)[0], win=f(w_in)[0], bgate=f(b_gate)[0], mng=f(mlstm_norm_gain)[0], sinks=f(attn_sinks)[0], wout=f(w_out)[0])
    shared.update(consts)
    in_maps = []
    for c in range(8):
        m = dict(shared)
        m['xp'] = np.ascontiguousarray(x_prompt[c // 4, SEQ * (c % 4):SEQ * (c % 4 + 1)])
        role = np.zeros((128, 17), np.float32)
        for r in range(4):
            if r < c % 4:
                role[:, r] = 1.0
            if r == c % 4 - 1:
                role[:, 8 + r] = 1.0
        role[:, 16] = NEG if c % 4 == 0 else 0.0
        m['c_role'] = role
        b0 = NB * c
        m['xs'] = x_sample[b0:b0 + NB].reshape(NS, D)
        m['ck'] = ckk[b0:b0 + NB].reshape(NB, 128, 128)
        m['cv'] = cvv[b0:b0 + NB].reshape(NB, 128, 128)
        m['sC'] = sCC[b0:b0 + NB]
        m['sn'] = snn[b0:b0 + NB].reshape(NB * 4, 128)
        m['sm'] = smm[b0:b0 + NB]
        in_maps.append(m)
    if _NC is None:
        _NC = build_program()
    res = run_bass_kernel_spmd(_NC, in_maps, core_ids=list(range(8)))
    r = res.results
    yp = np.stack([np.concatenate([r[4 * b + j]['yp'] for j in range(4)], 0) for b in range(2)], 0)
    ys = np.concatenate([r[c]['ys'].reshape(NB, 4, D) for c in range(8)], 0)
    pk = np.stack([r[3]['pk'], r[7]['pk']], 0).reshape(1, 2, 128, 2, 64)
    pv = np.stack([r[3]['pv'], r[7]['pv']], 0).reshape(1, 2, 128, 2, 64)
    pC = np.stack([r[3]['pC'], r[7]['pC']], 0)[None]
    pn = np.stack([r[3]['pn'], r[7]['pn']], 0)[None]
    pm = np.stack([r[3]['pm'].reshape(4), r[7]['pm'].reshape(4)], 0)[None]
    sk = np.concatenate([r[c]['sk'] for c in range(8)], 0).reshape(1, 128, 128, 2, 64)
    sv = np.concatenate([r[c]['sv'] for c in range(8)], 0).reshape(1, 128, 128, 2, 64)
    sCo = np.concatenate([r[c]['sCo'] for c in range(8)], 0)[None]
    sno = np.concatenate([r[c]['sno'].reshape(NB, 4, 128) for c in range(8)], 0)[None]
    smo = np.concatenate([r[c]['smo'] for c in range(8)], 0)[None]
    outs = (yp, ys, pk, pv, pC, pn, pm, sk, sv, sCo, sno, smo)
    return tuple((np.ascontiguousarray(o, dtype=np.float32) for o in outs))
```

```python
import contextlib
import os
import numpy as np
import concourse.bass as bass
import concourse.mybir as mybir
from concourse.bass_utils import run_bass_kernel_spmd
F32 = mybir.dt.float32
BF16 = mybir.dt.bfloat16
ALU = mybir.AluOpType
AF = mybir.ActivationFunctionType
AX = mybir.AxisListType
ENGS = ('pe', 'act', 'dve', 'pool', 'sp')
D = 1024
DFF = 2816
NJ = 22
DIN = 2824
SEQ = 2048
NTILE = 4
PW = 840
TP = 512
NS = 64
NB = 16
EPS = 1e-06
NEG = -30000.0

class _Op:
    __slots__ = ('eng', 'fn', 'deps', 'dma_key', 'dma_cnt', 'signal', 'sig_val', 'idx', 'inc')

class Sched:
    def __init__(self, nc):
        self.nc = nc
        self.ops = []
        self.last_writer = {}
        self.readers = {}
        self.dma_counts = {}
    ALIAS = {'e1': 'FGs', 'lfn': 'FGs', 'Fst': 'tg', 'Ug': 'IGs', 't5': 'tmpo'}

    def _norm(self, k):
        if isinstance(k, tuple) and k[0] in ('IGs', 'FGs'):
            k = k[0]
        return self.ALIAS.get(k, k) if not isinstance(k, tuple) else k

    def op(self, eng, fn, reads=(), writes=(), dma_key=None, inc=16):
        reads = [self._norm(k) for k in reads]
        writes = [self._norm(k) for k in writes]
        writes = writes + [k for k in reads if isinstance(k, tuple) and k[0] == 'bank']
        reads = [k for k in reads if not (isinstance(k, tuple) and k[0] == 'bank')]
        o = _Op()
        o.eng, o.fn, o.idx, o.dma_key, o.inc = (eng, fn, len(self.ops), dma_key, inc)
        o.signal, o.sig_val = (False, None)
        deps = set()
        for r in reads:
            w = self.last_writer.get(r)
            if w is not None:
                deps.add(w)
        for r in writes:
            w = self.last_writer.get(r)
            if w is not None:
                deps.add(w)
            deps.update(self.readers.get(r, ()))
        o.deps = deps
        if dma_key is not None:
            self.dma_counts[dma_key] = self.dma_counts.get(dma_key, 0) + inc
            o.dma_cnt = self.dma_counts[dma_key]
        else:
            o.dma_cnt = None
        self.ops.append(o)
        for r in reads:
            self.readers.setdefault(r, []).append(o.idx)
        for r in writes:
            self.last_writer[r] = o.idx
            self.readers[r] = []
        return o.idx

    def emit(self, final_wait_keys=()):
        nc, ops = (self.nc, self.ops)
        for o in ops:
            nd = set()
            for d in o.deps:
                p = ops[d]
                if p.dma_key is None and o.dma_key is None and (p.eng == o.eng == 'pe'):
                    continue
                nd.add(d)
            o.deps = nd
            for d in nd:
                if ops[d].dma_key is None:
                    ops[d].signal = True
        cnt = {e: 0 for e in ENGS}
        for o in ops:
            if o.dma_key is None and o.signal:
                cnt[o.eng] += 1
                o.sig_val = cnt[o.eng]
        with contextlib.ExitStack() as st:
            esem = {e: st.enter_context(nc.semaphore('s_' + e)) for e in ENGS}
            dsem = {}
            for i, k in enumerate(self.dma_counts):
                dsem[k] = st.enter_context(nc.semaphore('d_%d' % i))
            block = st.enter_context(nc.Block())

            def run(ename):

                def body(eng):
                    waited = {}
                    for o in ops:
                        if o.eng != ename:
                            continue
                        need = {}
                        for d in o.deps:
                            p = ops[d]
                            if p.dma_key is not None:
                                s, v = (dsem[p.dma_key], p.dma_cnt)
                            else:
                                s, v = (esem[p.eng], p.sig_val)
                            if need.get(id(s), (None, 0))[1] < v:
                                need[id(s)] = (s, v)
                        for key, (s, v) in need.items():
                            if waited.get(key, 0) < v:
                                eng.wait_ge(s, v)
                                waited[key] = v
                        ins = o.fn(eng)
                        if o.dma_key is not None:
                            ins.then_inc(dsem[o.dma_key], o.inc)
                        elif o.signal:
                            ins.then_inc(esem[ename], 1)
                    if ename == 'sp':
                        for k in final_wait_keys:
                            eng.wait_ge(dsem[k], self.dma_counts[k])
                return body
            block.tensor(run('pe'))
            block.scalar(run('act'))
            block.vector(run('dve'))
            block.gpsimd(run('pool'))
            block.sync(run('sp'))

def drain(gen):
    try:
        while True:
            next(gen)
    except StopIteration as e:
        return e.value

def run_rr(gens, steps=None):
    gens = list(gens)
    steps = dict(zip(map(id, gens), steps or [1] * len(gens)))
    while gens:
        for g_ in list(gens):
            try:
                for _ in range(steps[id(g_)]):
                    next(g_)
            except StopIteration:
                gens.remove(g_)

def V(ap, dims):
    return bass.AP(ap.tensor, ap.offset, [list(ap.ap[0])] + [list(d) for d in dims])

def build_program(nt=NTILE, upto=9, sample=True):
    assert nt == NTILE
    nc = bass.Bass('TRN2', target_bir_lowering=False)

    def din(name, shape, dt=F32):
        return nc.dram_tensor(name, list(shape), dt, kind='ExternalInput').ap()

    def dout(name, shape, dt=F32):
        return nc.dram_tensor(name, list(shape), dt, kind='ExternalOutput').ap()
    xp = din('xp', [SEQ, D])
    xs = din('xs', [NS, D])
    ck = din('ck', [NB, 128, 128])
    cv = din('cv', [NB, 128, 128])
    sC = din('sC', [NB, 4, 128, 128])
    sn = din('sn', [NB * 4, 128])
    sm = din('sm', [NB, 4])
    gains = din('gains', [6, D])
    wg = din('wg', [2, D, DFF])
    wu = din('wu', [2, D, DFF])
    wd = din('wd', [2, DFF, D])
    win = din('win', [D, DIN])
    bgate = din('bgate', [8])
    mng = din('mng', [512])
    sinks = din('sinks', [8])
    wout = din('wout', [D, D])
    c_ident = din('c_ident', [128, 128])
    c_maskp = din('c_maskp', [128, 128])
    c_masks = din('c_masks', [64, 64])
    c_bias = din('c_bias', [128, 256])
    c_slope = din('c_slope', [128, 8])
    c_bsc = din('c_bsc', [4, 128])
    c_tb = din('c_tb', [4, 124])
    c_bm = din('c_bm', [128, NB * 64])
    c_bmT = din('c_bmT', [64, NB])
    c_E = din('c_E', [4, 4 * 128])
    c_role = din('c_role', [128, 17])
    x1s = nc.dram_tensor('x1s', [NTILE, 128, 5 * D], F32).ap()
    zs = nc.dram_tensor('zs', [NTILE, 128, 18304], BF16).ap()
    gsI = nc.dram_tensor('gsI', [NTILE, 128, TP + NS], F32).ap()
    gsF = nc.dram_tensor('gsF', [NTILE, 128, TP + NS], F32).ap()
    exin = nc.dram_tensor('exin', [128, PW], F32)
    exout = nc.dram_tensor('exout', [4 * 128, PW], F32)
    yp = dout('yp', [SEQ, D])
    ys = dout('ys', [NS, D])
    pk = dout('pk', [128, 128])
    pv = dout('pv', [128, 128])
    pC = dout('pC', [4, 128, 128])
    pn = dout('pn', [4, 128])
    pm = dout('pm', [4, 1])
    sk = dout('sk', [NB, 128, 128])
    sv = dout('sv', [NB, 128, 128])
    sCo = dout('sCo', [NB, 4, 128, 128])
    sno = dout('sno', [NB * 4, 128])
    smo = dout('smo', [NB, 4])
    S = Sched(nc)
    out_keys = []
    with contextlib.ExitStack() as st:

        def sb(name, shape, dt=F32):
            return st.enter_context(nc.sbuf_tensor(name, list(shape), dt))
        TT_ = TP + NS
        X = sb('X', [128, 5, D])
        hnT = sb('hnT', [128, 8, TT_], BF16)
        big = sb('big', [128, 18304], BF16)
        wblk = [sb('wblk%d' % i, [128, 8, 256], BF16) for i in range(4)]
        wbig = sb('wbig', [128, NJ, D], BF16)
        gT = sb('gT', [128, 6, 8])
        gp = sb('gp', [128, 1, D])
        ident = sb('ident', [128, 128])
        identb = sb('identb', [128, 128], BF16)
        maskp = sb('maskp', [128, 128])
        masks = sb('masks', [64, 64])
        biasT = sb('biasT', [128, 1, 256])
        slopeb = sb('slopeb', [128, 8])
        bsc = sb('bsc', [4, 1, 128])
        tbl = sb('tbl', [4, 1, 124])
        bmb = sb('bmb', [128, NB, 64], BF16)
        bmT = sb('bmT', [64, NB])
        Esel = sb('Esel', [4, 4, 128])
        mngb = sb('mngb', [128, 512])
        sinkb = sb('sinkb', [128, 8])
        bi_l = sb('bi_l', [128, 1])
        nbf_l = sb('nbf_l', [128, 1])
        tmpn = sb('tmpn', [128, D])
        stt = sb('stt', [128, 8])
        xn = sb('xn', [128, D], BF16)
        sg = [sb('sg%d' % i, [128, 512]) for i in range(1)]
        IGs = sb('IGs', [128, TT_])
        FGs = sb('FGs', [128, TT_])
        e1 = FGs
        lfn = FGs
        Bneg = sb('Bneg', [128, TT_])
        Ug = IGs
        MU = sb('MU', [128, TP + 1])
        MUs = sb('MUs', [128, NB, 5])
        tg = sb('tg', [128, TT_])
        Fst = tg
        Bprev = sb('Bprev', [128, 1])
        MUprev = sb('MUprev', [128, 1])
        dd = sb('dd', [4, 4])
        dds = sb('dds', [4, NB])
        DECs = sb('DECs', [128, 4, 4])
        DECss = sb('DECss', [128, 4, NB])
        FT = sb('FT', [128, 5, 16])
        smin = sb('smin', [128, NB])
        smout = sb('smout', [128, NB])
        Cst = sb('Cst', [128, 4, 129])
        Cbf = sb('Cbf', [128, 4, 129], BF16)
        Sp = sb('Sp', [128, 4, 128], BF16)
        kw = sb('kw', [128, 4, 128], BF16)
        kwm = sb('kwm', [64, 4, 128], BF16)
        tmpo = sb('tmpo', [128, 4, 129])
        ND = sb('ND', [128, 4, 129])
        q5 = sb('q5', [128, 8, 4])
        og = sb('og', [128, 512], BF16)
        t5 = tmpo
        U = sb('U', [128, 5, D], BF16)
        Ssb = sb('Ssb', [128, 4, 256])
        Pb = sb('Pb', [128, 4, 256], BF16)
        PTs = sb('PTs', [128, 8, 128], BF16)
        sst = sb('sst', [128, 8, 8])
        kvf = sb('kvf', [128, 2, 256])
        qTm = sb('qTm', [128, 2, 4, 64], BF16)
        Cb = [sb('Cb%d' % i, [128, 4, 129]) for i in range(2)]
        Cbb = [sb('Cbb%d' % i, [128, 4, 129], BF16) for i in range(2)]
        snin = sb('snin', [64, 128])
        nTin = sb('nTin', [128, 64])
        nTout = sb('nTout', [128, 64])
        snout = sb('snout', [64, 128])
        ckb = [sb('ckb%d' % i, [128, 4, 128], BF16) for i in range(2)]
        kTc = [sb('kTc%d' % i, [128, 4, 128], BF16) for i in range(2)]
        cvb = [sb('cvb%d' % i, [128, 128], BF16) for i in range(2)]
        PTss = sb('PTss', [128, 8, 2, 4], BF16)
        uab = [sb('uab%d' % i, [4, 512], BF16) for i in range(2)]
        pn_sb = sb('pn_sb', [4, 128])
        pm_sb = sb('pm_sb', [128, 1])
        kaTh = sb('kaTh', [128, 4, 128], BF16)
        vah = sb('vah', [128, 128], BF16)
        role = sb('role', [128, 17])
        cmb = sb('cmb', [128, 12])
        ABt = sb('ABt', [128, 4, 2])
        hT = big[:, 0:NJ * TT_].rearrange('p (j t) -> p j t', t=TT_)
        o_ = [0]

        def carve(n):
            a = big[:, o_[0]:o_[0] + n]
            o_[0] += n
            return a
        qT = carve(4 * TT_).rearrange('p (h t) -> p h t', t=TT_)
        kT = carve(4 * TT_).rearrange('p (h t) -> p h t', t=TT_)
        osig = carve(5 * 512).rearrange('p (g c) -> p g c', c=512)
        qaT = carve(4 * TT_).rearrange('p (h t) -> p h t', t=TT_)
        KW = 128 + TT_
        kaT = carve(4 * KW).rearrange('p (h t) -> p h t', t=KW)
        va = carve(6 * 128).rearrange('p (g c) -> p g c', c=128)
        assert o_[0] >= NJ * TT_
        ZT = o_[0]
        ktok = carve(5 * 512).rearrange('p (g c) -> p g c', c=512)
        vaug = carve(5 * 4 * 130).rearrange('p (g h c) -> p g h c', h=4, c=130)
        assert o_[0] <= 18304
        ps = st.enter_context(nc.psum_tensor('ps', [128, 8, 512], F32))

        def bank(i, n=1):
            return ps[:, i:i + n, :].rearrange('p a b -> p (a b)')

        def bkeys(i, n=1):
            return [('bank', i + k) for k in range(n)]

        def MM(out, lhsT, rhs, start, stop, R, W, **kw_):
            S.op('pe', lambda e: e.matmul(out, lhsT=lhsT, rhs=rhs, start=start, stop=stop, **kw_), R, W)

        def TR(out, in_, idn, R, W):
            S.op('pe', lambda e: e.transpose(out=out, in_=in_, identity=idn), R, W)

        def ACT(out, in_, func, R, W, **kw_):
            S.op('act', lambda e: e.activation(out=out, in_=in_, func=func, **kw_), R, W)

        def TTo(out, in0, in1, op, R, W, eng='dve'):
            S.op(eng, lambda e: e.tensor_tensor(out=out, in0=in0, in1=in1, op=op), R, W)

        def STT(out, in0, scalar, in1, op0, op1, R, W, eng='dve'):
            S.op(eng, lambda e: e.scalar_tensor_tensor(out=out, in0=in0, scalar=scalar, in1=in1, op0=op0, op1=op1), R, W)

        def TS(out, in0, s1, s2, op0, op1, R, W, eng='dve'):
            if s2 is None:
                S.op(eng, lambda e: e.tensor_scalar(out=out, in0=in0, scalar1=s1, scalar2=None, op0=op0), R, W)
            else:
                S.op(eng, lambda e: e.tensor_scalar(out=out, in0=in0, scalar1=s1, scalar2=s2, op0=op0, op1=op1), R, W)

        def CP(out, in_, R, W, eng='dve'):
            if eng == 'act':
                S.op('act', lambda e: e.copy(out=out, in_=in_), R, W)
            else:
                S.op(eng, lambda e: e.tensor_copy(out=out, in_=in_), R, W)

        def RCP(out, in_, R, W):
            S.op('dve', lambda e: e.reciprocal(out=out, in_=in_), R, W)

        def RED(out, in_, op, R, W):
            S.op('dve', lambda e: e.tensor_reduce(out=out, in_=in_, axis=AX.X, op=op), R, W)

        def SCAN(out, d0, init, op0, R, W):
            S.op('dve', lambda e: e.tensor_tensor_scan(out=out, data0=d0, data1=d0, initial=init, op0=op0, op1=ALU.bypass), R, W)

        def MSET(ap, val, W, eng='dve'):
            S.op(eng, lambda e: e.memset(ap, val), (), W)
        dctr = [0]
        nodma = [False]

        def DMA(out, in_, R, W, key=None, eng='sp', slow=False):
            if nodma[0] and eng == 'pool' and (key is not None) and key.startswith('w_'):
                return key
            if key is None:
                dctr[0] += 1
                key = 'dk%d' % (dctr[0] % 24)
            if slow:
                S.op(eng, lambda e: e.dma_start(out=out, in_=in_, allow_slow_non_contiguous=True), R, W, dma_key=key)
            else:
                S.op(eng, lambda e: e.dma_start(out=out, in_=in_), R, W, dma_key=key)
            return key

        DMA(X[:, 0:4, :], xp[0:TP, :].rearrange('(g p) d -> p g d', p=128), (), [('X', g) for g in range(4)], key='x_in')

        def LD(t, src, name):
            DMA(t, src, (), [name], key='c_' + name)
        LD(ident[:], c_ident[:, :], 'ident')
        LD(maskp[:], c_maskp[:, :], 'maskp')
        LD(masks[:], c_masks[:, :], 'masks')
        LD(biasT[:].rearrange('p h k -> p (h k)'), c_bias[:, :], 'biasT')
        LD(slopeb[:], c_slope[:, :], 'slopeb')
        LD(bsc[:].rearrange('p h k -> p (h k)'), c_bsc[:, :], 'bsc')
        LD(tbl[:].rearrange('p h k -> p (h k)'), c_tb[:, :], 'tbl')
        DMA(bmb[:].rearrange('p b t -> p (b t)'), c_bm[:, :], (), ['bmb'], key='c_bmb', eng='pool')
        LD(bmT[:], c_bmT[:, :], 'bmT')
        LD(Esel[:].rearrange('p h k -> p (h k)'), c_E[:, :], 'Esel')
        LD(role[:], c_role[:, :], 'role')
        CP(identb[:], ident[:], ['ident'], ['identb'])
        DMA(tmpn[0:6, :], gains[:, :], (), ['tmpn'], key='c_gT')
        pg_ = bank(1)
        for c in range(8):
            TR(pg_[:, 6 * c:6 * c + 6], tmpn[0:6, c * 128:(c + 1) * 128], ident[0:6, 0:6], ['tmpn', 'ident'], bkeys(1))
        CP(gT[:], V(pg_[:, 0:1], [[1, 6], [6, 8]]), bkeys(1), ['gT'])

        def load_gp(gi):
            DMA(gp[:, 0, :], bass.AP(gains.tensor, gains[gi, :].offset, [[0, 128], [1, D]]), (), ['gp'], key='c_gp')
        DMA(mngb[:], bass.AP(mng.tensor, mng.offset, [[0, 128], [1, 512]]), (), ['mngb'], key='c_mngb')
        DMA(sinkb[:], bass.AP(sinks.tensor, sinks.offset, [[0, 128], [1, 8]]), (), ['sinkb'], key='c_sinkb')
        MSET(bi_l[:], 0.0, ['bi_l'])
        MSET(nbf_l[:], 0.0, ['nbf_l'])
        MSET(smin[:], 0.0, ['smin'])
        for f in range(4):
            DMA(bi_l[32 * f:32 * f + 4, :], bgate[0:4].rearrange('(p o) -> p o', o=1), ['bi_l'], [('bi_l', f)], key='c_bil', slow=True)
            DMA(nbf_l[32 * f:32 * f + 4, :], bgate[4:8].rearrange('(p o) -> p o', o=1), ['nbf_l'], [('nbf_l', f)], key='c_bil', slow=True)
            DMA(smin[32 * f:32 * f + 4, :], sm.rearrange('b h -> h b'), ['smin'], [('smin', f)], key='c_smin', slow=True)
        lane_keys = [(nm_, f) for nm_ in ('bi_l', 'nbf_l', 'smin') for f in range(4)]
        neg_done = [False]
        MSET(big[:], 0.0, ['kaT_h', 'va0', 'vones'], eng='pool')
        MSET(IGs[:], 0.0, ['IGs'], eng='pool')
        MSET(FGs[:], 0.0, ['FGs'], eng='pool')
        MSET(X[:, 4, :], 0.0, [('X', 4)], eng='pool')
        MSET(Cst[:], 0.0, ['Cst'])
        MSET(Cbf[:], 0.0, ['Cbf'], eng='pool')
        MSET(Bprev[:], 0.0, ['Bprev'])
        MSET(MUprev[:], NEG, ['MUprev'])
        MSET(kaT[:, :, 0:128], 0.0, ['kaT_h'], eng='pool')
        MSET(va[:, 0, :], 0.0, ['va0'], eng='pool')
        MSET(vaug[:, :, :, 128:129], 1.0, ['vones'], eng='pool')
        for i in range(2):
            MSET(ckb[i][:], 0.0, [('ckb', i)], eng='pool')
        DMA(snin[:], sn[:, :], (), ['snin'], key='c_snin')
        TR(bank(1)[:, 0:64], snin[:, :], ident[0:64, 0:64], ['snin', 'ident'], bkeys(1))
        CP(nTin[:], bank(1)[:, 0:64], bkeys(1), ['nTin'])
        wq = []
        wslot = [0]

        extra_keys = {}

        def wload(src_fn):
            s = wslot[0] % 4
            wslot[0] += 1
            rk = 'wblk%d' % s
            extra_keys.pop(rk, None)
            src_fn(wblk[s], rk, 'w_slot%d' % s)
            return (wblk[s], rk)

        def colblk(Wap, c0, ncols):

            def f(slot, rk, key):
                DMA(slot[:, :, 0:ncols], Wap[:, c0:c0 + ncols].rearrange('(kc p) c -> p kc c', p=128), (), [rk], key=key, eng='pool')
            return f

        def colparts(Wap, parts):

            def f(slot, rk, key):
                extra_keys[rk] = [(rk, pi) for pi in range(1, len(parts))]
                for pi, (d0, c0, ncols) in enumerate(parts):
                    DMA(slot[:, :, d0:d0 + ncols], Wap[:, c0:c0 + ncols].rearrange('(kc p) c -> p kc c', p=128),
                        () if pi == 0 else [rk], [rk] if pi == 0 else [(rk, pi)], key=key if pi == 0 else key + '_p%d' % pi, eng='pool')
            return f

        def load_wbig(f):
            for j2 in range(11):
                DMA(wbig[:, 2 * j2:2 * j2 + 2, :], wd[f, 256 * j2:256 * j2 + 256, :].rearrange('(j p) c -> p j c', p=128), (), [('wbig', j2)], key='w_big%d' % j2, eng='pool')
        rot = {}

        def R2(name, n=2):
            rot[name] = (rot.get(name, -1) + 1) % n
            return rot[name]

        def groups_of(has_s):
            gs = [(g, 128, g * 128) for g in range(4)]
            if has_s:
                gs.append((4, 64, TP))
            return gs

        def prenorm(gi, has_s):
            for g, npp, c0 in groups_of(has_s):
                Xg = ('X', g)
                MSET(stt[:npp, 0:1], 0.0, ['stt'])
                ACT(xn[:npp, :], X[:npp, g, :], AF.Square, [Xg], ['xn', 'stt'], accum_out=stt[:npp, 0:1])
                ACT(stt[:npp, 1:2], stt[:npp, 0:1], AF.Sqrt, ['stt'], ['stt'], scale=1.0 / D, bias=EPS)
                RCP(stt[:npp, 2:3], stt[:npp, 1:2], ['stt'], ['stt'])
                TS(xn[:npp, :], X[:npp, g, :], stt[:npp, 2:3], None, ALU.mult, None, [Xg, 'stt'], ['xn'])
                b = 6 + R2('pT')
                pT = bank(b).bitcast(BF16).rearrange('p (c t) -> p c t', t=128)
                for c in range(8):
                    TR(pT[:, c, 0:npp], xn[:npp, c * 128:(c + 1) * 128], identb[:npp, :npp], ['xn', 'identb'], bkeys(b))
                TTo(hnT[:, :, c0:c0 + npp], pT[:, :, 0:npp], V(gT[:, gi, :], [[1, 8], [0, npp]]), ALU.mult, bkeys(b) + ['gT'], [('hnT', g)])

        def postnorm(py, pkeys, fac, g, npp):
            Xg = ('X', g)
            MSET(stt[:npp, 4:5], 0.0, ['stt'])
            ACT(tmpn[:npp, :], py[:npp, :], AF.Square, pkeys, ['tmpn', 'stt'], accum_out=stt[:npp, 4:5])
            ACT(stt[:npp, 5:6], stt[:npp, 4:5], AF.Sqrt, ['stt'], ['stt'], scale=1.0 / D, bias=EPS)
            RCP(stt[:npp, 6:7], stt[:npp, 5:6], ['stt'], ['stt'])
            STT(tmpn[:npp, :], py[:npp, :], stt[:npp, 6:7], gp[:npp, 0, :], ALU.mult, ALU.mult, pkeys + ['stt', 'gp'], ['tmpn'])
            STT(X[:npp, g, :], tmpn[:npp, :], fac, X[:npp, g, :], ALU.mult, ALU.add, [Xg, 'tmpn'], [Xg])

        def nsplits(has_s):
            return [(0, TP)] + ([(TP, NS)] if has_s else [])

        first_wbig = [False]

        def ffn(f, gi_post, has_s):
            hkeys = [('hnT', g) for g, _, _ in groups_of(has_s)]
            for blk in range(11):
                wgs, wgk = wload(colblk(wg[f], 256 * blk, 256))
                wus, wuk = wload(colblk(wu[f], 256 * blk, 256))
                if blk == 1 and first_wbig[0]:
                    first_wbig[0] = False
                    load_wbig(0)
                for jj in range(2):
                    j = 2 * blk + jj
                    for n0, nn in nsplits(has_s):
                        bg = R2('pg')
                        bu = 2 + R2('pu')
                        pg = bank(bg)
                        pu = bank(bu)
                        for kc in range(8):
                            MM(pg[:, 0:nn], wgs[:, kc, jj * 128:(jj + 1) * 128], hnT[:, kc, n0:n0 + nn], kc == 0, kc == 7, [wgk] + hkeys, bkeys(bg))
                        for kc in range(8):
                            MM(pu[:, 0:nn], wus[:, kc, jj * 128:(jj + 1) * 128], hnT[:, kc, n0:n0 + nn], kc == 0, kc == 7, [wuk] + hkeys, bkeys(bu))
                        si = 0
                        ACT(sg[si][:, 0:nn], pg[:, 0:nn], AF.Silu, bkeys(bg), ['sg%d' % si])
                        TTo(hT[:, j, n0:n0 + nn], sg[si][:, 0:nn], pu[:, 0:nn], ALU.mult, ['sg%d' % si] + bkeys(bu), [('hT', j, n0)])
            load_gp(gi_post)
            hall = [('hT', j, n0) for j in range(NJ) for n0, _ in nsplits(has_s)]
            for g, npp, c0 in groups_of(has_s):
                pyb = (4, 0)[R2('py')]
                py = bank(pyb, 2)
                for hf in range(2):
                    for j in range(NJ):
                        MM(py[:npp, hf * 512:(hf + 1) * 512], hT[:, j, c0:c0 + npp], wbig[:, j, hf * 512:(hf + 1) * 512], j == 0, j == NJ - 1, hall + [('wbig', j // 2)], bkeys(pyb, 2))
                postnorm(py, bkeys(pyb, 2), 0.5, g, npp)
        zk = lambda nm, g: ('z', nm, g)

        def w_in(has_s):
            grp = groups_of(has_s)
            hkeys = [('hnT', g) for g, _, _ in grp]

            def fm_chunk(ws, wk_, lhs_fn, evac):
                for n0, nn in nsplits(has_s):
                    b = R2('pg')
                    p = bank(b)
                    for kc in range(8):
                        MM(p[:, 0:nn], lhs_fn(ws, kc), hnT[:, kc, n0:n0 + nn], kc == 0, kc == 7, [wk_] + extra_keys.get(wk_, []) + hkeys, bkeys(b))
                    evac(p, b, n0, nn)

            def tm_block(ws, wk_, ncols, evac):
                for g, npp, c0 in grp:
                    b = 2 + R2('pu')
                    p = bank(b)
                    for kc in range(8):
                        MM(p[:npp, 0:ncols], hnT[:, kc, c0:c0 + npp], ws[:, kc, 0:ncols], kc == 0, kc == 7, [wk_, ('hnT', g)], bkeys(b))
                    evac(p, b, g, npp)
            sc_k = 128.0 ** (-0.5)
            for blk in range(2):
                ws, wk_ = wload(colblk(win, 256 * blk, 256))
                for jj in range(2):
                    h = 2 * blk + jj
                    fm_chunk(ws, wk_, lambda w_, kc, jj=jj: w_[:, kc, jj * 128:(jj + 1) * 128], lambda p, b, n0, nn, h=h: CP(qT[:, h, n0:n0 + nn], p[:, 0:nn], bkeys(b), [('qT', n0)], eng='act'))
            if os.environ.get('WSTOP') == '1':
                return
            for blk in range(2):
                ws, wk_ = wload(colblk(win, 512 + 256 * blk, 256))
                for jj in range(2):
                    h = 2 * blk + jj
                    fm_chunk(ws, wk_, lambda w_, kc, jj=jj: w_[:, kc, jj * 128:(jj + 1) * 128], lambda p, b, n0, nn, h=h: S.op('act', lambda e: e.mul(out=kT[:, h, n0:n0 + nn], in_=p[:, 0:nn], mul=sc_k), bkeys(b), [('kT', n0)]))
                tm_block(ws, wk_, 256, lambda p, b, g, npp, blk=blk: TS(ktok[:npp, g, 256 * blk:256 * blk + 256], p[:npp, 0:256], sc_k, None, ALU.mult, None, bkeys(b), [zk('ktok', g)]))
            if os.environ.get('WSTOP') == '2':
                return
            for blk in range(2):
                ws, wk_ = wload(colblk(win, 1024 + 256 * blk, 256))

                def ev_v(p, b, g, npp, blk=blk):
                    CP(vaug[:npp, g, 2 * blk:2 * blk + 2, 0:128], p[:npp, 0:256].rearrange('p (h c) -> p h c', c=128), bkeys(b), [zk('vaug', g)])
                    if blk == 1:
                        MSET(vaug[:npp, g, :, 128:129], 1.0, [zk('vaug', g)])
                tm_block(ws, wk_, 256, ev_v)
            if os.environ.get('WSTOP') == '3':
                return
            for blk in range(2):
                ws, wk_ = wload(colblk(win, 1536 + 256 * blk, 256))
                tm_block(ws, wk_, 256, lambda p, b, g, npp, blk=blk: ACT(osig[:npp, g, 256 * blk:256 * blk + 256], p[:npp, 0:256], AF.Sigmoid, bkeys(b), [zk('osig', g)]))
            if os.environ.get('WSTOP') == '4':
                return
            ws, wk_ = wload(colparts(win, [(32 * f_, 2048, 32) for f_ in range(4)] + [(128 + 32 * f_, 2052, 32) for f_ in range(4)]))
            fm_chunk(ws, wk_, lambda w_, kc: w_[:, kc, 0:128], lambda p, b, n0, nn: CP(IGs[:, n0:n0 + nn], p[:, 0:nn], bkeys(b), [('IGs', n0)], eng='act'))
            fm_chunk(ws, wk_, lambda w_, kc: w_[:, kc, 128:256], lambda p, b, n0, nn: CP(FGs[:, n0:n0 + nn], p[:, 0:nn], bkeys(b), [('FGs', n0)], eng='act'))
            if os.environ.get('WSTOP') == '5':
                return
            for blk in range(2):
                ws, wk_ = wload(colblk(win, 2056 + 256 * blk, 256))
                for jj in range(2):
                    c = 2 * blk + jj
                    fm_chunk(ws, wk_, lambda w_, kc, jj=jj: w_[:, kc, jj * 128:(jj + 1) * 128], lambda p, b, n0, nn, c=c: S.op('act', lambda e: e.mul(out=qaT[:, c, n0:n0 + nn], in_=p[:, 0:nn], mul=0.125), bkeys(b), [('qaT', n0)]))
            if os.environ.get('WSTOP') == '6':
                return
            for hk in range(2):

                def kaf(slot, rk, key, hk=hk):
                    MSET(slot[:, :, 64:192], 0.0, [rk], eng='pool')
                    for d0 in (0, 192):
                        DMA(slot[:, :, d0:d0 + 64], win[:, 2568 + 64 * hk:2568 + 64 * hk + 64].rearrange('(kc p) c -> p kc c', p=128), (), [rk], key=key, eng='pool')
                ws, wk_ = wload(kaf)
                for par in range(2):
                    fm_chunk(ws, wk_, lambda w_, kc, par=par: w_[:, kc, par * 128:(par + 1) * 128], lambda p, b, n0, nn, v=2 * hk + par: CP(kaT[:, v, 128 + n0:128 + n0 + nn], p[:, 0:nn], bkeys(b), [('kaT', n0)], eng='act'))
            if os.environ.get('WSTOP') == '7':
                return
            ws, wk_ = wload(colblk(win, 2568, 256))

            def ev_kv(p, b, g, npp):
                if has_s and g >= 3:
                    CP(kvf[:npp, g - 3, :], p[:npp, 0:256], bkeys(b), [('kvf', g)])
                CP(va[:npp, 1 + g, :], p[:npp, 128:256], bkeys(b), [('va', g)], eng='act')
            tm_block(ws, wk_, 256, ev_kv)

        def gates(has_s, phase=2):
            rI = [('IGs', n0) for n0, _ in nsplits(has_s)]
            rF = [('FGs', n0) for n0, _ in nsplits(has_s)]
            TTn = TP + (NS if has_s else 0)
            if not neg_done[0]:
                neg_done[0] = True
                S.op('act', lambda e: e.mul(out=nbf_l[:], in_=nbf_l[:], mul=-1.0), ['nbf_l', 'bi_l', 'smin'] + lane_keys, ['nbf_l', 'bi_l', 'smin'] + lane_keys)
            ACT(e1[:, 0:TTn], FGs[:, 0:TTn], AF.Exp, rF + ['nbf_l'], ['e1'], scale=-1.0, bias=nbf_l[:, 0:1])
            ACT(lfn[:, 0:TTn], e1[:, 0:TTn], AF.Ln, ['e1'], ['lfn'], bias=1.0)
            if os.environ.get('GSTOP') == '1':
                return
            SCAN(Bneg[:, 0:TP], lfn[:, 0:TP], Bprev[:, 0:1], ALU.add, ['lfn', 'Bprev'], ['Bneg'])
            STT(Ug[:, 0:TP], IGs[:, 0:TP], bi_l[:, 0:1], Bneg[:, 0:TP], ALU.add, ALU.add, rI + ['bi_l', 'Bneg'], ['Ug'])
            CP(MU[:, 0:1], MUprev[:, 0:1], ['MUprev'], ['MU'])
            SCAN(MU[:, 1:TP + 1], Ug[:, 0:TP], MUprev[:, 0:1], ALU.max, ['Ug', 'MUprev', 'MU'], ['MU'])
            CP(Bprev[:, 0:1], Bneg[:, TP - 1:TP], ['Bneg'], ['Bprev'])
            CP(MUprev[:, 0:1], MU[:, TP:TP + 1], ['MU'], ['MUprev'])
            if os.environ.get('GSTOP') == '2':
                return
            MUn = V(MU[:, 128:129], [[128, 4], [0, 128]])
            MUp = V(MU[:, 0:1], [[128, 4], [0, 128]])
            MUc = MU[:, 1:TP + 1].rearrange('p (c t) -> p c t', t=128)
            v3 = lambda a, lo: a[lo:lo + 32, 0:TP].rearrange('p (c t) -> p c t', t=128)
            sl = lambda a, lo: bass.AP(a.tensor, a.offset + lo * a.ap[0][0], [[a.ap[0][0], 32]] + [list(x) for x in a.ap[1:]])
            TTo(v3(tg, 0), v3(Ug, 0), sl(MUn, 0), ALU.subtract, ['Ug', 'MU'], ['tg'])
            TTo(v3(tg, 32), sl(MUn, 32), sl(MUc, 32), ALU.subtract, ['MU'], ['tg'])
            TTo(v3(tg, 64), sl(MUp, 64), sl(MUc, 64), ALU.subtract, ['MU'], ['tg'])
            TTo(v3(tg, 96), v3(Bneg, 96), sl(MUc, 96), ALU.subtract, ['MU', 'Bneg'], ['tg'])
            TTo(dd[0:4, 0:4], V(MU[0:4, 0:1], [[128, 4]]), V(MU[0:4, 128:129], [[128, 4]]), ALU.subtract, ['MU'], ['dd'])
            ACT(dd[0:4, 0:4], dd[0:4, 0:4], AF.Exp, ['dd'], ['dd'])
            if has_s:
                c0 = TP
                l3 = lfn[:, c0:c0 + NS].rearrange('p (b t) -> p b t', t=4)
                B3 = Bneg[:, c0:c0 + NS].rearrange('p (b t) -> p b t', t=4)
                U3 = Ug[:, c0:c0 + NS].rearrange('p (b t) -> p b t', t=4)
                I3 = IGs[:, c0:c0 + NS].rearrange('p (b t) -> p b t', t=4)
                CP(B3[:, :, 0:1], l3[:, :, 0:1], ['lfn'], ['Bneg'])
                for t in range(1, 4):
                    TTo(B3[:, :, t:t + 1], B3[:, :, t - 1:t], l3[:, :, t:t + 1], ALU.add, ['lfn', 'Bneg'], ['Bneg'])
                STT(U3, I3, bi_l[:, 0:1], B3, ALU.add, ALU.add, rI + ['bi_l', 'Bneg'], ['Ug'])
                CP(MUs[:, :, 0:1], smin[:].rearrange('p (b o) -> p b o', o=1), ['smin'], ['MUs'])
                for t in range(4):
                    TTo(MUs[:, :, t + 1:t + 2], MUs[:, :, t:t + 1], U3[:, :, t:t + 1], ALU.max, ['Ug', 'MUs'], ['MUs'])
                MUn_s = V(MUs[:, 0, 4:5], [[5, NB], [0, 4]])
                MUp_s = V(MUs[:, 0, 0:1], [[5, NB], [0, 4]])
                MUc_s = MUs[:, :, 1:5]
                t3 = lambda lo: tg[lo:lo + 32, c0:c0 + NS].rearrange('p (b t) -> p b t', t=4)
                TTo(t3(0), U3[0:32], sl(MUn_s, 0), ALU.subtract, ['Ug', 'MUs'], ['tg'])
                TTo(t3(32), sl(MUn_s, 32), MUc_s[32:64], ALU.subtract, ['MUs'], ['tg'])
                TTo(t3(64), sl(MUp_s, 64), MUc_s[64:96], ALU.subtract, ['MUs'], ['tg'])
                TTo(t3(96), B3[96:128], MUc_s[96:128], ALU.subtract, ['MUs', 'Bneg'], ['tg'])
                TTo(dds[0:4, :], V(MUs[0:4, 0, 0:1], [[5, NB]]), V(MUs[0:4, 0, 4:5], [[5, NB]]), ALU.subtract, ['MUs'], ['dds'])
                ACT(dds[0:4, :], dds[0:4, :], AF.Exp, ['dds'], ['dds'])
                TTo(smout[:, :], V(MUs[:, 0, 4:5], [[5, NB]]), V(B3[:, 0, 3:4], [[4, NB]]), ALU.subtract, ['MUs', 'Bneg'], ['smout'])
                DMA(smo.rearrange('b h -> h b'), smout[0:4, :], ['smout'], ['o_smo'], key='o_sm', slow=True)
            if os.environ.get('GSTOP') == '3':
                return
            TS(tg[:, 0:TTn], tg[:, 0:TTn], 80.0, None, ALU.min, None, ['tg'], ['tg'])
            ACT(Fst[:, 0:TTn], tg[:, 0:TTn], AF.Exp, ['tg'], ['Fst'])
            pd = bank(1)
            for h in range(4):
                MM(pd[:, 4 * h:4 * h + 4], Esel[0:4, h, :], dd[0:4, 0:4], True, True, ['Esel', 'dd'], bkeys(1))
            CP(DECs[:].rearrange('p h c -> p (h c)'), pd[:, 0:16], bkeys(1), ['DECs'])
            if has_s:
                for h in range(4):
                    MM(pd[:, 64 + NB * h:64 + NB * h + NB], Esel[0:4, h, :], dds[0:4, :], True, True, ['Esel', 'dds'], bkeys(1))
                CP(DECss[:].rearrange('p h c -> p (h c)'), pd[:, 64:64 + 4 * NB], bkeys(1), ['DECss'])
            if os.environ.get('GSTOP') == '4':
                return
            pf = bank(7)
            for g in range(4):
                TR(pf[:, g * 128:(g + 1) * 128], Fst[:, g * 128:(g + 1) * 128], ident[:, :], ['Fst', 'ident'], bkeys(7))
            CP(FT[:, 0:4, :].rearrange('p g (f h) -> p g f h', h=4), V(pf[:, 0:1], [[128, 4], [32, 4], [1, 4]]), bkeys(7), ['FT'])
            if has_s:
                pf2 = bank(6)
                TR(pf2[0:64, 0:128], Fst[:, TP:TP + NS], ident[:, :], ['Fst', 'ident'], bkeys(6))
                CP(FT[0:64, 4, :].rearrange('p (f h) -> p f h', h=4), V(pf2[0:64, 0:1], [[32, 4], [1, 4]]), bkeys(6), ['FT'])

        def mlstm_group(g, npp, c0, c_idx, is_s):
            zq = [('qT', 0), ('qT', TP), ('kT', 0), ('kT', TP)]
            pS = bank(0).rearrange('p (h t) -> p h t', t=128)
            for h in range(4):
                MM(pS[:npp, h, 0:npp], kT[:, h, c0:c0 + npp], qT[:, h, c0:c0 + npp], True, True, zq, bkeys(0))
                yield
            mk = masks if is_s else maskp
            for h in range(4):
                STT(Sp[:npp, h, 0:npp], pS[:npp, h, 0:npp], FT[:npp, g, h:h + 1], mk[:npp, :npp], ALU.mult, ALU.mult, bkeys(0) + ['FT', 'masks', 'maskp'], ['Sp'])
                yield
            TTo(kw[:npp, :, :], ktok[:npp, g, :].rearrange('p (h c) -> p h c', c=128), V(FT[:npp, g, 0:1], [[1, 4], [0, 128]]), ALU.mult, [zk('ktok', g), 'FT'], ['kw'])
            yield
            pKV = bank(1, 2).rearrange('p (h c) -> p h c', c=256)
            pO1 = bank(3, 2).rearrange('p (h c) -> p h c', c=256)
            pO2 = bank(3, 2).rearrange('p (h c) -> p h c', c=256)
            vk = [zk('vaug', g), 'vones']
            if not is_s:
                for h in range(4):
                    MM(pKV[:, h, 0:129], kw[:, h, :], vaug[:, g, h, 0:129], True, True, ['kw'] + vk, bkeys(1, 2))
                    yield
                for h in range(4):
                    MM(pO1[:, h, 0:129], qT[:, h, c0:c0 + 128], Cbf[:, h, :], True, True, zq + ['Cbf'], bkeys(3, 2))
                    yield
            else:
                MSET(tmpo[:64], 0.0, ['tmpo'])
                yield
                for b in range(NB):
                    i = b % 2
                    TTo(qTm[:, i, :, :], qT[:, :, c0:c0 + 64], V(bmb[:, b, 0:1], [[0, 4], [1, 64]]), ALU.mult, zq + ['bmb'], [('qTm', i)])
                    yield
                    DMA(Cb[i][:, :, 0:128], sC[b].rearrange('h k v -> k h v'), (), [('Cb', i)], key='cb%d' % i)
                    yield
                    CP(Cb[i][:, :, 128:129], nTin[:, 4 * b:4 * b + 4].rearrange('p (h o) -> p h o', o=1), ['nTin'], [('Cb', i)], eng='act')
                    yield
                    CP(Cbb[i][:], Cb[i][:], [('Cb', i)], [('Cbb', i)], eng='act')
                    yield
                    for h in range(4):
                        MM(pO1[0:64, h, 0:129], qTm[:, i, h, :], Cbb[i][:, h, :], True, True, [('qTm', i), ('Cbb', i)], bkeys(3, 2))
                        yield
                    TTo(tmpo[:64], pO1[0:64, :, 0:129], tmpo[:64], ALU.add, bkeys(3, 2) + ['tmpo'], ['tmpo'])
                    yield
                    TS(kwm[:, :, :].rearrange('p h c -> p (h c)'), kw[0:64, :, :].rearrange('p h c -> p (h c)'), bmT[:, b:b + 1], None, ALU.mult, None, ['kw', 'bmT'], ['kwm'])
                    yield
                    for h in range(4):
                        MM(pKV[:, h, 0:129], kwm[:, h, :], vaug[0:64, g, h, 0:129], True, True, ['kwm'] + vk, bkeys(1, 2))
                        yield
                    TTo(Cb[i][:], Cb[i][:], V(DECss[:, 0, b:b + 1], [[NB, 4], [0, 129]]), ALU.mult, [('Cb', i), 'DECss'], [('Cb', i)])
                    yield
                    TTo(Cb[i][:], Cb[i][:], pKV[:, :, 0:129], ALU.add, [('Cb', i)] + bkeys(1, 2), [('Cb', i)])
                    yield
                    DMA(sCo[b].rearrange('h k v -> k h v'), Cb[i][:, :, 0:128], [('Cb', i)], ['o_sC%d' % i], key='o_sC%d' % i)
                    yield
                    CP(nTout[:, 4 * b:4 * b + 4].rearrange('p (h o) -> p h o', o=1), Cb[i][:, :, 128:129], [('Cb', i)], ['nTout'], eng='act')
                    yield
            if is_s:
                TTo(tmpo[:npp], tmpo[:npp], V(FT[:npp, g, 8:9], [[1, 4], [0, 129]]), ALU.mult, ['tmpo', 'FT'], ['tmpo'])
                yield
            else:
                TTo(tmpo[:npp], pO1[:npp, :, 0:129], V(FT[:npp, g, 8:9], [[1, 4], [0, 129]]), ALU.mult, bkeys(3, 2) + ['FT'], ['tmpo'])
                yield
            for h in range(4):
                MM(pO2[:npp, h, 0:129], Sp[:npp, h, 0:npp], vaug[:npp, g, h, 0:129], True, True, ['Sp'] + vk, bkeys(3, 2))
                yield
            for h in range(4):
                STT(ND[:npp, h, :], pO2[:npp, h, 0:129], FT[:npp, g, 4 + h:5 + h], tmpo[:npp, h, :], ALU.mult, ALU.add, bkeys(3, 2) + ['FT', 'tmpo'], ['ND'])
                yield
            TS(q5[:npp, 0, :], ND[:npp, :, 128], -1.0, None, ALU.mult, None, ['ND'], ['q5'])
            yield
            TTo(q5[:npp, 0, :], q5[:npp, 0, :], ND[:npp, :, 128], ALU.max, ['ND', 'q5'], ['q5'])
            yield
            TTo(q5[:npp, 0, :], q5[:npp, 0, :], FT[:npp, g, 12:16], ALU.max, ['q5', 'FT'], ['q5'])
            yield
            RCP(q5[:npp, 1, :], q5[:npp, 0, :], ['q5'], ['q5'])
            yield
            MSET(q5[:npp, 2, :], 0.0, ['q5'])
            yield
            for h in range(4):
                ACT(kw[:npp, h, :], ND[:npp, h, 0:128], AF.Square, ['ND'], ['kw', 'q5'], accum_out=q5[:npp, 2, h:h + 1])
                yield
            TTo(q5[:npp, 3, :], q5[:npp, 1, :], q5[:npp, 1, :], ALU.mult, ['q5'], ['q5'])
            yield
            TTo(q5[:npp, 4, :], q5[:npp, 3, :], q5[:npp, 2, :], ALU.mult, ['q5'], ['q5'])
            yield
            ACT(q5[:npp, 5, :], q5[:npp, 4, :], AF.Ln, ['q5'], ['q5'], scale=1.0 / 128, bias=EPS)
            yield
            ACT(q5[:npp, 6, :], q5[:npp, 5, :], AF.Exp, ['q5'], ['q5'], scale=-0.5)
            yield
            TTo(q5[:npp, 7, :], q5[:npp, 6, :], q5[:npp, 1, :], ALU.mult, ['q5'], ['q5'])
            yield
            TTo(og[:npp, :], osig[:npp, g, :], mngb[:npp, :], ALU.mult, [zk('osig', g), 'mngb'], ['og'])
            yield
            TTo(t5[:npp, :, 0:128], ND[:npp, :, 0:128], V(q5[:npp, 7, 0:1], [[1, 4], [0, 128]]), ALU.mult, ['ND', 'q5', 'tmpo'], ['tmpo'])
            yield
            TTo(U[:npp, g, 0:512].rearrange('p (h c) -> p h c', c=128), t5[:npp, :, 0:128], og[:npp, :].rearrange('p (h c) -> p h c', c=128), ALU.mult, ['tmpo', 'og'], [('U', g, 0)])
            yield
            if not is_s:
                TTo(Cst[:], Cst[:], V(DECs[:, 0, c_idx:c_idx + 1], [[4, 4], [0, 129]]), ALU.mult, ['Cst', 'DECs'], ['Cst'])
                yield
                TTo(Cst[:], Cst[:], pKV[:, :, 0:129], ALU.add, ['Cst'] + bkeys(1, 2), ['Cst'])
                yield
                CP(Cbf[:], Cst[:], ['Cst'], ['Cbf'], eng='act')
                yield

        def softmax_tail(Sx, Px, npp, nk, R_, half):
            sk_ = sinkb[:npp, 4 * half:4 * half + 4]
            RED(sst[:npp, 0, 0:4], Sx, ALU.max, R_, ['sst'])
            yield
            TTo(sst[:npp, 1, 0:4], sst[:npp, 0, 0:4], sk_, ALU.max, ['sst', 'sinkb'], ['sst'])
            yield
            TS(sst[:npp, 7, 0:4], sst[:npp, 1, 0:4], -1.0, None, ALU.mult, None, ['sst'], ['sst'])
            yield
            MSET(sst[:npp, 2, 0:4], 0.0, ['sst'])
            yield
            for hh_ in range(4):
                ACT(Px[:, hh_, :], Sx[:, hh_, :], AF.Exp, R_ + ['sst'], ['Px', 'sst'], bias=sst[:npp, 7, hh_:hh_ + 1], accum_out=sst[:npp, 2, hh_:hh_ + 1])
                yield
            TTo(sst[:npp, 3, 0:4], sk_, sst[:npp, 1, 0:4], ALU.subtract, ['sst', 'sinkb'], ['sst'])
            yield
            ACT(sst[:npp, 4, 0:4], sst[:npp, 3, 0:4], AF.Exp, ['sst'], ['sst'])
            yield
            TTo(sst[:npp, 5, 0:4], sst[:npp, 4, 0:4], sst[:npp, 2, 0:4], ALU.add, ['sst'], ['sst'])
            yield
            RCP(sst[:npp, 6, 0:4], sst[:npp, 5, 0:4], ['sst'], ['sst'])
            yield
            return sst[:npp, 6, 0:4]

        def swa_group(g, first):
            zq = [('qaT', 0), ('kaT', 0), 'kaT_h']
            c0 = g * 128
            vkeys = ['va0', ('va', g)] + ([('va', g - 1)] if g > 0 else [])
            for half in range(2):
                pS = bank(5, 2).rearrange('p (h k) -> p h k', k=256)
                for hh in range(4):
                    h = 4 * half + hh
                    MM(pS[:, hh, :], qaT[:, h // 2, c0:c0 + 128], kaT[:, 2 * half + h % 2, c0:c0 + 256], True, True, zq, bkeys(5, 2))
                    yield
                for hh in range(4):
                    h = 4 * half + hh
                    STT(Ssb[:, hh, :], biasT[:, 0, :], slopeb[:, h:h + 1], pS[:, hh, :], ALU.mult, ALU.add, bkeys(5, 2) + ['biasT', 'slopeb'], ['Ssb'])
                    yield
                if first:
                    TS(Ssb[:, :, 0:128], Ssb[:, :, 0:128], role[:, 16:17], None, ALU.add, None, ['Ssb', 'role'], ['Ssb'])
                    yield
                if os.environ.get('SSTOP') == '1':
                    continue
                rden = (yield from softmax_tail(Ssb[:], Pb[:], 128, 256, ['Ssb'], half))
                if os.environ.get('SSTOP') == '2':
                    continue
                pb_ = 7
                PT = bank(pb_).bitcast(BF16).rearrange('p (h t) -> p h t', t=128)
                for hh in range(4):
                    for kt in range(2):
                        TR(PT[:, 2 * hh + kt, :], Pb[:, hh, kt * 128:(kt + 1) * 128], identb[:, :], ['Px', 'identb'], bkeys(pb_))
                        yield
                CP(PTs[:, 0:4, :], PT[:, 0:4, :], bkeys(pb_), ['PTs'], eng='act')
                yield
                CP(PTs[:, 4:8, :], PT[:, 4:8, :], bkeys(pb_), ['PTs'], eng='act')
                yield
                if os.environ.get('SSTOP') == '3':
                    continue
                pO = bank(7).rearrange('p (h c) -> p h c', c=64)
                for hh in range(4):
                    for kt in range(2):
                        MM(pO[:, hh, :], PTs[:, 2 * hh + kt, :], va[:, g + kt, half * 64:(half + 1) * 64], kt == 0, kt == 1, ['PTs'] + vkeys, bkeys(7))
                        yield
                TTo(U[:, g, 512 + 256 * half:768 + 256 * half].rearrange('p (h c) -> p h c', c=64), pO[:, 0:4, :], V(rden[:, 0:1], [[1, 4], [0, 64]]), ALU.mult, bkeys(7) + ['sst'], [('U', g, 1 + half)])
                yield

        def swa_sample():
            zq = [('qaT', TP), ('kaT', TP)]
            Ss = Ssb[0:4, 0:3, :].rearrange('p a k -> p (a k)').rearrange('p (h k) -> p h k', k=192)
            Ps = Pb[0:4, 0:3, :].rearrange('p a k -> p (a k)').rearrange('p (h k) -> p h k', k=192)
            for b in range(NB):
                i = b % 2
                for v in range(4):
                    hk, par = (v // 2, v % 2)
                    DMA(ckb[i][:, v, 64 * par:64 * par + 64], ck[b][:, 64 * hk:64 * hk + 64], [('ckb', i)], [('ckb', i, v)], key='ck%d_%d' % (i, v), eng='pool')
                    yield
                DMA(cvb[i][:], cv[b], (), [('cvb', i)], key='cv%d' % i, eng='pool')
                yield
                pk_ = bank(7)
                for v in range(4):
                    MM(pk_[:, v * 128:(v + 1) * 128], ckb[i][:, v, :], identb[:, :], True, True, [('ckb', i), ('ckb', i, v), 'identb'], bkeys(7))
                    yield
                CP(kTc[i][:].rearrange('p h t -> p (h t)'), pk_[:, 0:512], bkeys(7), [('kTc', i)], eng='act')
                yield
                for half in range(2):
                    pSc = bank(5).rearrange('p (h k) -> p h k', k=128)
                    pSn = bank(6).rearrange('p (h k) -> p h k', k=64)
                    for hh in range(4):
                        h = 4 * half + hh
                        q_ = qaT[:, h // 2, TP + 4 * b:TP + 4 * b + 4]
                        MM(pSc[0:4, hh, :], q_, kTc[i][:, 2 * half + h % 2, :], True, True, zq + [('kTc', i)], bkeys(5))
                        yield
                        MM(pSn[0:4, hh, :], q_, kaT[:, 2 * half + h % 2, 128 + TP:128 + TP + 64], True, True, zq, bkeys(6))
                        yield
                    for hh in range(4):
                        h = 4 * half + hh
                        STT(Ss[0:4, hh, 0:128], bsc[0:4, 0, :], slopeb[0:4, h:h + 1], pSc[0:4, hh, :], ALU.mult, ALU.add, bkeys(5) + ['bsc', 'slopeb'], ['Ssb'])
                        yield
                        STT(Ss[0:4, hh, 128:192], tbl[0:4, 0, 60 - 4 * b:124 - 4 * b], slopeb[0:4, h:h + 1], pSn[0:4, hh, :], ALU.mult, ALU.add, bkeys(6) + ['tbl', 'slopeb'], ['Ssb'])
                        yield
                    rden = (yield from softmax_tail(Ss, Ps, 4, 192, ['Ssb'], half))
                    PT = bank(6).bitcast(BF16)[:, 0:32].rearrange('p (h k q) -> p h k q', k=2, q=4)
                    for hh in range(4):
                        TR(PT[:, hh, 0, :], Ps[0:4, hh, 0:128], identb[0:4, 0:4], ['Px', 'identb'], bkeys(6))
                        yield
                        TR(PT[0:64, hh, 1, :], Ps[0:4, hh, 128:192], identb[0:4, 0:4], ['Px', 'identb'], bkeys(6))
                        yield
                    CP(PTss[:, 0:4, 0, :], PT[:, :, 0, :], bkeys(6), ['PTss'], eng='act')
                    yield
                    CP(PTss[0:64, 0:4, 1, :], PT[0:64, :, 1, :], bkeys(6), ['PTss'])
                    yield
                    pO = bank(7).rearrange('p (h c) -> p h c', c=64)
                    for hh in range(4):
                        MM(pO[0:4, hh, :], PTss[:, hh, 0, :], cvb[i][:, half * 64:(half + 1) * 64], True, False, ['PTss', ('cvb', i)], bkeys(7))
                        yield
                        MM(pO[0:4, hh, :], PTss[0:64, hh, 1, :], va[0:64, 5, half * 64:(half + 1) * 64], False, True, ['PTss', ('va', 4)], bkeys(7))
                        yield
                    TTo(uab[i][0:4, 256 * half:256 * half + 256].rearrange('p (h c) -> p h c', c=64), pO[0:4, 0:4, :], V(rden[:, 0:1], [[1, 4], [0, 64]]), ALU.mult, bkeys(7) + ['sst'], [('uab', i)])
                    yield
                DMA(U[4 * b:4 * b + 4, 4, 512:1024], uab[i][0:4, :], [('uab', i)], [('U', 4, 1, b)], key='uab%d' % i)
                yield
                DMA(sk[b, 0:124, :], ck[b, 4:128, :], (), [('o_sk', b)], key='o_sk')
                yield
                DMA(sv[b, 0:124, :], cv[b, 4:128, :], (), [('o_sv', b)], key='o_sv')
                yield

        def w_out_stage(has_s):
            grp = groups_of(has_s)
            for g, npp, c0 in grp:
                b = 6 + R2('pT')
                pT = bank(b).bitcast(BF16).rearrange('p (c t) -> p c t', t=128)
                for c in range(8):
                    TR(pT[:, c, 0:npp], U[:npp, g, c * 128:(c + 1) * 128], identb[:npp, :npp], [('U', g, 0), ('U', g, 1), ('U', g, 2), 'identb'] + [('U', 4, 1, bb) for bb in range(NB)], bkeys(b))
                CP(hnT[:, :, c0:c0 + npp], pT[:, :, 0:npp], bkeys(b), [('hnT', g)], eng='act')
            load_gp(3)
            slots = [wload(colblk(wout, 256 * blk, 256)) for blk in range(4)]
            for g, npp, c0 in grp:
                pyb = (4, 0)[R2('py')]
                py = bank(pyb, 2)
                for blk in range(4):
                    ws, wk_ = slots[blk]
                    for kc in range(8):
                        MM(py[:npp, 256 * blk:256 * blk + 256], hnT[:, kc, c0:c0 + npp], ws[:, kc, 0:256], kc == 0, kc == 7, [wk_, ('hnT', g)], bkeys(pyb, 2))
                postnorm(py, bkeys(pyb, 2), 1.0, g, npp)
        ZW = o_[0]
        ZSET = {'qT', 'kT', 'qaT', 'kaT', 'va', 'z', 'vones', 'kaT_h', 'va0', 'hT'}

        def zkeys():
            return [k for k in S.last_writer if (k[0] if isinstance(k, tuple) else k) in ZSET]

        def mlstm_state_only(g, c_idx):
            TTo(kw[:, :, :], ktok[:, g, :].rearrange('p (h c) -> p h c', c=128), V(FT[:, g, 0:1], [[1, 4], [0, 128]]), ALU.mult, [zk('ktok', g), 'FT'], ['kw'])
            pKV = bank(2, 2).rearrange('p (h c) -> p h c', c=256)
            for h in range(4):
                MM(pKV[:, h, 0:129], kw[:, h, :], vaug[:, g, h, 0:129], True, True, ['kw', zk('vaug', g), 'vones'], bkeys(2, 2))
            TTo(Cst[:], Cst[:], V(DECs[:, 0, c_idx:c_idx + 1], [[4, 4], [0, 129]]), ALU.mult, ['Cst', 'DECs'], ['Cst'])
            TTo(Cst[:], Cst[:], pKV[:, :, 0:129], ALU.add, ['Cst'] + bkeys(2, 2), ['Cst'])
        first_wbig = [True]
        prenorm(0, nt == 1 and sample)
        for t in range(nt):
            has_s = t == nt - 1 and sample
            ffn(0, 1, has_s)
            load_wbig(0 if t + 1 < nt else 1)
            prenorm(2, has_s)
            w_in(has_s)
            DMA(gsI[t], IGs[:], ['IGs'], ['gsI'], key='sp_gI')
            DMA(gsF[t], FGs[:], ['FGs'], ['gsF'], key='sp_gF')
            DMA(x1s[t], X[:].rearrange('p g d -> p (g d)'), [('X', g) for g in range(5)], ['x1s'], key='sp_x')
            DMA(zs[t, :, 0:ZW], big[:, 0:ZW], zkeys(), ['zs'], key='sp_z')
            if t + 1 < nt:
                nhs = t + 1 == nt - 1 and sample
                DMA(X[:, 0:4, :], xp[(t + 1) * TP:(t + 2) * TP, :].rearrange('(g p) d -> p g d', p=128), (), [('X', g) for g in range(4)], key='x_in')
                if nhs:
                    DMA(X[0:64, 4, :], xs[:, :], (), [('X', 4)], key='x_in_s')
                prenorm(0, nhs)
            gates(False, phase=1)
            for g in range(4):
                mlstm_state_only(g, g)
            if has_s:
                DMA(pk[:, :], kvf[:, 0, 0:128], [('kvf', 3)], ['o_pk'], key='o_pkv')
                DMA(pv[:, :], kvf[:, 0, 128:256], [('kvf', 3)], ['o_pv'], key='o_pkv')
                for b in range(NB):
                    DMA(sk[b, 124:128, :], kvf[4 * b:4 * b + 4, 1, 0:128], [('kvf', 4)], [('o_sk2', b)], key='o_pkv')
                    DMA(sv[b, 124:128, :], kvf[4 * b:4 * b + 4, 1, 128:256], [('kvf', 4)], [('o_sv2', b)], key='o_pkv')
        pay = tmpn[:, 0:PW]
        CP(pay[:, 0:516], Cst[:].rearrange('p h c -> p (h c)'), ['Cst'], ['tmpn'])
        MSET(pay[:, 518:520], 0.0, ['tmpn'])
        CP(pay[:, 516:517], MUprev[:, 0:1], ['MUprev'], ['tmpn'])
        CP(pay[:, 517:518], Bprev[:, 0:1], ['Bprev'], ['tmpn'])
        CP(pay[:, 520:776].bitcast(BF16).rearrange('p (v t) -> p v t', t=128), kaT[:, :, TP:TP + 128], zkeys(), ['tmpn'])
        CP(pay[:, 776:840].bitcast(BF16), va[:, 4, :], zkeys(), ['tmpn'])
        DMA(exin[:, :], pay, ['tmpn'], ['exin'], key='ex_in')

        def is_tail(k):
            return k == 'vones' or (isinstance(k, tuple) and k[0] == 'z' and k[1] in ('ktok', 'vaug'))

        def reload_head(t):
            hk_ = [k for k in zkeys() if not is_tail(k)]
            isqk = lambda k: isinstance(k, tuple) and k[0] in ('qT', 'kT')
            isht = lambda k: isinstance(k, tuple) and k[0] == 'hT'
            QK = 8 * TT_
            DMA(big[:, 0:QK], zs[t, :, 0:QK], ['zs'], [k for k in hk_ if isqk(k) or isht(k)], key='rl_z')
            DMA(big[:, QK:ZT], zs[t, :, QK:ZT], ['zs'], [k for k in hk_ if not isqk(k)], key='rl_z2')

        def reload_tail(t):
            DMA(big[:, ZT:ZW], zs[t, :, ZT:ZW], ['zs'], [k for k in zkeys() if is_tail(k)] + ['kw'], key='rl_zt')

        def reload_x(t):
            DMA(X[:].rearrange('p g d -> p (g d)'), x1s[t], ['x1s'], [('X', g) for g in range(5)], key='rl_x')

        def reload(t):
            reload_head(t)
            reload_tail(t)
            reload_x(t)

        def reload_g(t):
            DMA(IGs[:], gsI[t], ['gsI'], ['IGs'], key='rl_gI')
            DMA(FGs[:], gsF[t], ['gsF'], ['FGs'], key='rl_gF')
        reload(0)
        reload_g(0)
        S.op('pool', lambda e: e.collective_compute('AllGather', ALU.bypass, replica_groups=[[0, 1, 2, 3], [4, 5, 6, 7]], ins=[exin.ap().opt()], outs=[exout.ap().opt()]), ['exin'], ['exout'], dma_key='cc', inc=1)
        for g_ in (1, 2, 3):
            drain(swa_group(g_, False))
        MSET(Cst[:], 0.0, ['Cst'])
        MSET(cmb[:, 0:1], 0.0, ['cmb'])
        MSET(kaTh[:], 0.0, ['kaTh'])
        MSET(vah[:], 0.0, ['vah'])
        payr = Ssb[:].rearrange('p h k -> p (h k)')[:, 0:PW]
        for r in range(3):
            DMA(payr, exout[r * 128:(r + 1) * 128, :], ['exout'], ['Ssb'], key='ex_rd')
            mk_ = role[:, r:r + 1]
            TS(cmb[:, 1:2], payr[:, 517:518], mk_, None, ALU.mult, None, ['Ssb', 'role'], ['cmb'])
            TTo(cmb[:, 2:3], payr[:, 516:517], payr[:, 517:518], ALU.subtract, ['Ssb'], ['cmb'])
            TS(cmb[:, 2:3], cmb[:, 2:3], -NEG, mk_, ALU.add, ALU.mult, ['cmb', 'role'], ['cmb'])
            TS(cmb[:, 2:3], cmb[:, 2:3], NEG, None, ALU.add, None, ['cmb'], ['cmb'])
            TTo(cmb[:, 3:4], cmb[:, 0:1], cmb[:, 1:2], ALU.subtract, ['cmb'], ['cmb'])
            TTo(cmb[:, 4:5], cmb[:, 3:4], cmb[:, 2:3], ALU.max, ['cmb'], ['cmb'])
            TTo(cmb[:, 5:6], cmb[:, 3:4], cmb[:, 4:5], ALU.subtract, ['cmb'], ['cmb'])
            TTo(cmb[:, 6:7], cmb[:, 2:3], cmb[:, 4:5], ALU.subtract, ['cmb'], ['cmb'])
            ACT(cmb[:, 7:9], cmb[:, 5:7], AF.Exp, ['cmb'], ['cmb'])
            TS(cmb[:, 8:9], cmb[:, 8:9], mk_, None, ALU.mult, None, ['cmb', 'role'], ['cmb'])
            CP(cmb[:, 0:1], cmb[:, 4:5], ['cmb'], ['cmb'])
            pd = bank(1)
            for h in range(4):
                MM(pd[:, 2 * h:2 * h + 2], Esel[0:4, h, :], cmb[0:4, 7:9], True, True, ['Esel', 'cmb'], bkeys(1))
            CP(ABt[:].rearrange('p h c -> p (h c)'), pd[:, 0:8], bkeys(1), ['ABt'])
            TTo(Cst[:], Cst[:], V(ABt[:, 0, 0:1], [[2, 4], [0, 129]]), ALU.mult, ['Cst', 'ABt'], ['Cst'])
            TTo(tmpo[:], payr[:, 0:516].rearrange('p (h c) -> p h c', c=129), V(ABt[:, 0, 1:2], [[2, 4], [0, 129]]), ALU.mult, ['Ssb', 'ABt'], ['tmpo'])
            TTo(Cst[:], Cst[:], tmpo[:], ALU.add, ['Cst', 'tmpo'], ['Cst'])
            STT(kaTh[:].rearrange('p v t -> p (v t)'), payr[:, 520:776].bitcast(BF16), role[:, 8 + r:9 + r], kaTh[:].rearrange('p v t -> p (v t)'), ALU.mult, ALU.add, ['Ssb', 'role', 'kaTh'], ['kaTh'])
            STT(vah[:], payr[:, 776:840].bitcast(BF16), role[:, 8 + r:9 + r], vah[:], ALU.mult, ALU.add, ['Ssb', 'role', 'vah'], ['vah'])
        CP(MUprev[:, 0:1], cmb[:, 0:1], ['cmb'], ['MUprev'])
        MSET(Bprev[:], 0.0, ['Bprev'])
        CP(Cbf[:], Cst[:], ['Cst'], ['Cbf'], eng='act')
        for t in range(nt):
            has_s = t == nt - 1 and sample
            grp = groups_of(has_s)
            if t > 0:
                reload_x(t)
            CP(kaT[:, :, 0:128], kaTh[:], ['kaTh'], ['kaT_h'], eng='act')
            CP(va[:, 0, :], vah[:], ['vah'], ['va0'], eng='act')
            if t == 0:
                gates(has_s, phase=2)
            for g, npp, c0 in grp:
                gens_ = [mlstm_group(g, npp, c0, g, g == 4)]
                if g < 4:
                    if t > 0 or g == 0:
                        gens_.append(swa_group(g, t == 0 and g == 0))
                else:
                    gens_.append(swa_sample())
                run_rr(gens_, [2, 3] if len(gens_) == 2 and g < 4 else None)
            CP(kaTh[:], kaT[:, :, TP:TP + 128], [('kaT', 0)], ['kaTh'], eng='act')
            CP(vah[:], va[:, 4, :], [('va', 3)], ['vah'], eng='act')
            if t + 1 < nt:
                reload_tail(t + 1)
                reload_g(t + 1)
                gates(t + 1 == nt - 1 and sample, phase=2)
            w_out_stage(has_s)
            prenorm(4, has_s)
            ffn(1, 5, has_s)
            if t + 1 < nt:
                load_wbig(1)
                reload_head(t + 1)
            DMA(yp[t * TP:(t + 1) * TP, :].rearrange('(g p) d -> p g d', p=128), X[:, 0:4, :], [('X', g) for g in range(4)], ['o_yp'], key='o_y')
            if has_s:
                DMA(ys[:, :], X[0:64, 4, :], [('X', 4)], ['o_ys'], key='o_y')
        DMA(pC.rearrange('h k v -> k h v'), Cst[:, :, 0:128], ['Cst'], ['o_pC'], key='o_fin')
        TR(bank(1)[0:4, 0:128], Cst[:, :, 128], ident[:, :], ['Cst', 'ident'], bkeys(1))
        CP(pn_sb[:], bank(1)[0:4, 0:128], bkeys(1), ['pn_sb'])
        DMA(pn[:, :], pn_sb[:], ['pn_sb'], ['o_pn'], key='o_fin')
        TTo(pm_sb[:], MUprev[:], Bprev[:], ALU.subtract, ['MUprev', 'Bprev'], ['pm_sb'])
        DMA(pm[:, :], pm_sb[0:4, :], ['pm_sb'], ['o_pm'], key='o_fin')
        TR(bank(1)[0:64, 128:256], nTout[:, :], ident[:, :], ['nTout', 'ident'], bkeys(1))
        CP(snout[:], bank(1)[0:64, 128:256], bkeys(1), ['snout'])
        DMA(sno[:, :], snout[:], ['snout'], ['o_sno'], key='o_fin')
        out_keys = [k for k in S.dma_counts if k.startswith('o_')]
        S.emit(final_wait_keys=out_keys)
    return nc

def _consts():
    c = {}
    c['c_ident'] = np.eye(128, dtype=np.float32)
    s = np.arange(128)
    c['c_maskp'] = (s[:, None] <= s[None, :]).astype(np.float32)
    s = np.arange(64)
    c['c_masks'] = ((s[:, None] <= s[None, :]) & (s[:, None] // 4 == s[None, :] // 4)).astype(np.float32)
    slopes = np.exp2(-8.0 * np.arange(1, 9, dtype=np.float32) / 8).astype(np.float32)
    c['c_slope'] = np.broadcast_to(slopes[None, :], (128, 8)).copy()
    BIGN = -8000000.0
    qi = np.arange(128)[:, None]
    kj = np.arange(256)[None, :]
    dist = 128 + qi - kj
    valid = (dist >= 0) & (dist < 128)
    c['c_bias'] = np.where(valid, -dist.astype(np.float32), BIGN).astype(np.float32)
    t = np.arange(4)[:, None]
    j = np.arange(128)[None, :]
    d = 128 + t - j
    v = (d >= 0) & (d < 128)
    c['c_bsc'] = np.where(v, -d.astype(np.float32), BIGN).astype(np.float32)
    x = np.arange(124)[None, :] - 60
    d2 = t - x
    v2 = (x >= 0) & (x <= t)
    c['c_tb'] = np.where(v2, -d2.astype(np.float32), BIGN).astype(np.float32)
    bm = (np.arange(64)[None, :] // 4 == np.arange(NB)[:, None]).astype(np.float32)
    c['c_bm'] = np.broadcast_to(bm.reshape(1, NB * 64), (128, NB * 64)).copy()
    c['c_bmT'] = bm.T.copy()
    E = np.zeros((4, 4, 128), np.float32)
    for h in range(4):
        E[h, h, :] = 1.0
    c['c_E'] = E.reshape(4, 4 * 128)
    return c
_NC = None

def kernel(x_prompt, x_sample, cache_swa_k, cache_swa_v, state_mlstm_C, state_mlstm_n, state_mlstm_m, norm_gains, ffn_w_gate, ffn_w_up, ffn_w_down, w_in, b_gate, mlstm_norm_gain, attn_sinks, w_out):
    global _NC
    f = lambda a: np.ascontiguousarray(np.asarray(a, dtype=np.float32))
    x_prompt, x_sample = (f(x_prompt), f(x_sample))
    ckk, cvv = (f(cache_swa_k)[0], f(cache_swa_v)[0])
    sCC, snn, smm = (f(state_mlstm_C)[0], f(state_mlstm_n)[0], f(state_mlstm_m)[0])
    consts = _consts()
    shared = dict(gains=f(norm_gains)[0], wg=f(ffn_w_gate)[0], wu=f(ffn_w_up)[0], wd=f(ffn_w_down)[0], win=f(w_in)[0], bgate=f(b_gate)[0], mng=f(mlstm_norm_gain)[0], sinks=f(attn_sinks)[0], wout=f(w_out)[0])
    shared.update(consts)
    in_maps = []
    for c in range(8):
        m = dict(shared)
        m['xp'] = np.ascontiguousarray(x_prompt[c // 4, SEQ * (c % 4):SEQ * (c % 4 + 1)])
        role = np.zeros((128, 17), np.float32)
        for r in range(4):
            if r < c % 4:
                role[:, r] = 1.0
            if r == c % 4 - 1:
                role[:, 8 + r] = 1.0
        role[:, 16] = NEG if c % 4 == 0 else 0.0
        m['c_role'] = role
        b0 = NB * c
        m['xs'] = x_sample[b0:b0 + NB].reshape(NS, D)
        m['ck'] = ckk[b0:b0 + NB].reshape(NB, 128, 128)
        m['cv'] = cvv[b0:b0 + NB].reshape(NB, 128, 128)
        m['sC'] = sCC[b0:b0 + NB]
        m['sn'] = snn[b0:b0 + NB].reshape(NB * 4, 128)
        m['sm'] = smm[b0:b0 + NB]
        in_maps.append(m)
    if _NC is None:
        _NC = build_program()
    res = run_bass_kernel_spmd(_NC, in_maps, core_ids=list(range(8)))
    r = res.results
    yp = np.stack([np.concatenate([r[4 * b + j]['yp'] for j in range(4)], 0) for b in range(2)], 0)
    ys = np.concatenate([r[c]['ys'].reshape(NB, 4, D) for c in range(8)], 0)
    pk = np.stack([r[3]['pk'], r[7]['pk']], 0).reshape(1, 2, 128, 2, 64)
    pv = np.stack([r[3]['pv'], r[7]['pv']], 0).reshape(1, 2, 128, 2, 64)
    pC = np.stack([r[3]['pC'], r[7]['pC']], 0)[None]
    pn = np.stack([r[3]['pn'], r[7]['pn']], 0)[None]
    pm = np.stack([r[3]['pm'].reshape(4), r[7]['pm'].reshape(4)], 0)[None]
    sk = np.concatenate([r[c]['sk'] for c in range(8)], 0).reshape(1, 128, 128, 2, 64)
    sv = np.concatenate([r[c]['sv'] for c in range(8)], 0).reshape(1, 128, 128, 2, 64)
    sCo = np.concatenate([r[c]['sCo'] for c in range(8)], 0)[None]
    sno = np.concatenate([r[c]['sno'].reshape(NB, 4, 128) for c in range(8)], 0)[None]
    smo = np.concatenate([r[c]['smo'] for c in range(8)], 0)[None]
    outs = (yp, ys, pk, pv, pC, pn, pm, sk, sv, sCo, sno, smo)
    return tuple((np.ascontiguousarray(o, dtype=np.float32) for o in outs))
```

```python
import contextlib
import os
import numpy as np
import concourse.bass as bass
import concourse.mybir as mybir
from concourse.bass_utils import run_bass_kernel_spmd
F32 = mybir.dt.float32
BF16 = mybir.dt.bfloat16
ALU = mybir.AluOpType
AF = mybir.ActivationFunctionType
AX = mybir.AxisListType
ENGS = ('pe', 'act', 'dve', 'pool', 'sp')
D = 1024
DFF = 2816
NJ = 22
DIN = 2824
SEQ = 2048
NTILE = 4
PW = 840
TP = 512
NS = 64
NB = 16
EPS = 1e-06
NEG = -30000.0

class _Op:
    __slots__ = ('eng', 'fn', 'deps', 'dma_key', 'dma_cnt', 'signal', 'sig_val', 'idx', 'inc')

class Sched:
    def __init__(self, nc):
        self.nc = nc
        self.ops = []
        self.last_writer = {}
        self.readers = {}
        self.dma_counts = {}
    ALIAS = {'e1': 'FGs', 'lfn': 'FGs', 'Fst': 'tg', 'Ug': 'IGs', 't5': 'tmpo'}

    def _norm(self, k):
        if isinstance(k, tuple) and k[0] in ('IGs', 'FGs'):
            k = k[0]
        return self.ALIAS.get(k, k) if not isinstance(k, tuple) else k

    def op(self, eng, fn, reads=(), writes=(), dma_key=None, inc=16):
        reads = [self._norm(k) for k in reads]
        writes = [self._norm(k) for k in writes]
        writes = writes + [k for k in reads if isinstance(k, tuple) and k[0] == 'bank']
        reads = [k for k in reads if not (isinstance(k, tuple) and k[0] == 'bank')]
        o = _Op()
        o.eng, o.fn, o.idx, o.dma_key, o.inc = (eng, fn, len(self.ops), dma_key, inc)
        o.signal, o.sig_val = (False, None)
        deps = set()
        for r in reads:
            w = self.last_writer.get(r)
            if w is not None:
                deps.add(w)
        for r in writes:
            w = self.last_writer.get(r)
            if w is not None:
                deps.add(w)
            deps.update(self.readers.get(r, ()))
        o.deps = deps
        if dma_key is not None:
            self.dma_counts[dma_key] = self.dma_counts.get(dma_key, 0) + inc
            o.dma_cnt = self.dma_counts[dma_key]
        else:
            o.dma_cnt = None
        self.ops.append(o)
        for r in reads:
            self.readers.setdefault(r, []).append(o.idx)
        for r in writes:
            self.last_writer[r] = o.idx
            self.readers[r] = []
        return o.idx

    def emit(self, final_wait_keys=()):
        nc, ops = (self.nc, self.ops)
        for o in ops:
            nd = set()
            for d in o.deps:
                p = ops[d]
                if p.dma_key is None and o.dma_key is None and (p.eng == o.eng == 'pe'):
                    continue
                nd.add(d)
            o.deps = nd
            for d in nd:
                if ops[d].dma_key is None:
                    ops[d].signal = True
        cnt = {e: 0 for e in ENGS}
        for o in ops:
            if o.dma_key is None and o.signal:
                cnt[o.eng] += 1
                o.sig_val = cnt[o.eng]
        with contextlib.ExitStack() as st:
            esem = {e: st.enter_context(nc.semaphore('s_' + e)) for e in ENGS}
            dsem = {}
            for i, k in enumerate(self.dma_counts):
                dsem[k] = st.enter_context(nc.semaphore('d_%d' % i))
            block = st.enter_context(nc.Block())

            def run(ename):

                def body(eng):
                    waited = {}
                    for o in ops:
                        if o.eng != ename:
                            continue
                        need = {}
                        for d in o.deps:
                            p = ops[d]
                            if p.dma_key is not None:
                                s, v = (dsem[p.dma_key], p.dma_cnt)
                            else:
                                s, v = (esem[p.eng], p.sig_val)
                            if need.get(id(s), (None, 0))[1] < v:
                                need[id(s)] = (s, v)
                        for key, (s, v) in need.items():
                            if waited.get(key, 0) < v:
                                eng.wait_ge(s, v)
                                waited[key] = v
                        ins = o.fn(eng)
                        if o.dma_key is not None:
                            ins.then_inc(dsem[o.dma_key], o.inc)
                        elif o.signal:
                            ins.then_inc(esem[ename], 1)
                    if ename == 'sp':
                        for k in final_wait_keys:
                            eng.wait_ge(dsem[k], self.dma_counts[k])
                return body
            block.tensor(run('pe'))
            block.scalar(run('act'))
            block.vector(run('dve'))
            block.gpsimd(run('pool'))
            block.sync(run('sp'))

def drain(gen):
    try:
        while True:
            next(gen)
    except StopIteration as e:
        return e.value

def run_rr(gens, steps=None):
    gens = list(gens)
    steps = dict(zip(map(id, gens), steps or [1] * len(gens)))
    while gens:
        for g_ in list(gens):
            try:
                for _ in range(steps[id(g_)]):
                    next(g_)
            except StopIteration:
                gens.remove(g_)

def V(ap, dims):
    return bass.AP(ap.tensor, ap.offset, [list(ap.ap[0])] + [list(d) for d in dims])

def build_program(nt=NTILE, upto=9, sample=True):
    assert nt == NTILE
    nc = bass.Bass('TRN2', target_bir_lowering=False)

    def din(name, shape, dt=F32):
        return nc.dram_tensor(name, list(shape), dt, kind='ExternalInput').ap()

    def dout(name, shape, dt=F32):
        return nc.dram_tensor(name, list(shape), dt, kind='ExternalOutput').ap()
    xp = din('xp', [SEQ, D])
    xs = din('xs', [NS, D])
    ck = din('ck', [NB, 128, 128])
    cv = din('cv', [NB, 128, 128])
    sC = din('sC', [NB, 4, 128, 128])
    sn = din('sn', [NB * 4, 128])
    sm = din('sm', [NB, 4])
    gains = din('gains', [6, D])
    wg = din('wg', [2, D, DFF])
    wu = din('wu', [2, D, DFF])
    wd = din('wd', [2, DFF, D])
    win = din('win', [D, DIN])
    bgate = din('bgate', [8])
    mng = din('mng', [512])
    sinks = din('sinks', [8])
    wout = din('wout', [D, D])
    c_ident = din('c_ident', [128, 128])
    c_maskp = din('c_maskp', [128, 128])
    c_masks = din('c_masks', [64, 64])
    c_bias = din('c_bias', [128, 256])
    c_slope = din('c_slope', [128, 8])
    c_bsc = din('c_bsc', [4, 128])
    c_tb = din('c_tb', [4, 124])
    c_bm = din('c_bm', [128, NB * 64])
    c_bmT = din('c_bmT', [64, NB])
    c_E = din('c_E', [4, 4 * 128])
    c_role = din('c_role', [128, 17])
    x1s = nc.dram_tensor('x1s', [NTILE, 128, 5 * D], F32).ap()
    zs = nc.dram_tensor('zs', [NTILE, 128, 18304], BF16).ap()
    gsI = nc.dram_tensor('gsI', [NTILE, 128, TP + NS], F32).ap()
    gsF = nc.dram_tensor('gsF', [NTILE, 128, TP + NS], F32).ap()
    exin = nc.dram_tensor('exin', [128, PW], F32)
    exout = nc.dram_tensor('exout', [4 * 128, PW], F32)
    yp = dout('yp', [SEQ, D])
    ys = dout('ys', [NS, D])
    pk = dout('pk', [128, 128])
    pv = dout('pv', [128, 128])
    pC = dout('pC', [4, 128, 128])
    pn = dout('pn', [4, 128])
    pm = dout('pm', [4, 1])
    sk = dout('sk', [NB, 128, 128])
    sv = dout('sv', [NB, 128, 128])
    sCo = dout('sCo', [NB, 4, 128, 128])
    sno = dout('sno', [NB * 4, 128])
    smo = dout('smo', [NB, 4])
    S = Sched(nc)
    out_keys = []
    with contextlib.ExitStack() as st:

        def sb(name, shape, dt=F32):
            return st.enter_context(nc.sbuf_tensor(name, list(shape), dt))
        TT_ = TP + NS
        X = sb('X', [128, 5, D])
        hnT = sb('hnT', [128, 8, TT_], BF16)
        big = sb('big', [128, 18304], BF16)
        wblk = [sb('wblk%d' % i, [128, 8, 256], BF16) for i in range(4)]
        wbig = sb('wbig', [128, NJ, D], BF16)
        gT = sb('gT', [128, 6, 8])
        gp = sb('gp', [128, 1, D])
        ident = sb('ident', [128, 128])
        identb = sb('identb', [128, 128], BF16)
        maskp = sb('maskp', [128, 128])
        masks = sb('masks', [64, 64])
        biasT = sb('biasT', [128, 1, 256])
        slopeb = sb('slopeb', [128, 8])
        bsc = sb('bsc', [4, 1, 128])
        tbl = sb('tbl', [4, 1, 124])
        bmb = sb('bmb', [128, NB, 64], BF16)
        bmT = sb('bmT', [64, NB])
        Esel = sb('Esel', [4, 4, 128])
        mngb = sb('mngb', [128, 512])
        sinkb = sb('sinkb', [128, 8])
        bi_l = sb('bi_l', [128, 1])
        nbf_l = sb('nbf_l', [128, 1])
        tmpn = sb('tmpn', [128, D])
        stt = sb('stt', [128, 8])
        xn = sb('xn', [128, D], BF16)
        sg = [sb('sg%d' % i, [128, 512]) for i in range(1)]
        IGs = sb('IGs', [128, TT_])
        FGs = sb('FGs', [128, TT_])
        e1 = FGs
        lfn = FGs
        Bneg = sb('Bneg', [128, TT_])
        Ug = IGs
        MU = sb('MU', [128, TP + 1])
        MUs = sb('MUs', [128, NB, 5])
        tg = sb('tg', [128, TT_])
        Fst = tg
        Bprev = sb('Bprev', [128, 1])
        MUprev = sb('MUprev', [128, 1])
        dd = sb('dd', [4, 4])
        dds = sb('dds', [4, NB])
        DECs = sb('DECs', [128, 4, 4])
        DECss = sb('DECss', [128, 4, NB])
        FT = sb('FT', [128, 5, 16])
        smin = sb('smin', [128, NB])
        smout = sb('smout', [128, NB])
        Cst = sb('Cst', [128, 4, 129])
        Cbf = sb('Cbf', [128, 4, 129], BF16)
        Sp = sb('Sp', [128, 4, 128], BF16)
        kw = sb('kw', [128, 4, 128], BF16)
        kwm = sb('kwm', [64, 4, 128], BF16)
        tmpo = sb('tmpo', [128, 4, 129])
        ND = sb('ND', [128, 4, 129])
        q5 = sb('q5', [128, 8, 4])
        og = sb('og', [128, 512], BF16)
        t5 = tmpo
        U = sb('U', [128, 5, D], BF16)
        Ssb = sb('Ssb', [128, 4, 256])
        Pb = sb('Pb', [128, 4, 256], BF16)
        PTs = sb('PTs', [128, 8, 128], BF16)
        sst = sb('sst', [128, 8, 8])
        kvf = sb('kvf', [128, 2, 256])
        qTm = sb('qTm', [128, 2, 4, 64], BF16)
        Cb = [sb('Cb%d' % i, [128, 4, 129]) for i in range(2)]
        Cbb = [sb('Cbb%d' % i, [128, 4, 129], BF16) for i in range(2)]
        snin = sb('snin', [64, 128])
        nTin = sb('nTin', [128, 64])
        nTout = sb('nTout', [128, 64])
        snout = sb('snout', [64, 128])
        ckb = [sb('ckb%d' % i, [128, 4, 128], BF16) for i in range(2)]
        kTc = [sb('kTc%d' % i, [128, 4, 128], BF16) for i in range(2)]
        cvb = [sb('cvb%d' % i, [128, 128], BF16) for i in range(2)]
        PTss = sb('PTss', [128, 8, 2, 4], BF16)
        uab = [sb('uab%d' % i, [4, 512], BF16) for i in range(2)]
        pn_sb = sb('pn_sb', [4, 128])
        pm_sb = sb('pm_sb', [128, 1])
        kaTh = sb('kaTh', [128, 4, 128], BF16)
        vah = sb('vah', [128, 128], BF16)
        role = sb('role', [128, 17])
        cmb = sb('cmb', [128, 12])
        ABt = sb('ABt', [128, 4, 2])
        hT = big[:, 0:NJ * TT_].rearrange('p (j t) -> p j t', t=TT_)
        o_ = [0]

        def carve(n):
            a = big[:, o_[0]:o_[0] + n]
            o_[0] += n
            return a
        qT = carve(4 * TT_).rearrange('p (h t) -> p h t', t=TT_)
        kT = carve(4 * TT_).rearrange('p (h t) -> p h t', t=TT_)
        osig = carve(5 * 512).rearrange('p (g c) -> p g c', c=512)
        qaT = carve(4 * TT_).rearrange('p (h t) -> p h t', t=TT_)
        KW = 128 + TT_
        kaT = carve(4 * KW).rearrange('p (h t) -> p h t', t=KW)
        va = carve(6 * 128).rearrange('p (g c) -> p g c', c=128)
        assert o_[0] >= NJ * TT_
        ZT = o_[0]
        ktok = carve(5 * 512).rearrange('p (g c) -> p g c', c=512)
        vaug = carve(5 * 4 * 130).rearrange('p (g h c) -> p g h c', h=4, c=130)
        assert o_[0] <= 18304
        ps = st.enter_context(nc.psum_tensor('ps', [128, 8, 512], F32))

        def bank(i, n=1):
            return ps[:, i:i + n, :].rearrange('p a b -> p (a b)')

        def bkeys(i, n=1):
            return [('bank', i + k) for k in range(n)]

        def MM(out, lhsT, rhs, start, stop, R, W, **kw_):
            S.op('pe', lambda e: e.matmul(out, lhsT=lhsT, rhs=rhs, start=start, stop=stop, **kw_), R, W)

        def TR(out, in_, idn, R, W):
            S.op('pe', lambda e: e.transpose(out=out, in_=in_, identity=idn), R, W)

        def ACT(out, in_, func, R, W, **kw_):
            S.op('act', lambda e: e.activation(out=out, in_=in_, func=func, **kw_), R, W)

        def TTo(out, in0, in1, op, R, W, eng='dve'):
            S.op(eng, lambda e: e.tensor_tensor(out=out, in0=in0, in1=in1, op=op), R, W)

        def STT(out, in0, scalar, in1, op0, op1, R, W, eng='dve'):
            S.op(eng, lambda e: e.scalar_tensor_tensor(out=out, in0=in0, scalar=scalar, in1=in1, op0=op0, op1=op1), R, W)

        def TS(out, in0, s1, s2, op0, op1, R, W, eng='dve'):
            if s2 is None:
                S.op(eng, lambda e: e.tensor_scalar(out=out, in0=in0, scalar1=s1, scalar2=None, op0=op0), R, W)
            else:
                S.op(eng, lambda e: e.tensor_scalar(out=out, in0=in0, scalar1=s1, scalar2=s2, op0=op0, op1=op1), R, W)

        def CP(out, in_, R, W, eng='dve'):
            if eng == 'act':
                S.op('act', lambda e: e.copy(out=out, in_=in_), R, W)
            else:
                S.op(eng, lambda e: e.tensor_copy(out=out, in_=in_), R, W)

        def RCP(out, in_, R, W):
            S.op('dve', lambda e: e.reciprocal(out=out, in_=in_), R, W)

        def RED(out, in_, op, R, W):
            S.op('dve', lambda e: e.tensor_reduce(out=out, in_=in_, axis=AX.X, op=op), R, W)

        def SCAN(out, d0, init, op0, R, W):
            S.op('dve', lambda e: e.tensor_tensor_scan(out=out, data0=d0, data1=d0, initial=init, op0=op0, op1=ALU.bypass), R, W)

        def MSET(ap, val, W, eng='dve'):
            S.op(eng, lambda e: e.memset(ap, val), (), W)
        dctr = [0]
        nodma = [False]

        def DMA(out, in_, R, W, key=None, eng='sp', slow=False):
            if nodma[0] and eng == 'pool' and (key is not None) and key.startswith('w_'):
                return key
            if key is None:
                dctr[0] += 1
                key = 'dk%d' % (dctr[0] % 24)
            if slow:
                S.op(eng, lambda e: e.dma_start(out=out, in_=in_, allow_slow_non_contiguous=True), R, W, dma_key=key)
            else:
                S.op(eng, lambda e: e.dma_start(out=out, in_=in_), R, W, dma_key=key)
            return key

        DMA(X[:, 0:4, :], xp[0:TP, :].rearrange('(g p) d -> p g d', p=128), (), [('X', g) for g in range(4)], key='x_in')

        def LD(t, src, name):
            DMA(t, src, (), [name], key='c_' + name)
        LD(ident[:], c_ident[:, :], 'ident')
        LD(maskp[:], c_maskp[:, :], 'maskp')
        LD(masks[:], c_masks[:, :], 'masks')
        LD(biasT[:].rearrange('p h k -> p (h k)'), c_bias[:, :], 'biasT')
        LD(slopeb[:], c_slope[:, :], 'slopeb')
        LD(bsc[:].rearrange('p h k -> p (h k)'), c_bsc[:, :], 'bsc')
        LD(tbl[:].rearrange('p h k -> p (h k)'), c_tb[:, :], 'tbl')
        DMA(bmb[:].rearrange('p b t -> p (b t)'), c_bm[:, :], (), ['bmb'], key='c_bmb', eng='pool')
        LD(bmT[:], c_bmT[:, :], 'bmT')
        LD(Esel[:].rearrange('p h k -> p (h k)'), c_E[:, :], 'Esel')
        LD(role[:], c_role[:, :], 'role')
        CP(identb[:], ident[:], ['ident'], ['identb'])
        DMA(tmpn[0:6, :], gains[:, :], (), ['tmpn'], key='c_gT')
        pg_ = bank(1)
        for c in range(8):
            TR(pg_[:, 6 * c:6 * c + 6], tmpn[0:6, c * 128:(c + 1) * 128], ident[0:6, 0:6], ['tmpn', 'ident'], bkeys(1))
        CP(gT[:], V(pg_[:, 0:1], [[1, 6], [6, 8]]), bkeys(1), ['gT'])

        def load_gp(gi):
            DMA(gp[:, 0, :], bass.AP(gains.tensor, gains[gi, :].offset, [[0, 128], [1, D]]), (), ['gp'], key='c_gp')
        DMA(mngb[:], bass.AP(mng.tensor, mng.offset, [[0, 128], [1, 512]]), (), ['mngb'], key='c_mngb')
        DMA(sinkb[:], bass.AP(sinks.tensor, sinks.offset, [[0, 128], [1, 8]]), (), ['sinkb'], key='c_sinkb')
        MSET(bi_l[:], 0.0, ['bi_l'])
        MSET(nbf_l[:], 0.0, ['nbf_l'])
        MSET(smin[:], 0.0, ['smin'])
        for f in range(4):
            DMA(bi_l[32 * f:32 * f + 4, :], bgate[0:4].rearrange('(p o) -> p o', o=1), ['bi_l'], [('bi_l', f)], key='c_bil', slow=True)
            DMA(nbf_l[32 * f:32 * f + 4, :], bgate[4:8].rearrange('(p o) -> p o', o=1), ['nbf_l'], [('nbf_l', f)], key='c_bil', slow=True)
            DMA(smin[32 * f:32 * f + 4, :], sm.rearrange('b h -> h b'), ['smin'], [('smin', f)], key='c_smin', slow=True)
        lane_keys = [(nm_, f) for nm_ in ('bi_l', 'nbf_l', 'smin') for f in range(4)]
        neg_done = [False]
        MSET(big[:], 0.0, ['kaT_h', 'va0', 'vones'], eng='pool')
        MSET(IGs[:], 0.0, ['IGs'], eng='pool')
        MSET(FGs[:], 0.0, ['FGs'], eng='pool')
        MSET(X[:, 4, :], 0.0, [('X', 4)], eng='pool')
        MSET(Cst[:], 0.0, ['Cst'])
        MSET(Cbf[:], 0.0, ['Cbf'], eng='pool')
        MSET(Bprev[:], 0.0, ['Bprev'])
        MSET(MUprev[:], NEG, ['MUprev'])
        MSET(kaT[:, :, 0:128], 0.0, ['kaT_h'], eng='pool')
        MSET(va[:, 0, :], 0.0, ['va0'], eng='pool')
        MSET(vaug[:, :, :, 128:129], 1.0, ['vones'], eng='pool')
        for i in range(2):
            MSET(ckb[i][:], 0.0, [('ckb', i)], eng='pool')
        DMA(snin[:], sn[:, :], (), ['snin'], key='c_snin')
        TR(bank(1)[:, 0:64], snin[:, :], ident[0:64, 0:64], ['snin', 'ident'], bkeys(1))
        CP(nTin[:], bank(1)[:, 0:64], bkeys(1), ['nTin'])
        wq = []
        wslot = [0]

        extra_keys = {}

        def wload(src_fn):
            s = wslot[0] % 4
            wslot[0] += 1
            rk = 'wblk%d' % s
            extra_keys.pop(rk, None)
            src_fn(wblk[s], rk, 'w_slot%d' % s)
            return (wblk[s], rk)

        def colblk(Wap, c0, ncols):

            def f(slot, rk, key):
                DMA(slot[:, :, 0:ncols], Wap[:, c0:c0 + ncols].rearrange('(kc p) c -> p kc c', p=128), (), [rk], key=key, eng='pool')
            return f

        def colparts(Wap, parts):

            def f(slot, rk, key):
                extra_keys[rk] = [(rk, pi) for pi in range(1, len(parts))]
                for pi, (d0, c0, ncols) in enumerate(parts):
                    DMA(slot[:, :, d0:d0 + ncols], Wap[:, c0:c0 + ncols].rearrange('(kc p) c -> p kc c', p=128),
                        () if pi == 0 else [rk], [rk] if pi == 0 else [(rk, pi)], key=key if pi == 0 else key + '_p%d' % pi, eng='pool')
            return f

        def load_wbig(f):
            for j2 in range(11):
                DMA(wbig[:, 2 * j2:2 * j2 + 2, :], wd[f, 256 * j2:256 * j2 + 256, :].rearrange('(j p) c -> p j c', p=128), (), [('wbig', j2)], key='w_big%d' % j2, eng='pool')
        rot = {}

        def R2(name, n=2):
            rot[name] = (rot.get(name, -1) + 1) % n
            return rot[name]

        def groups_of(has_s):
            gs = [(g, 128, g * 128) for g in range(4)]
            if has_s:
                gs.append((4, 64, TP))
            return gs

        def prenorm(gi, has_s):
            for g, npp, c0 in groups_of(has_s):
                Xg = ('X', g)
                MSET(stt[:npp, 0:1], 0.0, ['stt'])
                ACT(xn[:npp, :], X[:npp, g, :], AF.Square, [Xg], ['xn', 'stt'], accum_out=stt[:npp, 0:1])
                ACT(stt[:npp, 1:2], stt[:npp, 0:1], AF.Ln, ['stt'], ['stt'], scale=1.0 / D, bias=EPS)
                ACT(stt[:npp, 2:3], stt[:npp, 1:2], AF.Exp, ['stt'], ['stt'], scale=-0.5)
                TS(xn[:npp, :], X[:npp, g, :], stt[:npp, 2:3], None, ALU.mult, None, [Xg, 'stt'], ['xn'])
                b = 6 + R2('pT')
                pT = bank(b).bitcast(BF16).rearrange('p (c t) -> p c t', t=128)
                for c in range(8):
                    TR(pT[:, c, 0:npp], xn[:npp, c * 128:(c + 1) * 128], identb[:npp, :npp], ['xn', 'identb'], bkeys(b))
                TTo(hnT[:, :, c0:c0 + npp], pT[:, :, 0:npp], V(gT[:, gi, :], [[1, 8], [0, npp]]), ALU.mult, bkeys(b) + ['gT'], [('hnT', g)])

        def postnorm(py, pkeys, fac, g, npp):
            Xg = ('X', g)
            MSET(stt[:npp, 4:5], 0.0, ['stt'])
            ACT(tmpn[:npp, :], py[:npp, :], AF.Square, pkeys, ['tmpn', 'stt'], accum_out=stt[:npp, 4:5])
            ACT(stt[:npp, 5:6], stt[:npp, 4:5], AF.Ln, ['stt'], ['stt'], scale=1.0 / D, bias=EPS)
            ACT(stt[:npp, 6:7], stt[:npp, 5:6], AF.Exp, ['stt'], ['stt'], scale=-0.5)
            STT(tmpn[:npp, :], py[:npp, :], stt[:npp, 6:7], gp[:npp, 0, :], ALU.mult, ALU.mult, pkeys + ['stt', 'gp'], ['tmpn'])
            STT(X[:npp, g, :], tmpn[:npp, :], fac, X[:npp, g, :], ALU.mult, ALU.add, [Xg, 'tmpn'], [Xg])

        def nsplits(has_s):
            return [(0, TP)] + ([(TP, NS)] if has_s else [])

        first_wbig = [False]

        def ffn(f, gi_post, has_s):
            hkeys = [('hnT', g) for g, _, _ in groups_of(has_s)]
            for blk in range(11):
                wgs, wgk = wload(colblk(wg[f], 256 * blk, 256))
                wus, wuk = wload(colblk(wu[f], 256 * blk, 256))
                if blk == 1 and first_wbig[0]:
                    first_wbig[0] = False
                    load_wbig(0)
                for jj in range(2):
                    j = 2 * blk + jj
                    for n0, nn in nsplits(has_s):
                        bg = R2('pg')
                        bu = 2 + R2('pu')
                        pg = bank(bg)
                        pu = bank(bu)
                        for kc in range(8):
                            MM(pg[:, 0:nn], wgs[:, kc, jj * 128:(jj + 1) * 128], hnT[:, kc, n0:n0 + nn], kc == 0, kc == 7, [wgk] + hkeys, bkeys(bg))
                        for kc in range(8):
                            MM(pu[:, 0:nn], wus[:, kc, jj * 128:(jj + 1) * 128], hnT[:, kc, n0:n0 + nn], kc == 0, kc == 7, [wuk] + hkeys, bkeys(bu))
                        si = 0
                        ACT(sg[si][:, 0:nn], pg[:, 0:nn], AF.Silu, bkeys(bg), ['sg%d' % si])
                        TTo(hT[:, j, n0:n0 + nn], sg[si][:, 0:nn], pu[:, 0:nn], ALU.mult, ['sg%d' % si] + bkeys(bu), [('hT', j, n0)])
            load_gp(gi_post)
            hall = [('hT', j, n0) for j in range(NJ) for n0, _ in nsplits(has_s)]
            for g, npp, c0 in groups_of(has_s):
                pyb = (4, 0)[R2('py')]
                py = bank(pyb, 2)
                for hf in range(2):
                    for j in range(NJ):
                        MM(py[:npp, hf * 512:(hf + 1) * 512], hT[:, j, c0:c0 + npp], wbig[:, j, hf * 512:(hf + 1) * 512], j == 0, j == NJ - 1, hall + [('wbig', j // 2)], bkeys(pyb, 2))
                postnorm(py, bkeys(pyb, 2), 0.5, g, npp)
        zk = lambda nm, g: ('z', nm, g)

        def w_in(has_s):
            grp = groups_of(has_s)
            hkeys = [('hnT', g) for g, _, _ in grp]

            def fm_chunk(ws, wk_, lhs_fn, evac):
                for n0, nn in nsplits(has_s):
                    b = R2('pg')
                    p = bank(b)
                    for kc in range(8):
                        MM(p[:, 0:nn], lhs_fn(ws, kc), hnT[:, kc, n0:n0 + nn], kc == 0, kc == 7, [wk_] + extra_keys.get(wk_, []) + hkeys, bkeys(b))
                    evac(p, b, n0, nn)

            def tm_block(ws, wk_, ncols, evac):
                for g, npp, c0 in grp:
                    b = 2 + R2('pu')
                    p = bank(b)
                    for kc in range(8):
                        MM(p[:npp, 0:ncols], hnT[:, kc, c0:c0 + npp], ws[:, kc, 0:ncols], kc == 0, kc == 7, [wk_, ('hnT', g)], bkeys(b))
                    evac(p, b, g, npp)
            sc_k = 128.0 ** (-0.5)
            for blk in range(2):
                ws, wk_ = wload(colblk(win, 256 * blk, 256))
                for jj in range(2):
                    h = 2 * blk + jj
                    fm_chunk(ws, wk_, lambda w_, kc, jj=jj: w_[:, kc, jj * 128:(jj + 1) * 128], lambda p, b, n0, nn, h=h: CP(qT[:, h, n0:n0 + nn], p[:, 0:nn], bkeys(b), [('qT', n0)], eng='act'))
            if os.environ.get('WSTOP') == '1':
                return
            for blk in range(2):
                ws, wk_ = wload(colblk(win, 512 + 256 * blk, 256))
                for jj in range(2):
                    h = 2 * blk + jj
                    fm_chunk(ws, wk_, lambda w_, kc, jj=jj: w_[:, kc, jj * 128:(jj + 1) * 128], lambda p, b, n0, nn, h=h: S.op('act', lambda e: e.mul(out=kT[:, h, n0:n0 + nn], in_=p[:, 0:nn], mul=sc_k), bkeys(b), [('kT', n0)]))
                tm_block(ws, wk_, 256, lambda p, b, g, npp, blk=blk: TS(ktok[:npp, g, 256 * blk:256 * blk + 256], p[:npp, 0:256], sc_k, None, ALU.mult, None, bkeys(b), [zk('ktok', g)]))
            if os.environ.get('WSTOP') == '2':
                return
            for blk in range(2):
                ws, wk_ = wload(colblk(win, 1024 + 256 * blk, 256))

                def ev_v(p, b, g, npp, blk=blk):
                    CP(vaug[:npp, g, 2 * blk:2 * blk + 2, 0:128], p[:npp, 0:256].rearrange('p (h c) -> p h c', c=128), bkeys(b), [zk('vaug', g)])
                    if blk == 1:
                        MSET(vaug[:npp, g, :, 128:129], 1.0, [zk('vaug', g)])
                tm_block(ws, wk_, 256, ev_v)
            if os.environ.get('WSTOP') == '3':
                return
            for blk in range(2):
                ws, wk_ = wload(colblk(win, 1536 + 256 * blk, 256))
                tm_block(ws, wk_, 256, lambda p, b, g, npp, blk=blk: ACT(osig[:npp, g, 256 * blk:256 * blk + 256], p[:npp, 0:256], AF.Sigmoid, bkeys(b), [zk('osig', g)]))
            if os.environ.get('WSTOP') == '4':
                return
            ws, wk_ = wload(colparts(win, [(32 * f_, 2048, 32) for f_ in range(4)] + [(128 + 32 * f_, 2052, 32) for f_ in range(4)]))
            fm_chunk(ws, wk_, lambda w_, kc: w_[:, kc, 0:128], lambda p, b, n0, nn: CP(IGs[:, n0:n0 + nn], p[:, 0:nn], bkeys(b), [('IGs', n0)], eng='act'))
            fm_chunk(ws, wk_, lambda w_, kc: w_[:, kc, 128:256], lambda p, b, n0, nn: CP(FGs[:, n0:n0 + nn], p[:, 0:nn], bkeys(b), [('FGs', n0)], eng='act'))
            if os.environ.get('WSTOP') == '5':
                return
            for blk in range(2):
                ws, wk_ = wload(colblk(win, 2056 + 256 * blk, 256))
                for jj in range(2):
                    c = 2 * blk + jj
                    fm_chunk(ws, wk_, lambda w_, kc, jj=jj: w_[:, kc, jj * 128:(jj + 1) * 128], lambda p, b, n0, nn, c=c: S.op('act', lambda e: e.mul(out=qaT[:, c, n0:n0 + nn], in_=p[:, 0:nn], mul=0.125), bkeys(b), [('qaT', n0)]))
            if os.environ.get('WSTOP') == '6':
                return
            for hk in range(2):

                def kaf(slot, rk, key, hk=hk):
                    MSET(slot[:, :, 64:192], 0.0, [rk], eng='pool')
                    for d0 in (0, 192):
                        DMA(slot[:, :, d0:d0 + 64], win[:, 2568 + 64 * hk:2568 + 64 * hk + 64].rearrange('(kc p) c -> p kc c', p=128), (), [rk], key=key, eng='pool')
                ws, wk_ = wload(kaf)
                for par in range(2):
                    fm_chunk(ws, wk_, lambda w_, kc, par=par: w_[:, kc, par * 128:(par + 1) * 128], lambda p, b, n0, nn, v=2 * hk + par: CP(kaT[:, v, 128 + n0:128 + n0 + nn], p[:, 0:nn], bkeys(b), [('kaT', n0)], eng='act'))
            if os.environ.get('WSTOP') == '7':
                return
            ws, wk_ = wload(colblk(win, 2568, 256))

            def ev_kv(p, b, g, npp):
                if has_s and g >= 3:
                    CP(kvf[:npp, g - 3, :], p[:npp, 0:256], bkeys(b), [('kvf', g)])
                CP(va[:npp, 1 + g, :], p[:npp, 128:256], bkeys(b), [('va', g)], eng='act')
            tm_block(ws, wk_, 256, ev_kv)

        def gates(has_s, phase=2):
            rI = [('IGs', n0) for n0, _ in nsplits(has_s)]
            rF = [('FGs', n0) for n0, _ in nsplits(has_s)]
            TTn = TP + (NS if has_s else 0)
            if not neg_done[0]:
                neg_done[0] = True
                S.op('act', lambda e: e.mul(out=nbf_l[:], in_=nbf_l[:], mul=-1.0), ['nbf_l', 'bi_l', 'smin'] + lane_keys, ['nbf_l', 'bi_l', 'smin'] + lane_keys)
            ACT(e1[:, 0:TTn], FGs[:, 0:TTn], AF.Exp, rF + ['nbf_l'], ['e1'], scale=-1.0, bias=nbf_l[:, 0:1])
            ACT(lfn[:, 0:TTn], e1[:, 0:TTn], AF.Ln, ['e1'], ['lfn'], bias=1.0)
            if os.environ.get('GSTOP') == '1':
                return
            SCAN(Bneg[:, 0:TP], lfn[:, 0:TP], Bprev[:, 0:1], ALU.add, ['lfn', 'Bprev'], ['Bneg'])
            STT(Ug[:, 0:TP], IGs[:, 0:TP], bi_l[:, 0:1], Bneg[:, 0:TP], ALU.add, ALU.add, rI + ['bi_l', 'Bneg'], ['Ug'])
            CP(MU[:, 0:1], MUprev[:, 0:1], ['MUprev'], ['MU'])
            SCAN(MU[:, 1:TP + 1], Ug[:, 0:TP], MUprev[:, 0:1], ALU.max, ['Ug', 'MUprev', 'MU'], ['MU'])
            CP(Bprev[:, 0:1], Bneg[:, TP - 1:TP], ['Bneg'], ['Bprev'])
            CP(MUprev[:, 0:1], MU[:, TP:TP + 1], ['MU'], ['MUprev'])
            if os.environ.get('GSTOP') == '2':
                return
            MUn = V(MU[:, 128:129], [[128, 4], [0, 128]])
            MUp = V(MU[:, 0:1], [[128, 4], [0, 128]])
            MUc = MU[:, 1:TP + 1].rearrange('p (c t) -> p c t', t=128)
            v3 = lambda a, lo: a[lo:lo + 32, 0:TP].rearrange('p (c t) -> p c t', t=128)
            sl = lambda a, lo: bass.AP(a.tensor, a.offset + lo * a.ap[0][0], [[a.ap[0][0], 32]] + [list(x) for x in a.ap[1:]])
            TTo(v3(tg, 0), v3(Ug, 0), sl(MUn, 0), ALU.subtract, ['Ug', 'MU'], ['tg'])
            TTo(v3(tg, 32), sl(MUn, 32), sl(MUc, 32), ALU.subtract, ['MU'], ['tg'])
            TTo(v3(tg, 64), sl(MUp, 64), sl(MUc, 64), ALU.subtract, ['MU'], ['tg'])
            TTo(v3(tg, 96), v3(Bneg, 96), sl(MUc, 96), ALU.subtract, ['MU', 'Bneg'], ['tg'])
            TTo(dd[0:4, 0:4], V(MU[0:4, 0:1], [[128, 4]]), V(MU[0:4, 128:129], [[128, 4]]), ALU.subtract, ['MU'], ['dd'])
            ACT(dd[0:4, 0:4], dd[0:4, 0:4], AF.Exp, ['dd'], ['dd'])
            if has_s:
                c0 = TP
                l3 = lfn[:, c0:c0 + NS].rearrange('p (b t) -> p b t', t=4)
                B3 = Bneg[:, c0:c0 + NS].rearrange('p (b t) -> p b t', t=4)
                U3 = Ug[:, c0:c0 + NS].rearrange('p (b t) -> p b t', t=4)
                I3 = IGs[:, c0:c0 + NS].rearrange('p (b t) -> p b t', t=4)
                CP(B3[:, :, 0:1], l3[:, :, 0:1], ['lfn'], ['Bneg'])
                for t in range(1, 4):
                    TTo(B3[:, :, t:t + 1], B3[:, :, t - 1:t], l3[:, :, t:t + 1], ALU.add, ['lfn', 'Bneg'], ['Bneg'])
                STT(U3, I3, bi_l[:, 0:1], B3, ALU.add, ALU.add, rI + ['bi_l', 'Bneg'], ['Ug'])
                CP(MUs[:, :, 0:1], smin[:].rearrange('p (b o) -> p b o', o=1), ['smin'], ['MUs'])
                for t in range(4):
                    TTo(MUs[:, :, t + 1:t + 2], MUs[:, :, t:t + 1], U3[:, :, t:t + 1], ALU.max, ['Ug', 'MUs'], ['MUs'])
                MUn_s = V(MUs[:, 0, 4:5], [[5, NB], [0, 4]])
                MUp_s = V(MUs[:, 0, 0:1], [[5, NB], [0, 4]])
                MUc_s = MUs[:, :, 1:5]
                t3 = lambda lo: tg[lo:lo + 32, c0:c0 + NS].rearrange('p (b t) -> p b t', t=4)
                TTo(t3(0), U3[0:32], sl(MUn_s, 0), ALU.subtract, ['Ug', 'MUs'], ['tg'])
                TTo(t3(32), sl(MUn_s, 32), MUc_s[32:64], ALU.subtract, ['MUs'], ['tg'])
                TTo(t3(64), sl(MUp_s, 64), MUc_s[64:96], ALU.subtract, ['MUs'], ['tg'])
                TTo(t3(96), B3[96:128], MUc_s[96:128], ALU.subtract, ['MUs', 'Bneg'], ['tg'])
                TTo(dds[0:4, :], V(MUs[0:4, 0, 0:1], [[5, NB]]), V(MUs[0:4, 0, 4:5], [[5, NB]]), ALU.subtract, ['MUs'], ['dds'])
                ACT(dds[0:4, :], dds[0:4, :], AF.Exp, ['dds'], ['dds'])
                TTo(smout[:, :], V(MUs[:, 0, 4:5], [[5, NB]]), V(B3[:, 0, 3:4], [[4, NB]]), ALU.subtract, ['MUs', 'Bneg'], ['smout'])
                DMA(smo.rearrange('b h -> h b'), smout[0:4, :], ['smout'], ['o_smo'], key='o_sm', slow=True)
            if os.environ.get('GSTOP') == '3':
                return
            TS(tg[:, 0:TTn], tg[:, 0:TTn], 80.0, None, ALU.min, None, ['tg'], ['tg'])
            ACT(Fst[:, 0:TTn], tg[:, 0:TTn], AF.Exp, ['tg'], ['Fst'])
            pd = bank(1)
            for h in range(4):
                MM(pd[:, 4 * h:4 * h + 4], Esel[0:4, h, :], dd[0:4, 0:4], True, True, ['Esel', 'dd'], bkeys(1))
            CP(DECs[:].rearrange('p h c -> p (h c)'), pd[:, 0:16], bkeys(1), ['DECs'])
            if has_s:
                for h in range(4):
                    MM(pd[:, 64 + NB * h:64 + NB * h + NB], Esel[0:4, h, :], dds[0:4, :], True, True, ['Esel', 'dds'], bkeys(1))
                CP(DECss[:].rearrange('p h c -> p (h c)'), pd[:, 64:64 + 4 * NB], bkeys(1), ['DECss'])
            if os.environ.get('GSTOP') == '4':
                return
            pf = bank(7)
            for g in range(4):
                TR(pf[:, g * 128:(g + 1) * 128], Fst[:, g * 128:(g + 1) * 128], ident[:, :], ['Fst', 'ident'], bkeys(7))
            CP(FT[:, 0:4, :].rearrange('p g (f h) -> p g f h', h=4), V(pf[:, 0:1], [[128, 4], [32, 4], [1, 4]]), bkeys(7), ['FT'])
            if has_s:
                pf2 = bank(6)
                TR(pf2[0:64, 0:128], Fst[:, TP:TP + NS], ident[:, :], ['Fst', 'ident'], bkeys(6))
                CP(FT[0:64, 4, :].rearrange('p (f h) -> p f h', h=4), V(pf2[0:64, 0:1], [[32, 4], [1, 4]]), bkeys(6), ['FT'])

        def mlstm_group(g, npp, c0, c_idx, is_s):
            zq = [('qT', 0), ('qT', TP), ('kT', 0), ('kT', TP)]
            pS = bank(0).rearrange('p (h t) -> p h t', t=128)
            for h in range(4):
                MM(pS[:npp, h, 0:npp], kT[:, h, c0:c0 + npp], qT[:, h, c0:c0 + npp], True, True, zq, bkeys(0))
                yield
            mk = masks if is_s else maskp
            for h in range(4):
                STT(Sp[:npp, h, 0:npp], pS[:npp, h, 0:npp], FT[:npp, g, h:h + 1], mk[:npp, :npp], ALU.mult, ALU.mult, bkeys(0) + ['FT', 'masks', 'maskp'], ['Sp'])
                yield
            TTo(kw[:npp, :, :], ktok[:npp, g, :].rearrange('p (h c) -> p h c', c=128), V(FT[:npp, g, 0:1], [[1, 4], [0, 128]]), ALU.mult, [zk('ktok', g), 'FT'], ['kw'])
            yield
            pKV = bank(1, 2).rearrange('p (h c) -> p h c', c=256)
            pO1 = bank(3, 2).rearrange('p (h c) -> p h c', c=256)
            pO2 = bank(3, 2).rearrange('p (h c) -> p h c', c=256)
            vk = [zk('vaug', g), 'vones']
            if not is_s:
                for h in range(4):
                    MM(pKV[:, h, 0:129], kw[:, h, :], vaug[:, g, h, 0:129], True, True, ['kw'] + vk, bkeys(1, 2))
                    yield
                for h in range(4):
                    MM(pO1[:, h, 0:129], qT[:, h, c0:c0 + 128], Cbf[:, h, :], True, True, zq + ['Cbf'], bkeys(3, 2))
                    yield
            else:
                MSET(tmpo[:64], 0.0, ['tmpo'])
                yield
                for b in range(NB):
                    i = b % 2
                    TTo(qTm[:, i, :, :], qT[:, :, c0:c0 + 64], V(bmb[:, b, 0:1], [[0, 4], [1, 64]]), ALU.mult, zq + ['bmb'], [('qTm', i)])
                    yield
                    DMA(Cb[i][:, :, 0:128], sC[b].rearrange('h k v -> k h v'), (), [('Cb', i)], key='cb%d' % i)
                    yield
                    CP(Cb[i][:, :, 128:129], nTin[:, 4 * b:4 * b + 4].rearrange('p (h o) -> p h o', o=1), ['nTin'], [('Cb', i)], eng='act')
                    yield
                    CP(Cbb[i][:], Cb[i][:], [('Cb', i)], [('Cbb', i)], eng='act')
                    yield
                    for h in range(4):
                        MM(pO1[0:64, h, 0:129], qTm[:, i, h, :], Cbb[i][:, h, :], True, True, [('qTm', i), ('Cbb', i)], bkeys(3, 2))
                        yield
                    TTo(tmpo[:64], pO1[0:64, :, 0:129], tmpo[:64], ALU.add, bkeys(3, 2) + ['tmpo'], ['tmpo'])
                    yield
                    TS(kwm[:, :, :].rearrange('p h c -> p (h c)'), kw[0:64, :, :].rearrange('p h c -> p (h c)'), bmT[:, b:b + 1], None, ALU.mult, None, ['kw', 'bmT'], ['kwm'])
                    yield
                    for h in range(4):
                        MM(pKV[:, h, 0:129], kwm[:, h, :], vaug[0:64, g, h, 0:129], True, True, ['kwm'] + vk, bkeys(1, 2))
                        yield
                    TTo(Cb[i][:], Cb[i][:], V(DECss[:, 0, b:b + 1], [[NB, 4], [0, 129]]), ALU.mult, [('Cb', i), 'DECss'], [('Cb', i)])
                    yield
                    TTo(Cb[i][:], Cb[i][:], pKV[:, :, 0:129], ALU.add, [('Cb', i)] + bkeys(1, 2), [('Cb', i)])
                    yield
                    DMA(sCo[b].rearrange('h k v -> k h v'), Cb[i][:, :, 0:128], [('Cb', i)], ['o_sC%d' % i], key='o_sC%d' % i)
                    yield
                    CP(nTout[:, 4 * b:4 * b + 4].rearrange('p (h o) -> p h o', o=1), Cb[i][:, :, 128:129], [('Cb', i)], ['nTout'], eng='act')
                    yield
            if is_s:
                TTo(tmpo[:npp], tmpo[:npp], V(FT[:npp, g, 8:9], [[1, 4], [0, 129]]), ALU.mult, ['tmpo', 'FT'], ['tmpo'])
                yield
            else:
                TTo(tmpo[:npp], pO1[:npp, :, 0:129], V(FT[:npp, g, 8:9], [[1, 4], [0, 129]]), ALU.mult, bkeys(3, 2) + ['FT'], ['tmpo'])
                yield
            for h in range(4):
                MM(pO2[:npp, h, 0:129], Sp[:npp, h, 0:npp], vaug[:npp, g, h, 0:129], True, True, ['Sp'] + vk, bkeys(3, 2))
                yield
            for h in range(4):
                STT(ND[:npp, h, :], pO2[:npp, h, 0:129], FT[:npp, g, 4 + h:5 + h], tmpo[:npp, h, :], ALU.mult, ALU.add, bkeys(3, 2) + ['FT', 'tmpo'], ['ND'])
                yield
            TS(q5[:npp, 0, :], ND[:npp, :, 128], -1.0, None, ALU.mult, None, ['ND'], ['q5'])
            yield
            TTo(q5[:npp, 0, :], q5[:npp, 0, :], ND[:npp, :, 128], ALU.max, ['ND', 'q5'], ['q5'])
            yield
            TTo(q5[:npp, 0, :], q5[:npp, 0, :], FT[:npp, g, 12:16], ALU.max, ['q5', 'FT'], ['q5'])
            yield
            MSET(q5[:npp, 2, :], 0.0, ['q5'])
            yield
            for h in range(4):
                ACT(kw[:npp, h, :], ND[:npp, h, 0:128], AF.Square, ['ND'], ['kw', 'q5'], scale=128.0 ** (-0.5), accum_out=q5[:npp, 2, h:h + 1])
                yield
            TTo(q5[:npp, 3, :], q5[:npp, 0, :], q5[:npp, 0, :], ALU.mult, ['q5'], ['q5'])
            yield
            STT(q5[:npp, 4, :], q5[:npp, 3, :], EPS, q5[:npp, 2, :], ALU.mult, ALU.add, ['q5'], ['q5'])
            yield
            ACT(q5[:npp, 5, :], q5[:npp, 4, :], AF.Ln, ['q5'], ['q5'])
            yield
            ACT(q5[:npp, 7, :], q5[:npp, 5, :], AF.Exp, ['q5'], ['q5'], scale=-0.5)
            yield
            TTo(og[:npp, :], osig[:npp, g, :], mngb[:npp, :], ALU.mult, [zk('osig', g), 'mngb'], ['og'])
            yield
            TTo(t5[:npp, :, 0:128], ND[:npp, :, 0:128], V(q5[:npp, 7, 0:1], [[1, 4], [0, 128]]), ALU.mult, ['ND', 'q5', 'tmpo'], ['tmpo'])
            yield
            TTo(U[:npp, g, 0:512].rearrange('p (h c) -> p h c', c=128), t5[:npp, :, 0:128], og[:npp, :].rearrange('p (h c) -> p h c', c=128), ALU.mult, ['tmpo', 'og'], [('U', g, 0)])
            yield
            if not is_s:
                TTo(Cst[:], Cst[:], V(DECs[:, 0, c_idx:c_idx + 1], [[4, 4], [0, 129]]), ALU.mult, ['Cst', 'DECs'], ['Cst'])
                yield
                TTo(Cst[:], Cst[:], pKV[:, :, 0:129], ALU.add, ['Cst'] + bkeys(1, 2), ['Cst'])
                yield
                CP(Cbf[:], Cst[:], ['Cst'], ['Cbf'], eng='act')
                yield

        def softmax_tail(Sx, Px, npp, nk, R_, half):
            sk_ = sinkb[:npp, 4 * half:4 * half + 4]
            RED(sst[:npp, 0, 0:4], Sx, ALU.max, R_, ['sst'])
            yield
            TTo(sst[:npp, 1, 0:4], sst[:npp, 0, 0:4], sk_, ALU.max, ['sst', 'sinkb'], ['sst'])
            yield
            TS(sst[:npp, 7, 0:4], sst[:npp, 1, 0:4], -1.0, None, ALU.mult, None, ['sst'], ['sst'])
            yield
            MSET(sst[:npp, 2, 0:4], 0.0, ['sst'])
            yield
            for hh_ in range(4):
                ACT(Px[:, hh_, :], Sx[:, hh_, :], AF.Exp, R_ + ['sst'], ['Px', 'sst'], bias=sst[:npp, 7, hh_:hh_ + 1], accum_out=sst[:npp, 2, hh_:hh_ + 1])
                yield
            TTo(sst[:npp, 3, 0:4], sk_, sst[:npp, 1, 0:4], ALU.subtract, ['sst', 'sinkb'], ['sst'])
            yield
            ACT(sst[:npp, 4, 0:4], sst[:npp, 3, 0:4], AF.Exp, ['sst'], ['sst'])
            yield
            TTo(sst[:npp, 5, 0:4], sst[:npp, 4, 0:4], sst[:npp, 2, 0:4], ALU.add, ['sst'], ['sst'])
            yield
            RCP(sst[:npp, 6, 0:4], sst[:npp, 5, 0:4], ['sst'], ['sst'])
            yield
            return sst[:npp, 6, 0:4]

        def swa_group(g, first):
            zq = [('qaT', 0), ('kaT', 0), 'kaT_h']
            c0 = g * 128
            vkeys = ['va0', ('va', g)] + ([('va', g - 1)] if g > 0 else [])
            for half in range(2):
                pS = bank(5, 2).rearrange('p (h k) -> p h k', k=256)
                for hh in range(4):
                    h = 4 * half + hh
                    MM(pS[:, hh, :], qaT[:, h // 2, c0:c0 + 128], kaT[:, 2 * half + h % 2, c0:c0 + 256], True, True, zq, bkeys(5, 2))
                    yield
                for hh in range(4):
                    h = 4 * half + hh
                    STT(Ssb[:, hh, :], biasT[:, 0, :], slopeb[:, h:h + 1], pS[:, hh, :], ALU.mult, ALU.add, bkeys(5, 2) + ['biasT', 'slopeb'], ['Ssb'])
                    yield
                if first:
                    TS(Ssb[:, :, 0:128], Ssb[:, :, 0:128], role[:, 16:17], None, ALU.add, None, ['Ssb', 'role'], ['Ssb'])
                    yield
                if os.environ.get('SSTOP') == '1':
                    continue
                rden = (yield from softmax_tail(Ssb[:], Pb[:], 128, 256, ['Ssb'], half))
                if os.environ.get('SSTOP') == '2':
                    continue
                pb_ = 7
                PT = bank(pb_).bitcast(BF16).rearrange('p (h t) -> p h t', t=128)
                for hh in range(4):
                    for kt in range(2):
                        TR(PT[:, 2 * hh + kt, :], Pb[:, hh, kt * 128:(kt + 1) * 128], identb[:, :], ['Px', 'identb'], bkeys(pb_))
                        yield
                CP(PTs[:, 0:4, :], PT[:, 0:4, :], bkeys(pb_), ['PTs'], eng='act')
                yield
                CP(PTs[:, 4:8, :], PT[:, 4:8, :], bkeys(pb_), ['PTs'], eng='act')
                yield
                if os.environ.get('SSTOP') == '3':
                    continue
                pO = bank(7).rearrange('p (h c) -> p h c', c=64)
                for hh in range(4):
                    for kt in range(2):
                        MM(pO[:, hh, :], PTs[:, 2 * hh + kt, :], va[:, g + kt, half * 64:(half + 1) * 64], kt == 0, kt == 1, ['PTs'] + vkeys, bkeys(7))
                        yield
                TTo(U[:, g, 512 + 256 * half:768 + 256 * half].rearrange('p (h c) -> p h c', c=64), pO[:, 0:4, :], V(rden[:, 0:1], [[1, 4], [0, 64]]), ALU.mult, bkeys(7) + ['sst'], [('U', g, 1 + half)])
                yield

        def swa_sample():
            zq = [('qaT', TP), ('kaT', TP)]
            Ss = Ssb[0:4, 0:3, :].rearrange('p a k -> p (a k)').rearrange('p (h k) -> p h k', k=192)
            Ps = Pb[0:4, 0:3, :].rearrange('p a k -> p (a k)').rearrange('p (h k) -> p h k', k=192)
            for b in range(NB):
                i = b % 2
                for v in range(4):
                    hk, par = (v // 2, v % 2)
                    DMA(ckb[i][:, v, 64 * par:64 * par + 64], ck[b][:, 64 * hk:64 * hk + 64], [('ckb', i)], [('ckb', i, v)], key='ck%d_%d' % (i, v), eng='pool')
                    yield
                DMA(cvb[i][:], cv[b], (), [('cvb', i)], key='cv%d' % i, eng='pool')
                yield
                pk_ = bank(7)
                for v in range(4):
                    MM(pk_[:, v * 128:(v + 1) * 128], ckb[i][:, v, :], identb[:, :], True, True, [('ckb', i), ('ckb', i, v), 'identb'], bkeys(7))
                    yield
                CP(kTc[i][:].rearrange('p h t -> p (h t)'), pk_[:, 0:512], bkeys(7), [('kTc', i)], eng='act')
                yield
                for half in range(2):
                    pSc = bank(5).rearrange('p (h k) -> p h k', k=128)
                    pSn = bank(6).rearrange('p (h k) -> p h k', k=64)
                    for hh in range(4):
                        h = 4 * half + hh
                        q_ = qaT[:, h // 2, TP + 4 * b:TP + 4 * b + 4]
                        MM(pSc[0:4, hh, :], q_, kTc[i][:, 2 * half + h % 2, :], True, True, zq + [('kTc', i)], bkeys(5))
                        yield
                        MM(pSn[0:4, hh, :], q_, kaT[:, 2 * half + h % 2, 128 + TP:128 + TP + 64], True, True, zq, bkeys(6))
                        yield
                    for hh in range(4):
                        h = 4 * half + hh
                        STT(Ss[0:4, hh, 0:128], bsc[0:4, 0, :], slopeb[0:4, h:h + 1], pSc[0:4, hh, :], ALU.mult, ALU.add, bkeys(5) + ['bsc', 'slopeb'], ['Ssb'])
                        yield
                        STT(Ss[0:4, hh, 128:192], tbl[0:4, 0, 60 - 4 * b:124 - 4 * b], slopeb[0:4, h:h + 1], pSn[0:4, hh, :], ALU.mult, ALU.add, bkeys(6) + ['tbl', 'slopeb'], ['Ssb'])
                        yield
                    rden = (yield from softmax_tail(Ss, Ps, 4, 192, ['Ssb'], half))
                    PT = bank(6).bitcast(BF16)[:, 0:32].rearrange('p (h k q) -> p h k q', k=2, q=4)
                    for hh in range(4):
                        TR(PT[:, hh, 0, :], Ps[0:4, hh, 0:128], identb[0:4, 0:4], ['Px', 'identb'], bkeys(6))
                        yield
                        TR(PT[0:64, hh, 1, :], Ps[0:4, hh, 128:192], identb[0:4, 0:4], ['Px', 'identb'], bkeys(6))
                        yield
                    CP(PTss[:, 0:4, 0, :], PT[:, :, 0, :], bkeys(6), ['PTss'], eng='act')
                    yield
                    CP(PTss[0:64, 0:4, 1, :], PT[0:64, :, 1, :], bkeys(6), ['PTss'])
                    yield
                    pO = bank(7).rearrange('p (h c) -> p h c', c=64)
                    for hh in range(4):
                        MM(pO[0:4, hh, :], PTss[:, hh, 0, :], cvb[i][:, half * 64:(half + 1) * 64], True, False, ['PTss', ('cvb', i)], bkeys(7))
                        yield
                        MM(pO[0:4, hh, :], PTss[0:64, hh, 1, :], va[0:64, 5, half * 64:(half + 1) * 64], False, True, ['PTss', ('va', 4)], bkeys(7))
                        yield
                    TTo(uab[i][0:4, 256 * half:256 * half + 256].rearrange('p (h c) -> p h c', c=64), pO[0:4, 0:4, :], V(rden[:, 0:1], [[1, 4], [0, 64]]), ALU.mult, bkeys(7) + ['sst'], [('uab', i)])
                    yield
                DMA(U[4 * b:4 * b + 4, 4, 512:1024], uab[i][0:4, :], [('uab', i)], [('U', 4, 1, b)], key='uab%d' % i)
                yield
                DMA(sk[b, 0:124, :], ck[b, 4:128, :], (), [('o_sk', b)], key='o_sk')
                yield
                DMA(sv[b, 0:124, :], cv[b, 4:128, :], (), [('o_sv', b)], key='o_sv')
                yield

        def w_out_stage(has_s):
            grp = groups_of(has_s)
            for g, npp, c0 in grp:
                b = 6 + R2('pT')
                pT = bank(b).bitcast(BF16).rearrange('p (c t) -> p c t', t=128)
                for c in range(8):
                    TR(pT[:, c, 0:npp], U[:npp, g, c * 128:(c + 1) * 128], identb[:npp, :npp], [('U', g, 0), ('U', g, 1), ('U', g, 2), 'identb'] + [('U', 4, 1, bb) for bb in range(NB)], bkeys(b))
                CP(hnT[:, :, c0:c0 + npp], pT[:, :, 0:npp], bkeys(b), [('hnT', g)], eng='act')
            load_gp(3)
            slots = [wload(colblk(wout, 256 * blk, 256)) for blk in range(4)]
            for g, npp, c0 in grp:
                pyb = (4, 0)[R2('py')]
                py = bank(pyb, 2)
                for blk in range(4):
                    ws, wk_ = slots[blk]
                    for kc in range(8):
                        MM(py[:npp, 256 * blk:256 * blk + 256], hnT[:, kc, c0:c0 + npp], ws[:, kc, 0:256], kc == 0, kc == 7, [wk_, ('hnT', g)], bkeys(pyb, 2))
                postnorm(py, bkeys(pyb, 2), 1.0, g, npp)
        ZW = o_[0]
        ZSET = {'qT', 'kT', 'qaT', 'kaT', 'va', 'z', 'vones', 'kaT_h', 'va0', 'hT'}

        def zkeys():
            return [k for k in S.last_writer if (k[0] if isinstance(k, tuple) else k) in ZSET]

        def mlstm_state_only(g, c_idx):
            TTo(kw[:, :, :], ktok[:, g, :].rearrange('p (h c) -> p h c', c=128), V(FT[:, g, 0:1], [[1, 4], [0, 128]]), ALU.mult, [zk('ktok', g), 'FT'], ['kw'])
            pKV = bank(2, 2).rearrange('p (h c) -> p h c', c=256)
            for h in range(4):
                MM(pKV[:, h, 0:129], kw[:, h, :], vaug[:, g, h, 0:129], True, True, ['kw', zk('vaug', g), 'vones'], bkeys(2, 2))
            TTo(Cst[:], Cst[:], V(DECs[:, 0, c_idx:c_idx + 1], [[4, 4], [0, 129]]), ALU.mult, ['Cst', 'DECs'], ['Cst'])
            TTo(Cst[:], Cst[:], pKV[:, :, 0:129], ALU.add, ['Cst'] + bkeys(2, 2), ['Cst'])
        first_wbig = [True]
        prenorm(0, nt == 1 and sample)
        for t in range(nt):
            has_s = t == nt - 1 and sample
            ffn(0, 1, has_s)
            load_wbig(0 if t + 1 < nt else 1)
            prenorm(2, has_s)
            w_in(has_s)
            DMA(gsI[t], IGs[:], ['IGs'], ['gsI'], key='sp_gI')
            DMA(gsF[t], FGs[:], ['FGs'], ['gsF'], key='sp_gF')
            DMA(x1s[t], X[:].rearrange('p g d -> p (g d)'), [('X', g) for g in range(5)], ['x1s'], key='sp_x')
            DMA(zs[t, :, 0:ZW], big[:, 0:ZW], zkeys(), ['zs'], key='sp_z')
            if t + 1 < nt:
                nhs = t + 1 == nt - 1 and sample
                DMA(X[:, 0:4, :], xp[(t + 1) * TP:(t + 2) * TP, :].rearrange('(g p) d -> p g d', p=128), (), [('X', g) for g in range(4)], key='x_in')
                if nhs:
                    DMA(X[0:64, 4, :], xs[:, :], (), [('X', 4)], key='x_in_s')
                prenorm(0, nhs)
            gates(False, phase=1)
            for g in range(4):
                mlstm_state_only(g, g)
            if has_s:
                DMA(pk[:, :], kvf[:, 0, 0:128], [('kvf', 3)], ['o_pk'], key='o_pkv')
                DMA(pv[:, :], kvf[:, 0, 128:256], [('kvf', 3)], ['o_pv'], key='o_pkv')
                for b in range(NB):
                    DMA(sk[b, 124:128, :], kvf[4 * b:4 * b + 4, 1, 0:128], [('kvf', 4)], [('o_sk2', b)], key='o_pkv')
                    DMA(sv[b, 124:128, :], kvf[4 * b:4 * b + 4, 1, 128:256], [('kvf', 4)], [('o_sv2', b)], key='o_pkv')
        pay = tmpn[:, 0:PW]
        CP(pay[:, 0:516], Cst[:].rearrange('p h c -> p (h c)'), ['Cst'], ['tmpn'])
        MSET(pay[:, 518:520], 0.0, ['tmpn'])
        CP(pay[:, 516:517], MUprev[:, 0:1], ['MUprev'], ['tmpn'])
        CP(pay[:, 517:518], Bprev[:, 0:1], ['Bprev'], ['tmpn'])
        CP(pay[:, 520:776].bitcast(BF16).rearrange('p (v t) -> p v t', t=128), kaT[:, :, TP:TP + 128], zkeys(), ['tmpn'])
        CP(pay[:, 776:840].bitcast(BF16), va[:, 4, :], zkeys(), ['tmpn'])
        DMA(exin[:, :], pay, ['tmpn'], ['exin'], key='ex_in')

        def is_tail(k):
            return k == 'vones' or (isinstance(k, tuple) and k[0] == 'z' and k[1] in ('ktok', 'vaug'))

        def reload_head(t):
            hk_ = [k for k in zkeys() if not is_tail(k)]
            isqk = lambda k: isinstance(k, tuple) and k[0] in ('qT', 'kT')
            isht = lambda k: isinstance(k, tuple) and k[0] == 'hT'
            QK = 8 * TT_
            DMA(big[:, 0:QK], zs[t, :, 0:QK], ['zs'], [k for k in hk_ if isqk(k) or isht(k)], key='rl_z')
            DMA(big[:, QK:ZT], zs[t, :, QK:ZT], ['zs'], [k for k in hk_ if not isqk(k)], key='rl_z2')

        def reload_tail(t):
            DMA(big[:, ZT:ZW], zs[t, :, ZT:ZW], ['zs'], [k for k in zkeys() if is_tail(k)] + ['kw'], key='rl_zt')

        def reload_x(t):
            DMA(X[:].rearrange('p g d -> p (g d)'), x1s[t], ['x1s'], [('X', g) for g in range(5)], key='rl_x')

        def reload(t):
            reload_head(t)
            reload_tail(t)
            reload_x(t)

        def reload_g(t):
            DMA(IGs[:], gsI[t], ['gsI'], ['IGs'], key='rl_gI')
            DMA(FGs[:], gsF[t], ['gsF'], ['FGs'], key='rl_gF')
        reload(0)
        reload_g(0)
        S.op('pool', lambda e: e.collective_compute('AllGather', ALU.bypass, replica_groups=[[0, 1, 2, 3], [4, 5, 6, 7]], ins=[exin.ap().opt()], outs=[exout.ap().opt()]), ['exin'], ['exout'], dma_key='cc', inc=1)
        for g_ in (1, 2, 3):
            drain(swa_group(g_, False))
        MSET(Cst[:], 0.0, ['Cst'])
        MSET(cmb[:, 0:1], 0.0, ['cmb'])
        MSET(kaTh[:], 0.0, ['kaTh'])
        MSET(vah[:], 0.0, ['vah'])
        payr = Ssb[:].rearrange('p h k -> p (h k)')[:, 0:PW]
        for r in range(3):
            DMA(payr, exout[r * 128:(r + 1) * 128, :], ['exout'], ['Ssb'], key='ex_rd')
            mk_ = role[:, r:r + 1]
            TS(cmb[:, 1:2], payr[:, 517:518], mk_, None, ALU.mult, None, ['Ssb', 'role'], ['cmb'])
            TTo(cmb[:, 2:3], payr[:, 516:517], payr[:, 517:518], ALU.subtract, ['Ssb'], ['cmb'])
            TS(cmb[:, 2:3], cmb[:, 2:3], -NEG, mk_, ALU.add, ALU.mult, ['cmb', 'role'], ['cmb'])
            TS(cmb[:, 2:3], cmb[:, 2:3], NEG, None, ALU.add, None, ['cmb'], ['cmb'])
            TTo(cmb[:, 3:4], cmb[:, 0:1], cmb[:, 1:2], ALU.subtract, ['cmb'], ['cmb'])
            TTo(cmb[:, 4:5], cmb[:, 3:4], cmb[:, 2:3], ALU.max, ['cmb'], ['cmb'])
            TTo(cmb[:, 5:6], cmb[:, 3:4], cmb[:, 4:5], ALU.subtract, ['cmb'], ['cmb'])
            TTo(cmb[:, 6:7], cmb[:, 2:3], cmb[:, 4:5], ALU.subtract, ['cmb'], ['cmb'])
            ACT(cmb[:, 7:9], cmb[:, 5:7], AF.Exp, ['cmb'], ['cmb'])
            TS(cmb[:, 8:9], cmb[:, 8:9], mk_, None, ALU.mult, None, ['cmb', 'role'], ['cmb'])
            CP(cmb[:, 0:1], cmb[:, 4:5], ['cmb'], ['cmb'])
            pd = bank(1)
            for h in range(4):
                MM(pd[:, 2 * h:2 * h + 2], Esel[0:4, h, :], cmb[0:4, 7:9], True, True, ['Esel', 'cmb'], bkeys(1))
            CP(ABt[:].rearrange('p h c -> p (h c)'), pd[:, 0:8], bkeys(1), ['ABt'])
            TTo(Cst[:], Cst[:], V(ABt[:, 0, 0:1], [[2, 4], [0, 129]]), ALU.mult, ['Cst', 'ABt'], ['Cst'])
            TTo(tmpo[:], payr[:, 0:516].rearrange('p (h c) -> p h c', c=129), V(ABt[:, 0, 1:2], [[2, 4], [0, 129]]), ALU.mult, ['Ssb', 'ABt'], ['tmpo'])
            TTo(Cst[:], Cst[:], tmpo[:], ALU.add, ['Cst', 'tmpo'], ['Cst'])
            STT(kaTh[:].rearrange('p v t -> p (v t)'), payr[:, 520:776].bitcast(BF16), role[:, 8 + r:9 + r], kaTh[:].rearrange('p v t -> p (v t)'), ALU.mult, ALU.add, ['Ssb', 'role', 'kaTh'], ['kaTh'])
            STT(vah[:], payr[:, 776:840].bitcast(BF16), role[:, 8 + r:9 + r], vah[:], ALU.mult, ALU.add, ['Ssb', 'role', 'vah'], ['vah'])
        CP(MUprev[:, 0:1], cmb[:, 0:1], ['cmb'], ['MUprev'])
        MSET(Bprev[:], 0.0, ['Bprev'])
        CP(Cbf[:], Cst[:], ['Cst'], ['Cbf'], eng='act')
        for t in range(nt):
            has_s = t == nt - 1 and sample
            grp = groups_of(has_s)
            if t > 0:
                reload_x(t)
            CP(kaT[:, :, 0:128], kaTh[:], ['kaTh'], ['kaT_h'], eng='act')
            CP(va[:, 0, :], vah[:], ['vah'], ['va0'], eng='act')
            if t == 0:
                gates(has_s, phase=2)
            for g, npp, c0 in grp:
                gens_ = [mlstm_group(g, npp, c0, g, g == 4)]
                if g < 4:
                    if t > 0 or g == 0:
                        gens_.append(swa_group(g, t == 0 and g == 0))
                else:
                    gens_.append(swa_sample())
                run_rr(gens_, [2, 3] if len(gens_) == 2 and g < 4 else None)
            CP(kaTh[:], kaT[:, :, TP:TP + 128], [('kaT', 0)], ['kaTh'], eng='act')
            CP(vah[:], va[:, 4, :], [('va', 3)], ['vah'], eng='act')
            if t + 1 < nt:
                reload_tail(t + 1)
                reload_g(t + 1)
                gates(t + 1 == nt - 1 and sample, phase=2)
            w_out_stage(has_s)
            prenorm(4, has_s)
            ffn(1, 5, has_s)
            if t + 1 < nt:
                load_wbig(1)
                reload_head(t + 1)
            DMA(yp[t * TP:(t + 1) * TP, :].rearrange('(g p) d -> p g d', p=128), X[:, 0:4, :], [('X', g) for g in range(4)], ['o_yp'], key='o_y')
            if has_s:
                DMA(ys[:, :], X[0:64, 4, :], [('X', 4)], ['o_ys'], key='o_y')
        DMA(pC.rearrange('h k v -> k h v'), Cst[:, :, 0:128], ['Cst'], ['o_pC'], key='o_fin')
        TR(bank(1)[0:4, 0:128], Cst[:, :, 128], ident[:, :], ['Cst', 'ident'], bkeys(1))
        CP(pn_sb[:], bank(1)[0:4, 0:128], bkeys(1), ['pn_sb'])
        DMA(pn[:, :], pn_sb[:], ['pn_sb'], ['o_pn'], key='o_fin')
        TTo(pm_sb[:], MUprev[:], Bprev[:], ALU.subtract, ['MUprev', 'Bprev'], ['pm_sb'])
        DMA(pm[:, :], pm_sb[0:4, :], ['pm_sb'], ['o_pm'], key='o_fin')
        TR(bank(1)[0:64, 128:256], nTout[:, :], ident[:, :], ['nTout', 'ident'], bkeys(1))
        CP(snout[:], bank(1)[0:64, 128:256], bkeys(1), ['snout'])
        DMA(sno[:, :], snout[:], ['snout'], ['o_sno'], key='o_fin')
        out_keys = [k for k in S.dma_counts if k.startswith('o_')]
        S.emit(final_wait_keys=out_keys)
    return nc

def _consts():
    c = {}
    c['c_ident'] = np.eye(128, dtype=np.float32)
    s = np.arange(128)
    c['c_maskp'] = (s[:, None] <= s[None, :]).astype(np.float32)
    s = np.arange(64)
    c['c_masks'] = ((s[:, None] <= s[None, :]) & (s[:, None] // 4 == s[None, :] // 4)).astype(np.float32)
    slopes = np.exp2(-8.0 * np.arange(1, 9, dtype=np.float32) / 8).astype(np.float32)
    c['c_slope'] = np.broadcast_to(slopes[None, :], (128, 8)).copy()
    BIGN = -8000000.0
    qi = np.arange(128)[:, None]
    kj = np.arange(256)[None, :]
    dist = 128 + qi - kj
    valid = (dist >= 0) & (dist < 128)
    c['c_bias'] = np.where(valid, -dist.astype(np.float32), BIGN).astype(np.float32)
    t = np.arange(4)[:, None]
    j = np.arange(128)[None, :]
    d = 128 + t - j
    v = (d >= 0) & (d < 128)
    c['c_bsc'] = np.where(v, -d.astype(np.float32), BIGN).astype(np.float32)
    x = np.arange(124)[None, :] - 60
    d2 = t - x
    v2 = (x >= 0) & (x <= t)
    c['c_tb'] = np.where(v2, -d2.astype(np.float32), BIGN).astype(np.float32)
    bm = (np.arange(64)[None, :] // 4 == np.arange(NB)[:, None]).astype(np.float32)
    c['c_bm'] = np.broadcast_to(bm.reshape(1, NB * 64), (128, NB * 64)).copy()
    c['c_bmT'] = bm.T.copy()
    E = np.zeros((4, 4, 128), np.float32)
    for h in range(4):
        E[h, h, :] = 1.0
    c['c_E'] = E.reshape(4, 4 * 128)
    return c
_NC = None

def kernel(x_prompt, x_sample, cache_swa_k, cache_swa_v, state_mlstm_C, state_mlstm_n, state_mlstm_m, norm_gains, ffn_w_gate, ffn_w_up, ffn_w_down, w_in, b_gate, mlstm_norm_gain, attn_sinks, w_out):
    global _NC
    f = lambda a: np.ascontiguousarray(np.asarray(a, dtype=np.float32))
    x_prompt, x_sample = (f(x_prompt), f(x_sample))
    ckk, cvv = (f(cache_swa_k)[0], f(cache_swa_v)[0])
    sCC, snn, smm = (f(state_mlstm_C)[0], f(state_mlstm_n)[0], f(state_mlstm_m)[0])
    consts = _consts()
    shared = dict(gains=f(norm_gains)[0], wg=f(ffn_w_gate)[0], wu=f(ffn_w_up)[0], wd=f(ffn_w_down)[0], win=f(w_in)[0], bgate=f(b_gate)[0], mng=f(mlstm_norm_gain)[0], sinks=f(attn_sinks)[0], wout=f(w_out)[0])
    shared.update(consts)
    in_maps = []
    for c in range(8):
        m = dict(shared)
        m['xp'] = np.ascontiguousarray(x_prompt[c // 4, SEQ * (c % 4):SEQ * (c % 4 + 1)])
        role = np.zeros((128, 17), np.float32)
        for r in range(4):
            if r < c % 4:
                role[:, r] = 1.0
            if r == c % 4 - 1:
                role[:, 8 + r] = 1.0
        role[:, 16] = NEG if c % 4 == 0 else 0.0
        m['c_role'] = role
        b0 = NB * c
        m['xs'] = x_sample[b0:b0 + NB].reshape(NS, D)
        m['ck'] = ckk[b0:b0 + NB].reshape(NB, 128, 128)
        m['cv'] = cvv[b0:b0 + NB].reshape(NB, 128, 128)
        m['sC'] = sCC[b0:b0 + NB]
        m['sn'] = snn[b0:b0 + NB].reshape(NB * 4, 128)
        m['sm'] = smm[b0:b0 + NB]
        in_maps.append(m)
    if _NC is None:
        _NC = build_program()
    res = run_bass_kernel_spmd(_NC, in_maps, core_ids=list(range(8)))
    r = res.results
    yp = np.stack([np.concatenate([r[4 * b + j]['yp'] for j in range(4)], 0) for b in range(2)], 0)
    ys = np.concatenate([r[c]['ys'].reshape(NB, 4, D) for c in range(8)], 0)
    pk = np.stack([r[3]['pk'], r[7]['pk']], 0).reshape(1, 2, 128, 2, 64)
    pv = np.stack([r[3]['pv'], r[7]['pv']], 0).reshape(1, 2, 128, 2, 64)
    pC = np.stack([r[3]['pC'], r[7]['pC']], 0)[None]
    pn = np.stack([r[3]['pn'], r[7]['pn']], 0)[None]
    pm = np.stack([r[3]['pm'].reshape(4), r[7]['pm'].reshape(4)], 0)[None]
    sk = np.concatenate([r[c]['sk'] for c in range(8)], 0).reshape(1, 128, 128, 2, 64)
    sv = np.concatenate([r[c]['sv'] for c in range(8)], 0).reshape(1, 128, 128, 2, 64)
    sCo = np.concatenate([r[c]['sCo'] for c in range(8)], 0)[None]
    sno = np.concatenate([r[c]['sno'].reshape(NB, 4, 128) for c in range(8)], 0)[None]
    smo = np.concatenate([r[c]['smo'] for c in range(8)], 0)[None]
    outs = (yp, ys, pk, pv, pC, pn, pm, sk, sv, sCo, sno, smo)
    return tuple((np.ascontiguousarray(o, dtype=np.float32) for o in outs))
```
